# Optimizing a Trainium2 kernel written in Bass

```python
import jax, jax.numpy as jnp
from jax import lax
import numpy as np

D_MODEL = 1024
BATCH = 32
SEQ = 256
DEPTH = 2
DEC_BATCH = 4
DEC_SEQ = 2048
PAST_LEN = 256

GRID_W = 64
N_EVEN = (DEPTH + 1) // 2
N_ODD = DEPTH // 2
EPS = 1e-6
DN_HEADS = 4
DN_DK = 128
DN_DV = 128
DN_CONV = 5
DN_CHUNK = 64
AT_HEADS = 4
AT_KV_HEADS = 2
HEAD_DIM = 128
Q_BLOCK = 128
ROPE_THETA = 10000.0
ML_HEADS = 8
ML_DQK = 64
ML_DV = 128
ML_CHUNK = 64
FFN_HIDDEN = ((8 * D_MODEL // 3 + 255) // 256) * 256

AB_SPLITS = (DN_HEADS * DN_DK, DN_HEADS * DN_DK, DN_HEADS * DN_DV, DN_HEADS * DN_DV,
             2 * DN_HEADS, 2 * DN_HEADS,
             AT_HEADS * HEAD_DIM, AT_KV_HEADS * HEAD_DIM, AT_KV_HEADS * HEAD_DIM)
AB_IN = sum(AB_SPLITS)
AB_OUT = DN_HEADS * DN_DV + AT_HEADS * HEAD_DIM
ML_SPLITS = (ML_HEADS * ML_DQK, ML_HEADS * ML_DQK, ML_HEADS * ML_DV, ML_HEADS * ML_DV,
             2 * ML_HEADS, 2 * ML_HEADS)
ML_IN = sum(ML_SPLITS)
ML_OUT = ML_HEADS * ML_DV

kernel_name = 'hybrid_deltanet_gqa_mlstm_diffusion_step'


def split_cols(p, sizes):
    idx = np.cumsum(sizes)[:-1].tolist()
    return jnp.split(p, idx, axis=-1)


def flip_t(a):
    return a[:, ::-1]


def rmsnorm(x, w):
    xf = x.astype(jnp.float32)
    y = xf * lax.rsqrt(jnp.mean(xf * xf, axis=-1, keepdims=True) + EPS)
    return (y * w.astype(jnp.float32)).astype(x.dtype)


def l2norm(x):
    xf = x.astype(jnp.float32)
    return xf * lax.rsqrt(jnp.sum(xf * xf, axis=-1, keepdims=True) + EPS)


def centred_dwconv(x, w):
    pad = DN_CONV // 2
    T = x.shape[1]
    xp = jnp.pad(x, ((0, 0), (pad, pad), (0, 0)))
    return sum(xp[:, j:j + T] * w[j] for j in range(DN_CONV))


def axial_angles(T):
    n_rows = T // GRID_W
    rows = jnp.repeat(jnp.arange(n_rows), GRID_W).astype(jnp.float32)
    cols = jnp.tile(jnp.arange(GRID_W), n_rows).astype(jnp.float32)
    nf = HEAD_DIM // 4
    inv = ROPE_THETA ** (-jnp.arange(nf, dtype=jnp.float32) / nf)
    return rows[:, None] * inv, cols[:, None] * inv


def rope_half(x, ang):
    x1, x2 = jnp.split(x, 2, axis=-1)
    cos = jnp.cos(ang)[None, :, None, :]
    sin = jnp.sin(ang)[None, :, None, :]
    return jnp.concatenate([x1 * cos - x2 * sin, x2 * cos + x1 * sin], axis=-1)


def apply_axial_rope(x, ang_r, ang_c):
    xf = x.astype(jnp.float32)
    xr, xc = jnp.split(xf, 2, axis=-1)
    return jnp.concatenate([rope_half(xr, ang_r), rope_half(xc, ang_c)], axis=-1).astype(x.dtype)


def block_attention(q, k, v):
    B_, T, HQ, D = q.shape
    HKV = k.shape[2]
    G = HQ // HKV
    nb = T // Q_BLOCK
    qb = q.reshape(B_, nb, Q_BLOCK, HKV, G, D).transpose(1, 0, 2, 3, 4, 5)
    scale = D ** -0.5

    def one_block(qblk):
        s = jnp.einsum('bqhgd,bshd->bhgqs', qblk, k).astype(jnp.float32) * scale
        p = jax.nn.softmax(s, axis=-1).astype(v.dtype)
        return jnp.einsum('bhgqs,bshd->bqhgd', p, v)

    o = lax.map(one_block, qb)
    return o.transpose(1, 0, 2, 3, 4, 5).reshape(B_, T, HQ * D)


def to_chunks(a, n_chunks, csize):
    B_, T, H = a.shape[:3]
    rest = a.shape[3:]
    a = a.reshape((B_, n_chunks, csize, H) + rest)
    return a.transpose((1, 0, 3, 2) + tuple(range(4, a.ndim)))


def from_chunks(o):
    N, B_, H, C, V = o.shape
    return o.transpose(1, 0, 3, 2, 4).reshape(B_, N * C, H, V)


def gated_delta_chunked(q, k, v, g, beta, s0):
    f32 = jnp.float32
    n = q.shape[1] // DN_CHUNK
    q = to_chunks(q.astype(f32) * DN_DK ** -0.5, n, DN_CHUNK)
    k = to_chunks(k.astype(f32), n, DN_CHUNK)
    v = to_chunks(v.astype(f32), n, DN_CHUNK)
    g = to_chunks(g.astype(f32), n, DN_CHUNK)
    beta = to_chunks(beta.astype(f32), n, DN_CHUNK)
    gc = jnp.cumsum(g, axis=-1)
    idx = jnp.arange(DN_CHUNK)
    incl = idx[:, None] >= idx[None, :]
    strict = idx[:, None] > idx[None, :]
    decay = jnp.exp(jnp.where(incl, gc[..., :, None] - gc[..., None, :], -jnp.inf))
    kb = k * beta[..., None]
    L = jnp.where(strict, jnp.einsum('nbhik,nbhjk->nbhij', kb, k) * decay, 0.0)
    a = L + jnp.eye(DN_CHUNK, dtype=f32)
    u = lax.linalg.triangular_solve(a, v * beta[..., None], left_side=True, lower=True, unit_diagonal=True)
    w = lax.linalg.triangular_solve(a, kb * jnp.exp(gc)[..., None], left_side=True, lower=True, unit_diagonal=True)
    qk = jnp.einsum('nbhik,nbhjk->nbhij', q, k) * decay
    g_last = gc[..., -1]
    q_dec = q * jnp.exp(gc)[..., None]
    k_dec = k * jnp.exp(g_last[..., None] - gc)[..., None]

    def step(S, inp):
        q_c, k_c, u_c, w_c, qk_c, gl = inp
        v_new = u_c - jnp.einsum('bhck,bhkv->bhcv', w_c, S)
        o = jnp.einsum('bhck,bhkv->bhcv', q_c, S) + jnp.einsum('bhij,bhjv->bhiv', qk_c, v_new)
        S = S * jnp.exp(gl)[..., None, None] + jnp.einsum('bhck,bhcv->bhkv', k_c, v_new)
        return S, o

    S, o = lax.scan(step, s0.astype(f32), (q_dec, k_dec, u, w, qk, g_last))
    return from_chunks(o), S


def mlstm_chunked(q, k, v, log_i, log_f, C0, n0, m0):
    f32 = jnp.float32
    n = q.shape[1] // ML_CHUNK
    q = to_chunks(q.astype(f32) * ML_DQK ** -0.5, n, ML_CHUNK)
    k = to_chunks(k.astype(f32), n, ML_CHUNK)
    v = to_chunks(v.astype(f32), n, ML_CHUNK)
    log_i = to_chunks(log_i.astype(f32), n, ML_CHUNK)
    log_f = to_chunks(log_f.astype(f32), n, ML_CHUNK)
    b = jnp.cumsum(log_f, axis=-1)
    idx = jnp.arange(ML_CHUNK)
    incl = idx[:, None] >= idx[None, :]
    log_d = jnp.where(incl, b[..., :, None] - b[..., None, :] + log_i[..., None, :], -jnp.inf)
    m_intra = jnp.max(log_d, axis=-1)
    qk = jnp.einsum('nbhtk,nbhsk->nbhts', q, k)
    b_last = b[..., -1]
    log_w_end = b_last[..., None] - b + log_i
    m_end = jnp.max(log_w_end, axis=-1)

    def step(carry, inp):
        C, nv, m = carry
        q_c, k_c, v_c, b_c, log_d_c, m_intra_c, qk_c, b_last_c, log_w_c, m_end_c = inp
        m_in = b_c + m[..., None]
        m_t = jnp.maximum(m_in, m_intra_c)
        w_inter = jnp.exp(m_in - m_t)
        s = qk_c * jnp.exp(log_d_c - m_t[..., None])
        num = w_inter[..., None] * jnp.einsum('bhtk,bhkv->bhtv', q_c, C) + jnp.einsum('bhts,bhsv->bhtv', s, v_c)
        den = w_inter * jnp.einsum('bhtk,bhk->bht', q_c, nv) + jnp.sum(s, axis=-1)
        h = num / jnp.maximum(jnp.abs(den), jnp.exp(-m_t))[..., None]
        m_new = jnp.maximum(b_last_c + m, m_end_c)
        a_state = jnp.exp(b_last_c + m - m_new)
        w_k = k_c * jnp.exp(log_w_c - m_new[..., None])[..., None]
        C = a_state[..., None, None] * C + jnp.einsum('bhsk,bhsv->bhkv', w_k, v_c)
        nv = a_state[..., None] * nv + jnp.sum(w_k, axis=2)
        return (C, nv, m_new), h

    carry0 = (C0.astype(f32), n0.astype(f32), m0.astype(f32))
    (C, nv, m), h = lax.scan(step, carry0, (q, k, v, b, log_d, m_intra, qk, b_last, log_w_end, m_end))
    return from_chunks(h), C, nv, m


def mixer_ab(h, W, e, ctx):
    B_, T = h.shape[0], h.shape[1]
    p = h @ W['ab_w_in'][e]
    dq, dk, dv, dz, da, db, aq, ak, av = split_cols(p, AB_SPLITS)
    qkv = jax.nn.silu(centred_dwconv(jnp.concatenate([dq, dk, dv], axis=-1), W['dn_conv_w'][e]))
    dq, dk, dv = jnp.split(qkv, 3, axis=-1)
    dq = l2norm(dq.reshape(B_, T, DN_HEADS, DN_DK))
    dk = l2norm(dk.reshape(B_, T, DN_HEADS, DN_DK))
    dv = dv.reshape(B_, T, DN_HEADS, DN_DV)
    da = da.reshape(B_, T, 2, DN_HEADS).astype(jnp.float32)
    db = db.reshape(B_, T, 2, DN_HEADS).astype(jnp.float32)
    g = -jnp.exp(W['dn_A_log'][e].astype(jnp.float32)) * jax.nn.softplus(da + W['dn_dt_bias'][e])
    beta = jax.nn.sigmoid(db)
    s0 = jnp.zeros((B_, 2, DN_HEADS, DN_DK, DN_DV), jnp.float32) if ctx is None else ctx[2]
    o_f, s_f = gated_delta_chunked(dq, dk, dv, g[:, :, 0], beta[:, :, 0], s0[:, 0])
    o_b, s_b = gated_delta_chunked(flip_t(dq), flip_t(dk), flip_t(dv), flip_t(g[:, :, 1]),
                                   flip_t(beta[:, :, 1]), s0[:, 1])
    o_dn = rmsnorm(o_f + flip_t(o_b), W['dn_norm_w'][e]) * jax.nn.silu(
        dz.reshape(B_, T, DN_HEADS, DN_DV).astype(jnp.float32))
    aq = rmsnorm(aq.reshape(B_, T, AT_HEADS, HEAD_DIM), W['at_q_norm'][e])
    ak = rmsnorm(ak.reshape(B_, T, AT_KV_HEADS, HEAD_DIM), W['at_k_norm'][e])
    av = av.reshape(B_, T, AT_KV_HEADS, HEAD_DIM)
    if ctx is None:
        o_at = block_attention(aq, ak, av)
    else:
        ang_r, ang_c = axial_angles(T)
        keys = jnp.concatenate([apply_axial_rope(ak, ang_r, ang_c), ctx[0].astype(ak.dtype)], axis=1)
        vals = jnp.concatenate([av, ctx[1].astype(av.dtype)], axis=1)
        o_at = block_attention(apply_axial_rope(aq, ang_r, ang_c), keys, vals)
    mixed = jnp.concatenate([o_dn.reshape(B_, T, DN_HEADS * DN_DV).astype(h.dtype),
                             o_at.astype(h.dtype)], axis=-1)
    y = mixed @ W['ab_w_out'][e]
    new = None if ctx is not None else (ak, av, jnp.stack([s_f, s_b], axis=1))
    return y, new


def mixer_c(h, W, o, ctx):
    B_, T = h.shape[0], h.shape[1]
    p = h @ W['ml_w_in'][o]
    q, k, v, og, ig, fg = split_cols(p, ML_SPLITS)
    q = q.reshape(B_, T, ML_HEADS, ML_DQK)
    k = k.reshape(B_, T, ML_HEADS, ML_DQK)
    v = v.reshape(B_, T, ML_HEADS, ML_DV)
    log_i = ig.reshape(B_, T, 2, ML_HEADS).astype(jnp.float32) + W['ml_i_bias'][o]
    log_f = jax.nn.log_sigmoid(fg.reshape(B_, T, 2, ML_HEADS).astype(jnp.float32) + W['ml_f_bias'][o])
    if ctx is None:
        C0 = jnp.zeros((B_, 2, ML_HEADS, ML_DQK, ML_DV), jnp.float32)
        n0 = jnp.zeros((B_, 2, ML_HEADS, ML_DQK), jnp.float32)
        m0 = jnp.zeros((B_, 2, ML_HEADS), jnp.float32)
    else:
        C0, n0, m0 = ctx
    h_f, C_f, n_f, m_f = mlstm_chunked(q, k, v, log_i[:, :, 0], log_f[:, :, 0], C0[:, 0], n0[:, 0], m0[:, 0])
    h_b, C_b, n_b, m_b = mlstm_chunked(flip_t(q), flip_t(k), flip_t(v), flip_t(log_i[:, :, 1]),
                                       flip_t(log_f[:, :, 1]), C0[:, 1], n0[:, 1], m0[:, 1])
    hs = rmsnorm(h_f + flip_t(h_b), W['ml_norm_w'][o]) * jax.nn.sigmoid(
        og.reshape(B_, T, ML_HEADS, ML_DV).astype(jnp.float32))
    y = hs.reshape(B_, T, ML_OUT).astype(h.dtype) @ W['ml_w_out'][o]
    new = None if ctx is not None else (jnp.stack([C_f, C_b], axis=1), jnp.stack([n_f, n_b], axis=1),
                                        jnp.stack([m_f, m_b], axis=1))
    return y, new


def swiglu(h, wg, wu, wd):
    return (jax.nn.silu(h @ wg) * (h @ wu)) @ wd


def trunk(x, cond, W, states):
    new = []
    for l in range(DEPTH):
        mod = jax.nn.silu(cond) @ W['ada_w'][l] + W['ada_b'][l]
        sh1, sc1, g1, sh2, sc2, g2 = jnp.split(mod[:, None, :], 6, axis=-1)
        h = rmsnorm(x, W['norm1_w'][l]) * (1 + sc1) + sh1
        if l % 2 == 0:
            e = l // 2
            ctx = None if states is None else (states[0][:, e], states[1][:, e], states[2][:, e])
            y, st = mixer_ab(h, W, e, ctx)
        else:
            o = l // 2
            ctx = None if states is None else (states[3][:, o], states[4][:, o], states[5][:, o])
            y, st = mixer_c(h, W, o, ctx)
        x = x + g1 * y
        h = rmsnorm(x, W['norm2_w'][l]) * (1 + sc2) + sh2
        x = x + g2 * swiglu(h, W['ffn_w_gate'][l], W['ffn_w_up'][l], W['ffn_w_down'][l])
        new.append(st)
    return x, new


def setup_inputs(seed: int = 0) -> dict:
    key = jax.random.key(seed)
    ks = iter(jax.random.split(key, 32))

    def nrm(shape, s):
        return s * jax.random.normal(next(ks), shape, jnp.float32)

    D = D_MODEL
    return {
        'x_prompt': nrm((BATCH, SEQ, D), 1.0),
        'x_sample': nrm((DEC_BATCH, DEC_SEQ, D), 1.0),
        'cache_attn_k': nrm((DEC_BATCH, N_EVEN, PAST_LEN, AT_KV_HEADS, HEAD_DIM), 1.0),
        'cache_attn_v': nrm((DEC_BATCH, N_EVEN, PAST_LEN, AT_KV_HEADS, HEAD_DIM), 1.0),
        'state_delta': nrm((DEC_BATCH, N_EVEN, 2, DN_HEADS, DN_DK, DN_DV), 0.3),
        'state_mlstm_C': nrm((DEC_BATCH, N_ODD, 2, ML_HEADS, ML_DQK, ML_DV), 0.3),
        'state_mlstm_n': nrm((DEC_BATCH, N_ODD, 2, ML_HEADS, ML_DQK), 0.3),
        'state_mlstm_m': nrm((DEC_BATCH, N_ODD, 2, ML_HEADS), 0.5),
        'c': nrm((DEC_BATCH, D), 1.0),
        'c_ctx': nrm((D,), 1.0),
        'ada_w': nrm((DEPTH, D, 6 * D), 0.5 * D ** -0.5),
        'ada_b': nrm((DEPTH, 6 * D), 0.01),
        'norm1_w': 1.0 + nrm((DEPTH, D), 0.05),
        'norm2_w': 1.0 + nrm((DEPTH, D), 0.05),
        'ffn_w_gate': nrm((DEPTH, D, FFN_HIDDEN), D ** -0.5),
        'ffn_w_up': nrm((DEPTH, D, FFN_HIDDEN), D ** -0.5),
        'ffn_w_down': nrm((DEPTH, FFN_HIDDEN, D), FFN_HIDDEN ** -0.5),
        'ab_w_in': nrm((N_EVEN, D, AB_IN), D ** -0.5),
        'ab_w_out': nrm((N_EVEN, AB_OUT, D), AB_OUT ** -0.5),
        'dn_conv_w': nrm((N_EVEN, DN_CONV, 3 * DN_HEADS * DN_DK), DN_CONV ** -0.5),
        'dn_A_log': -2.0 + nrm((N_EVEN, 2, DN_HEADS), 0.3),
        'dn_dt_bias': nrm((N_EVEN, 2, DN_HEADS), 0.1),
        'dn_norm_w': 1.0 + nrm((N_EVEN, DN_DV), 0.05),
        'at_q_norm': 1.0 + nrm((N_EVEN, HEAD_DIM), 0.05),
        'at_k_norm': 1.0 + nrm((N_EVEN, HEAD_DIM), 0.05),
        'ml_w_in': nrm((N_ODD, D, ML_IN), D ** -0.5),
        'ml_i_bias': nrm((N_ODD, 2, ML_HEADS), 0.1),
        'ml_f_bias': 3.0 + nrm((N_ODD, 2, ML_HEADS), 0.1),
        'ml_norm_w': 1.0 + nrm((N_ODD, ML_DV), 0.05),
        'ml_w_out': nrm((N_ODD, ML_OUT, D), ML_OUT ** -0.5),
    }


def reference(x_prompt, x_sample, cache_attn_k, cache_attn_v, state_delta, state_mlstm_C, state_mlstm_n,
              state_mlstm_m, c, c_ctx, ada_w, ada_b, norm1_w, norm2_w, ffn_w_gate, ffn_w_up, ffn_w_down,
              ab_w_in, ab_w_out, dn_conv_w, dn_A_log, dn_dt_bias, dn_norm_w, at_q_norm, at_k_norm,
              ml_w_in, ml_i_bias, ml_f_bias, ml_norm_w, ml_w_out):
    W = dict(ada_w=ada_w, ada_b=ada_b, norm1_w=norm1_w, norm2_w=norm2_w, ffn_w_gate=ffn_w_gate,
             ffn_w_up=ffn_w_up, ffn_w_down=ffn_w_down, ab_w_in=ab_w_in, ab_w_out=ab_w_out,
             dn_conv_w=dn_conv_w, dn_A_log=dn_A_log, dn_dt_bias=dn_dt_bias, dn_norm_w=dn_norm_w,
             at_q_norm=at_q_norm, at_k_norm=at_k_norm, ml_w_in=ml_w_in, ml_i_bias=ml_i_bias,
             ml_f_bias=ml_f_bias, ml_norm_w=ml_norm_w, ml_w_out=ml_w_out)
    y_prompt, ctx_states = trunk(x_prompt, c_ctx[None, :], W, None)
    even = ctx_states[0::2]
    odd = ctx_states[1::2]
    new_attn_k = jnp.stack([s[0] for s in even], axis=1)
    new_attn_v = jnp.stack([s[1] for s in even], axis=1)
    new_delta = jnp.stack([s[2] for s in even], axis=1)
    new_mlstm_C = jnp.stack([s[0] for s in odd], axis=1)
    new_mlstm_n = jnp.stack([s[1] for s in odd], axis=1)
    new_mlstm_m = jnp.stack([s[2] for s in odd], axis=1)
    y_sample, _ = trunk(x_sample, c, W,
                        (cache_attn_k, cache_attn_v, state_delta, state_mlstm_C, state_mlstm_n, state_mlstm_m))
    return (y_prompt, y_sample, new_attn_k, new_attn_v, new_delta, new_mlstm_C, new_mlstm_n, new_mlstm_m)
```

```python
import contextlib
import numpy as np
import concourse.bass as bass
import concourse.mybir as mybir
from concourse.bass_utils import run_bass_kernel_spmd

F32 = mybir.dt.float32
BF16 = mybir.dt.bfloat16
AF = mybir.ActivationFunctionType
ALU = mybir.AluOpType
AX = mybir.AxisListType

D = 1024
T = 2048
NCH = 16
FF = 2816
NFT = 22
AB_IN = 3088
ML_IN = 3104
EPS = 1e-6
NEG = -30000.0

SAME_ENGINE_SYNC = {"act": True, "dve": True, "pool": True, "pe": False, "sp": False}


class Sched:
    ENGS = ("pe", "act", "dve", "pool", "sp")

    def __init__(self, nc):
        self.nc = nc
        self.ops = {e: [] for e in self.ENGS}
        self.n = {e: 0 for e in self.ENGS}
        self.waited = {e: {} for e in self.ENGS}
        self.last_w = {}
        self.readers = {}
        self.dma_cum = {}
        self.needed = {e: set() for e in self.ENGS}
        self.final_waits = []
        self.final_eng = None
        self.fence = {}

    def _deps(self, eng, reads, writes):
        deps = []
        for k in reads:
            t = self.last_w.get(k)
            if t is not None:
                deps.append(t)
        for k in writes:
            t = self.last_w.get(k)
            if t is not None:
                deps.append(t)
            deps.extend(self.readers.get(k, ()))
        best = {}
        for (sk, v) in deps:
            if sk == eng and not SAME_ENGINE_SYNC.get(eng, False):
                continue
            if self.waited[eng].get(sk, 0) >= v:
                continue
            best[sk] = max(best.get(sk, 0), v)
        for sk, v in best.items():
            self.waited[eng][sk] = v
            if sk in self.ENGS:
                self.needed[sk].add(v)
        return list(best.items())

    def _commit(self, tok, reads, writes):
        for k in writes:
            self.last_w[k] = tok
            self.readers[k] = []
        for k in reads:
            if k in writes:
                continue
            self.readers.setdefault(k, []).append(tok)

    def op(self, eng, fn, reads=(), writes=()):
        waits = self._deps(eng, reads, writes)
        self.n[eng] += 1
        tok = (eng, self.n[eng])
        self.ops[eng].append((waits, fn, ("self", self.n[eng])))
        self._commit(tok, reads, writes)
        return tok

    def dma(self, queue, semkey, items, reads=(), writes=()):
        waits = self._deps(queue, reads, writes)
        cum = self.dma_cum.get(semkey, 0)
        if cum > 0 and self.waited[queue].get(semkey, 0) < cum:
            self.waited[queue][semkey] = cum
            waits.append((semkey, cum))
        final = cum + 16 * len(items)
        self.dma_cum[semkey] = final
        for i, (o, a, kw) in enumerate(items):
            def fn(e, o=o, a=a, kw=kw):
                return e.dma_start(out=o, in_=a, **kw)
            self.ops[queue].append((waits if i == 0 else [], fn, ("dma", semkey)))
        tok = (semkey, final)
        self._commit(tok, reads, writes)
        return tok

    def barrier(self):
        for e in self.ENGS:
            waits = []
            for e2 in self.ENGS:
                if e2 == e or self.n[e2] == 0 or e2 == "sp":
                    continue
                if self.waited[e].get(e2, 0) < self.n[e2]:
                    waits.append((e2, self.n[e2]))
                    self.waited[e][e2] = self.n[e2]
                    self.needed[e2].add(self.n[e2])
            for sk, cum in self.dma_cum.items():
                if self.waited[e].get(sk, 0) < cum:
                    waits.append((sk, cum))
                    self.waited[e][sk] = cum
            if waits:
                self.ops[e].append((waits, None, None))
        self.last_w = {}
        self.readers = {}

    def wait_all_dma(self, eng="sp"):
        self.final_waits = [(sk, v) for sk, v in self.dma_cum.items()]
        self.final_eng = eng

    def emit(self, block, sems):
        rank = {}
        for e in self.ENGS:
            rank[e] = {v: i + 1 for i, v in enumerate(sorted(self.needed[e]))}

        def val(sk, v):
            return rank[sk][v] if sk in self.ENGS else v

        def run(e, h):
            for waits, fn, inc in self.ops[e]:
                for sk, v in waits:
                    h.wait_ge(sems[sk], val(sk, v))
                if fn is None:
                    continue
                ins = fn(h)
                if inc[0] == "self":
                    if inc[1] in rank[e]:
                        if e in self.fence:
                            ins = self.fence[e](h)
                        ins.then_inc(sems[e], 1)
                else:
                    ins.then_inc(sems[inc[1]], 16)
            if self.final_waits and self.final_eng == e:
                for sk, v in self.final_waits:
                    h.wait_ge(sems[sk], v)

        @block.tensor
        def _(t):
            run("pe", t)

        @block.scalar
        def _(s):
            run("act", s)

        @block.vector
        def _(v):
            run("dve", v)

        @block.gpsimd
        def _(g):
            run("pool", g)

        @block.sync
        def _(s):
            run("sp", s)


C_I, C_ONE, C_U, C_LO, C_NLI, C_NUI, C_NLS, C_NUS, C_ROT = [i * 128 for i in range(9)]
NCONST = 9 * 128


def make_consts():
    i = np.arange(128)[:, None]
    j = np.arange(128)[None, :]
    c = np.zeros((128, NCONST), np.float32)
    c[:, C_I:C_I + 128] = (i == j)
    c[:, C_ONE:C_ONE + 128] = 1.0
    c[:, C_U:C_U + 128] = (i <= j)
    c[:, C_LO:C_LO + 128] = (i >= j)
    c[:, C_NLI:C_NLI + 128] = np.where(i >= j, 0.0, NEG)
    c[:, C_NUI:C_NUI + 128] = np.where(i <= j, 0.0, NEG)
    c[:, C_NLS:C_NLS + 128] = np.where(i > j, 0.0, NEG)
    c[:, C_NUS:C_NUS + 128] = np.where(i < j, 0.0, NEG)
    R = np.zeros((128, 128), np.float32)
    for p in range(128):
        if (p % 64) < 32:
            R[p, p + 32] = -1.0
        else:
            R[p, p - 32] = 1.0
    c[:, C_ROT:C_ROT + 128] = R.T
    return c


class Builder:
    def __init__(self, stop_after=None):
        self.stop_after = stop_after
        nc = bass.Bass("TRN2", target_bir_lowering=False)
        self.nc = nc
        self.S = Sched(nc)
        self.din = {}
        self.dout = {}
        self._uid = 0

    def inp(self, name, shape):
        self.din[name] = self.nc.dram_tensor(name, list(shape), F32, kind="ExternalInput").ap()
        return self.din[name]

    def outp(self, name, shape):
        self.dout[name] = self.nc.dram_tensor(name, list(shape), F32, kind="ExternalOutput").ap()
        return self.dout[name]

    def scratch(self, name, shape, dt):
        return self.nc.dram_tensor(name, list(shape), dt).ap()

    def sb(self, name, shape, dt=F32):
        return self.nc.alloc_sbuf_tensor(name, list(shape), dt).ap()

    def MM(self, out, lhsT, rhs, r, w, start=True, stop=True):
        self.S.op("pe", lambda e: e.matmul(out, lhsT=lhsT, rhs=rhs, start=start, stop=stop), r, w)

    def TR(self, out, in_, ident, r, w):
        self.S.op("pe", lambda e: e.transpose(out, in_, ident), r, w)

    def ACT(self, out, in_, func, r, w, bias=None, scale=1.0, accum=None):
        kw = {}
        if bias is not None:
            kw["bias"] = bias
        if accum is not None:
            kw["accum_out"] = accum
        self.S.op("act", lambda e: e.activation(out=out, in_=in_, func=func, scale=scale, **kw), r, w)

    def TT(self, out, a, b, op, r, w, eng="dve"):
        self.S.op(eng, lambda e: e.tensor_tensor(out=out, in0=a, in1=b, op=op), r, w)

    def TS(self, out, a, s1, op0, r, w, s2=None, op1=None, eng="dve"):
        if op1 is None:
            self.S.op(eng, lambda e: e.tensor_scalar(out=out, in0=a, scalar1=s1, scalar2=None, op0=op0), r, w)
        else:
            self.S.op(eng, lambda e: e.tensor_scalar(out=out, in0=a, scalar1=s1, scalar2=s2, op0=op0, op1=op1), r, w)

    def STT(self, out, in0, scalar, in1, op0, op1, r, w, eng="dve"):
        self.S.op(eng, lambda e: e.scalar_tensor_tensor(out=out, in0=in0, scalar=scalar, in1=in1, op0=op0, op1=op1), r, w)

    def CP(self, out, in_, r, w, eng="dve"):
        if eng == "act":
            self.S.op(eng, lambda e: e.activation(out=out, in_=in_, func=AF.Copy), r, w)
        else:
            self.S.op(eng, lambda e: e.tensor_copy(out=out, in_=in_), r, w)

    def RED(self, out, in_, op, r, w):
        self.S.op("dve", lambda e: e.tensor_reduce(out=out, in_=in_, axis=AX.X, op=op), r, w)

    def RCP(self, out, in_, r, w):
        self.S.op("dve", lambda e: e.reciprocal(out=out, in_=in_), r, w)

    def MSET(self, ap, v, w, eng="dve"):
        self.S.op(eng, lambda e: e.memset(ap, v), (), w)

    def DMAS(self, semkey, pairs, r, w, queue="sp"):
        self.S.dma(queue, semkey, [(o, a, {}) for o, a in pairs], r, w)

    def DMA(self, semkey, out, in_, r, w, queue="sp", **kw):
        self.S.dma(queue, semkey, [(out, in_, kw)], r, w)

    def build(self):
        nc, S = self.nc, self.S
        B = self
        xin = B.inp("xin", [T, D])
        consts = B.inp("consts", [128, NCONST])
        vecA = B.inp("vecA", [2, 72, 128])
        vecB = B.inp("vecB", [63, 128])
        bc0 = B.inp("bc0", [16])
        bc1 = B.inp("bc1", [32])
        mlnw = B.inp("mlnw", [128])
        knf = B.inp("knf", [1])
        maskb = B.inp("maskb", [128, 18 * 8])
        ropec = B.inp("ropec", [128, T])
        ropes = B.inp("ropes", [128, T])
        ck = B.inp("ck", [256, 256])
        cv = B.inp("cv", [256, 256])
        sd0 = B.inp("sd0", [2, 4, 128, 128])
        mC0 = B.inp("mC0", [2, 8, 64, 128])
        mn0 = B.inp("mn0", [2, 8, 64])
        mm0 = B.inp("mm0", [2, 8])
        ada_w = B.inp("ada_w", [2, D, 6 * D])
        ffn_g = B.inp("ffn_w_gate", [2, D, FF])
        ffn_u = B.inp("ffn_w_up", [2, D, FF])
        ffn_d = B.inp("ffn_w_down", [2, FF, D])
        ab_in = B.inp("ab_w_in", [1, D, AB_IN])
        ab_out = B.inp("ab_w_out", [1, D, D])
        ml_in = B.inp("ml_w_in", [1, D, ML_IN])
        ml_out = B.inp("ml_w_out", [1, D, D])

        y = B.outp("y", [T, D])
        ok = B.outp("ok", [T, 256])
        ov = B.outp("ov", [T, 256])
        od = B.outp("od", [8, 2, 4, 128, 128])
        oC = B.outp("oC", [8, 2, 8, 64, 128])
        on = B.outp("on", [8, 2, 8, 64])
        om = B.outp("om", [8, 2, 8])

        xT_d = B.scratch("xT_d", [8, 128, T], F32)

        ps = [nc.alloc_psum_tensor(f"ps{i}", [128, 512], F32).ap() for i in range(8)]
        pk = [f"ps{i}" for i in range(8)]

        cst = B.sb("cst", [128, NCONST])
        cstb = B.sb("cstb", [128, NCONST], BF16)
        epsc = B.sb("epsc", [128, 1])
        knc = B.sb("knc", [128, 1])
        hT = B.sb("hT", [128, 8, T], BF16)
        vA = B.sb("vA", [128, 2, 72])
        vB = B.sb("vB", [128, 63])
        modv = B.sb("modv", [128, 2, 48])
        AA = B.sb("AA", [128, 2, 2, 8])
        fsa = B.sb("fence_a", [128, 2])
        fsv = B.sb("fence_v", [128, 2])
        S.fence["act"] = lambda e: e.activation(out=fsa[:, 1:2], in_=fsa[:, 0:1], func=AF.Copy)
        S.fence["dve"] = lambda e: e.tensor_copy(out=fsv[:, 1:2], in_=fsv[:, 0:1])
        arena = B.sb("arena", [128, 41000])
        self._aoff = 0

        def carve(shape, dt=F32):
            n = int(np.prod(shape[1:]))
            words = n if dt == F32 else (n + 1) // 2
            words = (words + 7) // 8 * 8
            v = arena[0:shape[0], self._aoff:self._aoff + words]
            self._aoff += words
            assert self._aoff <= 41000, self._aoff
            if dt != F32:
                v = v.bitcast(BF16)[:, 0:n]
            else:
                v = v[:, 0:n]
            if len(shape) == 2:
                return v
            names = " ".join(f"a{i}" for i in range(len(shape) - 1))
            kw = {f"a{i}": shape[i + 1] for i in range(len(shape) - 1)}
            return v.rearrange(f"p ({names}) -> p {names}", **kw)

        def new_phase():
            S.barrier()
            self._aoff = 0

        I32 = cst[:, C_I:C_I + 128]
        ONE32 = cst[:, C_ONE:C_ONE + 128]
        Ib = cstb[:, C_I:C_I + 128]
        ONEb = cstb[:, C_ONE:C_ONE + 128]

        def bc4(ap2d):
            return ap2d.unsqueeze(1).to_broadcast([128, 4, 128])

        def bcl(ap2d, n=128):
            return ap2d.unsqueeze(2).to_broadcast([ap2d.shape[0], ap2d.shape[1], n])

        def v3(ap, a=4):
            return ap.rearrange("p (a b) -> p a b", a=a)

        B.DMA("d_c", cst, consts, (), ["cst"])
        B.DMA("d_cb", cstb, consts, (), ["cstb"], queue="pool")
        B.MSET(epsc, EPS, ["epsc"])
        B.DMA("d_c", knc, knf.partition_broadcast(128), (), ["knc"])
        vst = carve([128, 128])
        for l in range(2):
            B.DMA("d_v", vst[0:72, :], vecA[l], (), ["vst"])
            B.TR(ps[0][:, 0:72], vst[0:72, :], cst[0:72, C_I:C_I + 72], ["vst", "cst"], [pk[0]])
            B.CP(vA[:, l, :], ps[0][:, 0:72], [pk[0]], ["vA"])
        B.DMA("d_v", vst[0:63, :], vecB, (), ["vst"])
        B.TR(ps[0][:, 0:63], vst[0:63, :], cst[0:63, C_I:C_I + 63], ["vst", "cst"], [pk[0]])
        B.CP(vB, ps[0][:, 0:63], [pk[0]], ["vB"])

        xs = [carve([128, D]) for _ in range(2)]
        xo = [carve([128, 8, 128]) for _ in range(2)]
        for tt in range(NCH):
            s = tt % 2
            B.DMA(f"d_xs{s}", xs[s], xin[tt * 128:(tt + 1) * 128, :], (), [f"xs{s}"])
            for c in range(8):
                b = c // 4
                B.TR(ps[b][:, (c % 4) * 128:(c % 4 + 1) * 128], xs[s][:, c * 128:(c + 1) * 128], I32,
                     [f"xs{s}", "cst"], [pk[b]])
            for b in range(2):
                B.CP(xo[s][:, b * 4:(b + 1) * 4, :], v3(ps[b]), [pk[b]], [f"xo{s}"], eng=("dve" if b == 0 else "act"))
            B.DMA(f"d_xo{s}", xT_d[:, :, tt * 128:(tt + 1) * 128].rearrange("c p t -> p c t"), xo[s],
                  [f"xo{s}"], [("xT", tt // 4)])

        def ada_phase(l):
            new_phase()
            scb = carve([128, 8], BF16)
            sc32 = carve([128, 8])
            B.ACT(sc32, vA[:, l, 64:72], AF.Silu, ["vA"], ["sc32"])
            B.CP(scb, sc32, ["sc32"], ["scb"])
            wa = [carve([128, 8, 512]) for _ in range(4)]
            for g in range(12):
                s = g % 4
                B.DMA(f"d_wa{s}", wa[s], ada_w[l][:, g * 512:(g + 1) * 512].rearrange("(c p) n -> p c n", p=128),
                      (), [f"wa{s}"])
                for j in range(4):
                    col = g * 4 + j
                    for kc in range(8):
                        B.MM(ps[2][:, col:col + 1], wa[s][:, kc, j * 128:(j + 1) * 128], sc32[:, kc:kc + 1],
                             [f"wa{s}", "sc32"], [pk[2]], start=(kc == 0), stop=(kc == 7))
            B.TT(modv[:, l, :], ps[2][:, 0:48], vA[:, l, 0:48], ALU.add, [pk[2], "vA"], ["modv"])
            for i in range(2):
                sc = modv[:, l, (1 + 3 * i) * 8:(2 + 3 * i) * 8]
                B.STT(AA[:, l, i, :], sc, 1.0, vA[:, l, 48 + 8 * i:56 + 8 * i], ALU.add, ALU.mult, ["modv", "vA"], ["AA"])

        def mod(l, j):
            return modv[:, l, j * 8:(j + 1) * 8]

        def norm_phase(l, i):
            new_phase()
            xt = [carve([128, 8, 512]) for _ in range(2)]
            sq = [carve([128, 512]) for _ in range(2)]
            rs = carve([128, 512])
            tmp = [carve([128, 512]) for _ in range(2)]
            for tb in range(4):
                s = tb % 2
                B.DMA(f"d_xt{s}", xt[s], xT_d[:, :, tb * 512:(tb + 1) * 512].rearrange("c p t -> p c t"),
                      [("xT", tb)], [f"xt{s}"])
                for c in range(8):
                    q = c % 2
                    B.ACT(sq[q], xt[s][:, c, :], AF.Square, [f"xt{s}"], [f"sq{q}"])
                    B.MM(ps[3], ONE32, sq[q], ["cst", f"sq{q}"], [pk[3]], start=(c == 0), stop=(c == 7))
                B.ACT(rs, ps[3], AF.Sqrt, [pk[3], "epsc"], ["rs"], bias=epsc[:, 0:1], scale=1.0 / D)
                B.RCP(rs, rs, ["rs"], ["rs"])
                for c in range(8):
                    q = c % 2
                    B.STT(tmp[q], xt[s][:, c, :], AA[:, l, i, c:c + 1], rs, ALU.mult, ALU.mult,
                          [f"xt{s}", "AA", "rs"], [f"tmp{q}"])
                    B.ACT(hT[:, c, tb * 512:(tb + 1) * 512], tmp[q], AF.Identity, [f"tmp{q}", "modv"], ["hT"],
                          bias=mod(l, 3 * i)[:, c:c + 1])

        def res_proj(w_ap, KC, rhs, rkey, gate, wbuf, final, tok0, ntb):
            xr = [carve([128, 512]) for _ in range(2)]
            yo = [carve([128, 4, 128]) for _ in range(2)] if final else None
            cnt = 0
            for dt in range(8):
                for tb in range(ntb):
                    s = cnt % 2
                    cnt += 1
                    gtb = tok0 // 512 + tb
                    B.DMA(f"d_xr{s}", xr[s], xT_d[dt, :, gtb * 512:(gtb + 1) * 512], [("xT", dt, gtb)], [f"xr{s}"])
                    pb = 4 + s
                    for kc in range(KC):
                        B.MM(ps[pb], wbuf[:, kc, dt * 128:(dt + 1) * 128], rhs[:, kc, tb * 512:(tb + 1) * 512],
                             ["wbuf", rkey], [pk[pb]], start=(kc == 0), stop=(kc == KC - 1))
                    B.STT(xr[s], ps[pb], gate[:, dt:dt + 1], xr[s], ALU.mult, ALU.add, [pk[pb], "modv", f"xr{s}"], [f"xr{s}"])
                    if not final:
                        B.DMA(f"d_xw{s}", xT_d[dt, :, gtb * 512:(gtb + 1) * 512], xr[s], [f"xr{s}"], [("xT", dt, gtb)])
                    else:
                        pt = 6 + s
                        for k in range(4):
                            B.TR(ps[pt][:, k * 128:(k + 1) * 128], xr[s][:, k * 128:(k + 1) * 128], I32,
                                 [f"xr{s}", "cst"], [pk[pt]])
                        B.CP(yo[s], v3(ps[pt]), [pk[pt]], [f"yo{s}"], eng="act")
                        B.DMA(f"d_yo{s}", y.rearrange("(n p) f -> p n f", p=128)[:, gtb * 4:(gtb + 1) * 4, dt * 128:(dt + 1) * 128],
                              yo[s], [f"yo{s}"], [])

        def res_proj_tb(KC, rhs, rkey, gate, wbuf, norm):
            l, i = norm
            xr = [carve([128, 8, 512]) for _ in range(2)]
            sq = [carve([128, 512]) for _ in range(2)]
            rs = carve([128, 512])
            tmp = [carve([128, 512]) for _ in range(2)]
            cnt = 0
            for tb in range(4):
                s = tb % 2
                tsl = slice(tb * 512, (tb + 1) * 512)
                B.DMA(f"d_xr{s}", xr[s], xT_d[:, :, tsl].rearrange("c p t -> p c t"), (), [f"xr{s}"])
                for dt in range(8):
                    pb = 4 + cnt % 2
                    cnt += 1
                    for kc in range(KC):
                        B.MM(ps[pb], wbuf[:, kc, dt * 128:(dt + 1) * 128], rhs[:, kc, tsl],
                             ["wbuf", rkey], [pk[pb]], start=(kc == 0), stop=(kc == KC - 1))
                    B.STT(xr[s][:, dt, :], ps[pb], gate[:, dt:dt + 1], xr[s][:, dt, :], ALU.mult, ALU.add,
                          [pk[pb], "modv", f"xr{s}"], [f"xr{s}"])
                B.DMA(f"d_xw{s}", xT_d[:, :, tsl].rearrange("c p t -> p c t"), xr[s], [f"xr{s}"], [])
                for c in range(8):
                    q = c % 2
                    B.ACT(sq[q], xr[s][:, c, :], AF.Square, [f"xr{s}"], [f"sq{q}"])
                    B.MM(ps[3], ONE32, sq[q], ["cst", f"sq{q}"], [pk[3]], start=(c == 0), stop=(c == 7))
                B.ACT(rs, ps[3], AF.Sqrt, [pk[3], "epsc"], ["rs"], bias=epsc[:, 0:1], scale=1.0 / D)
                B.RCP(rs, rs, ["rs"], ["rs"])
                for c in range(8):
                    q = c % 2
                    B.STT(tmp[q], xr[s][:, c, :], AA[:, l, i, c:c + 1], rs, ALU.mult, ALU.mult,
                          [f"xr{s}", "AA", "rs"], [f"tmp{q}"])
                    B.ACT(hT[:, c, tsl], tmp[q], AF.Identity, [f"tmp{q}", "modv"], ["hT"],
                          bias=mod(l, 3 * i)[:, c:c + 1])

        def ffn_phase(l):
            new_phase()
            aT = carve([128, NFT, T], BF16)
            wd = carve([128, NFT, 1024], BF16)
            wg = [carve([128, 8, 256], BF16) for _ in range(2)]
            wu = [carve([128, 8, 256], BF16) for _ in range(2)]
            sg = [carve([128, 512]) for _ in range(2)]
            cnt = 0
            for g in range(11):
                s = g % 2
                B.DMA(f"d_wg{s}", wg[s], ffn_g[l][:, g * 256:(g + 1) * 256].rearrange("(c p) n -> p c n", p=128),
                      (), [f"wg{s}"], queue="pool")
                B.DMA(f"d_wu{s}", wu[s], ffn_u[l][:, g * 256:(g + 1) * 256].rearrange("(c p) n -> p c n", p=128),
                      (), [f"wu{s}"], queue="pool")
                if g == 1:
                    B.DMAS("d_wd", [(wd[:, f, :], ffn_d[l][f * 128:(f + 1) * 128, :]) for f in range(NFT)], (), ["wbuf"], queue="pool")
                for j in range(2):
                    f = g * 2 + j
                    for tb in range(4):
                        q = cnt % 2
                        cnt += 1
                        tsl = slice(tb * 512, (tb + 1) * 512)
                        for kc in range(8):
                            B.MM(ps[q], wg[s][:, kc, j * 128:(j + 1) * 128], hT[:, kc, tsl],
                                 [f"wg{s}", "hT"], [pk[q]], start=(kc == 0), stop=(kc == 7))
                        for kc in range(8):
                            B.MM(ps[2 + q], wu[s][:, kc, j * 128:(j + 1) * 128], hT[:, kc, tsl],
                                 [f"wu{s}", "hT"], [pk[2 + q]], start=(kc == 0), stop=(kc == 7))
                        B.ACT(sg[q], ps[q], AF.Silu, [pk[q]], [f"sg{q}"])
                        B.TT(aT[:, f, tsl], sg[q], ps[2 + q], ALU.mult, [f"sg{q}", pk[2 + q]], ["aT"])
            res_proj(None, NFT, aT, "aT", mod(l, 5), wd, final=(l == 1), tok0=0, ntb=4)

        def proj_fm(w_ap, c0, M, wslots, idx, consume):
            s = idx % 2
            wt = wslots[s]
            B.DMA(f"d_wt{s}", wt[:, :, 0:M], w_ap[:, c0:c0 + M].rearrange("(c p) n -> p c n", p=128), (), [f"wt{s}"], queue="pool")
            for tb in range(4):
                pb = (idx * 4 + tb) % 2
                for kc in range(8):
                    B.MM(ps[pb][0:M, :], wt[:, kc, 0:M], hT[:, kc, tb * 512:(tb + 1) * 512], [f"wt{s}", "hT"], [pk[pb]],
                         start=(kc == 0), stop=(kc == 7))
                consume(ps[pb], pk[pb], tb)

        def proj_tm(w_ap, c0, N, wbuf, wkey, consume, pbase=2):
            B.DMA("d_" + wkey, wbuf[:, :, 0:N], w_ap[:, c0:c0 + N].rearrange("(c p) n -> p c n", p=128), (), [wkey], queue="pool")
            for tt in range(NCH):
                pb = pbase + tt % 2
                for kc in range(8):
                    B.MM(ps[pb][:, 0:N], hT[:, kc, tt * 128:(tt + 1) * 128], wbuf[:, kc, 0:N], ["hT", wkey], [pk[pb]],
                         start=(kc == 0), stop=(kc == 7))
                consume(ps[pb], pk[pb], tt)

        def mixer_ab():
            new_phase()
            W = ab_in[0]
            qT_d = B.scratch("qT_d", [4, 128, T], BF16)
            kT_d = B.scratch("kT_d", [4, 128, T], BF16)
            zs_d = B.scratch("zs_d", [4, 128, T], BF16)
            ktok_d = B.scratch("ktok_d", [NCH, 128, 4, 128], BF16)
            vtok_d = B.scratch("vtok_d", [NCH, 128, 4, 128], BF16)
            oTf_d = B.scratch("oTf_d", [NCH, 128, 4, 128], F32)
            mixT = carve([128, 8, T], BF16)
            wsl = [carve([128, 8, 128], BF16) for _ in range(2)]
            off_persist = self._aoff

            wab = carve([128, 8, 16], BF16)
            ab = carve([128, NCH, 16])
            b0 = carve([128, 16])
            gg = carve([128, 2, NCH, 4])
            nbeta = carve([128, 2, NCH, 4])
            beta = carve([128, 2, NCH, 4])
            t8 = carve([128, NCH, 8])
            nA = carve([128, 8])
            gc = carve([128, 2, NCH, 4])
            gl = carve([128, 2, NCH, 4])
            EG = carve([128, 2, NCH, 4])
            EKD = carve([128, 2, NCH, 4])
            EGL = carve([128, 2, NCH, 4])
            off_rec = self._aoff
            ci2 = [carve([128, 8, 260]) for _ in range(2)]
            acc2 = [carve([128, 8, 256]) for _ in range(2)]
            sa2 = [carve([128, T]) for _ in range(2)]
            sqb = [carve([128, 512]) for _ in range(2)]
            rsb2 = [carve([128, 512]) for _ in range(2)]
            nb2 = [carve([128, T], BF16) for _ in range(2)]
            tok2 = [carve([128, NCH, 128], BF16) for _ in range(2)]
            for par in range(2):
                B.MSET(ci2[par], 0.0, [f"ci{par}"])
            idx = 0
            for grp in range(3):
                for h in range(4):
                    ct = grp * 4 + h
                    par = idx % 2
                    ci, acc, sa, nb_, tok, rsb = ci2[par], acc2[par], sa2[par], nb2[par], tok2[par], rsb2[par]
                    kci, kacc, ksa, knb, ktk, krs = f"ci{par}", f"acc{par}", f"sa{par}", f"nb{par}", f"tok{par}", f"rsb{par}"
                    accf = acc.rearrange("p s t -> p (s t)")

                    def cons(p, pkk, tb, ci=ci, kci=kci):
                        B.CP(ci[:, 2 * tb:2 * tb + 2, 2:258], p.rearrange("p (s t) -> p s t", s=2), [pkk], [kci], eng="act")
                    proj_fm(W, ct * 128, 128, wsl, idx, cons)
                    idx += 1
                    B.TS(ci[:, 1:8, 0:2], ci[:, 0:7, 256:258], knc[:, 0:1], ALU.mult, [kci, "knc"], [kci])
                    B.TS(ci[:, 0:7, 258:260], ci[:, 1:8, 2:4], knc[:, 0:1], ALU.mult, [kci, "knc"], [kci])
                    B.TS(acc, ci[:, :, 0:256], vB[:, ct:ct + 1], ALU.mult, [kci, "vB"], [kacc])
                    for j in range(1, 5):
                        B.STT(acc, ci[:, :, j:j + 256], vB[:, j * 12 + ct:j * 12 + ct + 1], acc, ALU.mult, ALU.add,
                              [kci, "vB", kacc], [kacc])
                    B.ACT(sa, accf, AF.Silu, [kacc], [ksa])
                    if grp < 2:
                        for tb in range(4):
                            q = tb % 2
                            B.ACT(sqb[q], sa[:, tb * 512:(tb + 1) * 512], AF.Square, [ksa], [f"sqb{q}"])
                            B.MM(ps[2], ONE32, sqb[q], ["cst", f"sqb{q}"], [pk[2]])
                            B.ACT(rsb, ps[2], AF.Sqrt, [pk[2], "epsc"], [krs], bias=epsc[:, 0:1])
                            B.RCP(rsb, rsb, [krs], [krs])
                            B.TT(nb_[:, tb * 512:(tb + 1) * 512], sa[:, tb * 512:(tb + 1) * 512], rsb, ALU.mult, [ksa, krs], [knb])
                        B.DMA(f"d_qk{par}", (qT_d if grp == 0 else kT_d)[h], nb_, [knb], [("qT" if grp == 0 else "kT", h)])
                    else:
                        B.CP(nb_, sa, [ksa], [knb])
                    if grp >= 1:
                        for g4 in range(4):
                            pbt = ps[3].bitcast(BF16)
                            for k in range(4):
                                n = g4 * 4 + k
                                B.TR(pbt[:, k * 128:(k + 1) * 128], nb_[:, n * 128:(n + 1) * 128], Ib, [knb, "cstb"], [pk[3]])
                            B.CP(tok[:, g4 * 4:(g4 + 1) * 4, :], v3(pbt[:, 0:512]), [pk[3]], [ktk], eng="act")
                        dst = ktok_d if grp == 1 else vtok_d
                        B.DMA(f"d_tok{par}", dst[:, :, h, :].rearrange("n t d -> t n d"), tok, [ktk], [("ktok" if grp == 1 else "vtok", h)])
            for h in range(4):
                par = idx % 2
                nb_, knb = nb2[par], f"nb{par}"

                def consz(p, pkk, tb, nb_=nb_, knb=knb):
                    B.ACT(nb_[:, tb * 512:(tb + 1) * 512], p, AF.Silu, [pkk], [knb])
                proj_fm(W, 1536 + h * 128, 128, wsl, idx, consz)
                idx += 1
                B.DMA(f"d_qk{par}", zs_d[h], nb_, [knb], [("zs", h)])
            B.DMA("d_b0", b0, bc0.partition_broadcast(128), (), ["b0"])

            def consab(p, pkk, tt):
                B.CP(ab[:, tt, :], p[:, 0:16], [pkk], ["ab"])
            proj_tm(W, 2048, 16, wab, "wab", consab)
            B.ACT(nA, b0[:, 0:8], AF.Exp, ["b0"], ["nA"])
            B.TS(nA, nA, -1.0, ALU.mult, ["nA"], ["nA"])
            B.TT(t8, ab[:, :, 0:8], b0[:, 8:16].unsqueeze(1).to_broadcast([128, NCH, 8]), ALU.add, ["ab", "b0"], ["t8"])
            B.ACT(t8, t8, AF.Exp, ["t8"], ["t8"])
            B.ACT(t8, t8, AF.Ln, ["t8", "cst"], ["t8"], bias=ONE32[:, 0:1])
            B.TT(t8, t8, nA.unsqueeze(1).to_broadcast([128, NCH, 8]), ALU.mult, ["t8", "nA"], ["t8"])
            for dr in range(2):
                B.CP(gg[:, dr], t8[:, :, dr * 4:(dr + 1) * 4], ["t8"], ["gg"])
            B.ACT(t8, ab[:, :, 8:16], AF.Sigmoid, ["ab"], ["t8"])
            for dr in range(2):
                B.CP(beta[:, dr], t8[:, :, dr * 4:(dr + 1) * 4], ["t8"], ["beta"])
                B.TS(nbeta[:, dr], t8[:, :, dr * 4:(dr + 1) * 4], -1.0, ALU.mult, ["t8"], ["nbeta"])
            for dr in range(2):
                tri = cst[:, C_U:C_U + 128] if dr == 0 else cst[:, C_LO:C_LO + 128]
                B.MM(ps[2][:, 0:64], tri, gg[:, dr].rearrange("p n h -> p (n h)"), ["cst", "gg"], [pk[2]])
                B.CP(gc[:, dr].rearrange("p n h -> p (n h)"), ps[2][:, 0:64], [pk[2]], ["gc"])
                B.MM(ps[2][:, 0:64], ONE32, gg[:, dr].rearrange("p n h -> p (n h)"), ["cst", "gg"], [pk[2]])
                B.CP(gl[:, dr].rearrange("p n h -> p (n h)"), ps[2][:, 0:64], [pk[2]], ["gl"])
            f2 = lambda a: a.rearrange("p d n h -> p (d n h)")
            B.ACT(f2(EG), f2(gc), AF.Exp, ["gc"], ["EG"])
            B.ACT(f2(EGL), f2(gl), AF.Exp, ["gl"], ["EGL"])
            B.TT(f2(EKD), f2(gl), f2(gc), ALU.subtract, ["gl", "gc"], ["EKD"])
            B.ACT(f2(EKD), f2(EKD), AF.Exp, ["EKD"], ["EKD"])

            if self.stop_after == "dn_pre":
                return
            S.barrier()
            self._aoff = off_rec
            oTb_d = B.scratch("oTb_d", [NCH, 128, 4, 128], F32)
            f3 = lambda a: a.rearrange("p h t -> p (h t)")
            scale = 128.0 ** -0.5
            Sst = [carve([128, 4, 128]) for _ in range(2)]
            Sb = [carve([128, 4, 128], BF16) for _ in range(2)]
            for dr in range(2):
                B.DMA("d_s0", Sst[dr], sd0[dr].rearrange("h d v -> d h v"), (), [f"S{dr}"])
                B.CP(Sb[dr], Sst[dr], [f"S{dr}"], [f"Sb{dr}"])

            def dn_stream(dr):
                X = f"x{dr}"
                pb = [ps[4 * dr + i] for i in range(4)]
                pkb = [pk[4 * dr + i] for i in range(4)]
                A_, B_, C_, D_ = 0, 1, 2, 3
                qc = [carve([128, 4, 128], BF16) for _ in range(2)]
                kc_ = [carve([128, 4, 128], BF16) for _ in range(2)]
                ktc = [carve([128, 4, 128], BF16) for _ in range(2)]
                vtc = [carve([128, 4, 128], BF16) for _ in range(2)]
                wA = carve([128, 4, 128])
                wB = carve([128, 4, 128])
                EGr = carve([128, 4, 128])
                u_ = carve([128, 4, 128])
                ot = carve([128, 4, 128])
                QKDT = carve([128, 4, 128], BF16)
                P = [carve([128, 4, 128]) for _ in range(2)]
                PT = [carve([128, 4, 128]) for _ in range(2)]
                RT = [carve([128, 4, 128]) for _ in range(2)]
                MT = carve([128, 4, 128], BF16)
                kg = carve([128, 4, 128], BF16)
                kdec = carve([128, 4, 128], BF16)
                wT = carve([128, 4, 128], BF16)
                vnew = carve([128, 4, 128], BF16)
                qd = carve([128, 4, 128], BF16)
                order = range(NCH) if dr == 0 else range(NCH - 1, -1, -1)
                NEGs = cst[:, C_NLS:C_NLS + 128] if dr == 0 else cst[:, C_NUS:C_NUS + 128]
                NEGt = cst[:, C_NUI:C_NUI + 128] if dr == 0 else cst[:, C_NLI:C_NLI + 128]
                K = lambda nm: X + nm
                for vi, n in enumerate(order):
                    s = vi % 2
                    tsl = slice(n * 128, (n + 1) * 128)
                    B.DMA(f"d_qc{s}{X}", qc[s], qT_d[:, :, tsl].rearrange("h d t -> d h t"), [("qT", h) for h in range(4)], [K(f"qc{s}")])
                    B.DMA(f"d_kc{s}{X}", kc_[s], kT_d[:, :, tsl].rearrange("h d t -> d h t"), [("kT", h) for h in range(4)], [K(f"kc{s}")])
                    B.DMA(f"d_ktc{s}{X}", ktc[s], ktok_d[n], [("ktok", h) for h in range(4)], [K(f"ktc{s}")])
                    B.DMA(f"d_vtc{s}{X}", vtc[s], vtok_d[n], [("vtok", h) for h in range(4)], [K(f"vtc{s}")])
                    gcn = gc[:, dr, n, :]
                    for h in range(4):
                        B.MM(pb[A_][:, h * 128:(h + 1) * 128], kc_[s][:, h, :], kc_[s][:, h, :], [K(f"kc{s}")], [pkb[A_]])
                    for h in range(4):
                        B.MM(pb[B_][:, h * 128:(h + 1) * 128], kc_[s][:, h, :], qc[s][:, h, :], [K(f"kc{s}"), K(f"qc{s}")], [pkb[B_]])
                    B.TT(wA, bc4(I32), bcl(gcn), ALU.mult, ["cst", "gc"], [K("wA")])
                    yield
                    B.MM(pb[C_], ONE32, f3(wA), ["cst", K("wA")], [pkb[C_]])
                    yield
                    B.TT(wA, bc4(NEGs), v3(pb[C_]), ALU.subtract, ["cst", pkb[C_]], [K("wA")])
                    B.TT(wA, wA, bcl(gcn), ALU.add, [K("wA"), "gc"], [K("wA")])
                    B.TT(wB, v3(pb[C_]), bc4(NEGt), ALU.add, ["cst", pkb[C_]], [K("wB")])
                    B.TT(wB, wB, bcl(gcn), ALU.subtract, [K("wB"), "gc"], [K("wB")])
                    yield
                    B.ACT(f3(wA), f3(wA), AF.Exp, [K("wA")], [K("wA")])
                    B.ACT(f3(wB), f3(wB), AF.Exp, [K("wB")], [K("wB")])
                    B.ACT(f3(EGr), pb[C_], AF.Exp, [pkb[C_], K("wA"), K("wB")], [K("EGr")])
                    yield
                    B.TT(wA, v3(pb[A_]), wA, ALU.mult, [pkb[A_], K("wA")], [K("wA")])
                    B.TT(P[0], wA, bcl(nbeta[:, dr, n, :]), ALU.mult, [K("wA"), "nbeta"], [K("P0")])
                    B.STT(QKDT, v3(pb[B_]), scale, wB, ALU.mult, ALU.mult, [pkb[B_], K("wB")], [K("QKDT")])
                    yield
                    for h in range(4):
                        B.TR(pb[D_][:, h * 128:(h + 1) * 128], P[0][:, h, :], I32, [K("P0"), "cst"], [pkb[D_]])
                    yield
                    B.CP(PT[0], v3(pb[D_]), [pkb[D_]], [K("PT0")], eng="act")
                    yield
                    B.TT(RT[0], PT[0], bc4(I32), ALU.add, [K("PT0"), "cst"], [K("RT0")])
                    cur = 0
                    for it in range(6):
                        nx = 1 - cur
                        for h in range(4):
                            B.MM(pb[A_][:, h * 128:(h + 1) * 128], PT[cur][:, h, :], P[cur][:, h, :], [K(f"PT{cur}"), K(f"P{cur}")], [pkb[A_]])
                        if it < 5:
                            for h in range(4):
                                B.MM(pb[B_][:, h * 128:(h + 1) * 128], P[cur][:, h, :], PT[cur][:, h, :], [K(f"PT{cur}"), K(f"P{cur}")], [pkb[B_]])
                        yield
                        B.CP(P[nx], v3(pb[A_]), [pkb[A_]], [K(f"P{nx}")], eng="act")
                        if it < 5:
                            B.CP(PT[nx], v3(pb[B_]), [pkb[B_]], [K(f"PT{nx}")], eng="dve")
                        yield
                        for h in range(4):
                            B.MM(pb[C_][:, h * 128:(h + 1) * 128], P[nx][:, h, :], RT[cur][:, h, :], [K(f"P{nx}"), K(f"RT{cur}")], [pkb[C_]])
                        yield
                        B.TT(RT[nx], RT[cur], v3(pb[C_]), ALU.add, [K(f"RT{cur}"), pkb[C_]], [K(f"RT{nx}")])
                        cur = nx
                    B.TT(MT, RT[cur], bcl(beta[:, dr, n, :]), ALU.mult, [K(f"RT{cur}"), "beta"], [K("MT")])
                    B.TT(kg, ktc[s], bcl(EG[:, dr, n, :]), ALU.mult, [K(f"ktc{s}"), "EG"], [K("kg")])
                    B.TT(kdec, ktc[s], bcl(EKD[:, dr, n, :]), ALU.mult, [K(f"ktc{s}"), "EKD"], [K("kdec")])
                    B.STT(qd, qc[s], scale, EGr, ALU.mult, ALU.mult, [K(f"qc{s}"), K("EGr")], [K("qd")])
                    yield
                    for h in range(4):
                        B.MM(pb[A_][:, h * 128:(h + 1) * 128], MT[:, h, :], vtc[s][:, h, :], [K("MT"), K(f"vtc{s}")], [pkb[A_]])
                    for h in range(4):
                        B.MM(pb[B_][:, h * 128:(h + 1) * 128], kg[:, h, :], MT[:, h, :], [K("MT"), K("kg")], [pkb[B_]])
                    yield
                    B.CP(u_, v3(pb[A_]), [pkb[A_]], [K("u")], eng="act")
                    B.CP(wT, v3(pb[B_]), [pkb[B_]], [K("wT")], eng="act")
                    yield
                    for h in range(4):
                        B.MM(pb[C_][:, h * 128:(h + 1) * 128], wT[:, h, :], Sb[dr][:, h, :], [K("wT"), f"Sb{dr}"], [pkb[C_]])
                    yield
                    B.TT(vnew, u_, v3(pb[C_]), ALU.subtract, [K("u"), pkb[C_]], [K("vnew")])
                    yield
                    for h in range(4):
                        B.MM(pb[A_][:, h * 128:(h + 1) * 128], Sb[dr][:, h, :], qd[:, h, :], [K("qd"), f"Sb{dr}"], [pkb[A_]], start=True, stop=False)
                        B.MM(pb[A_][:, h * 128:(h + 1) * 128], vnew[:, h, :], QKDT[:, h, :], [K("vnew"), K("QKDT")], [pkb[A_]], start=False, stop=True)
                    for h in range(4):
                        B.MM(pb[B_][:, h * 128:(h + 1) * 128], kdec[:, h, :], vnew[:, h, :], [K("kdec"), K("vnew")], [pkb[B_]])
                    B.TT(Sst[dr], Sst[dr], bcl(EGL[:, dr, n, :]), ALU.mult, [f"S{dr}", "EGL"], [f"S{dr}"])
                    yield
                    B.TT(Sst[dr], Sst[dr], v3(pb[B_]), ALU.add, [f"S{dr}", pkb[B_]], [f"S{dr}"])
                    seg_end = (n % 2 == 1) if dr == 0 else (n % 2 == 0)
                    if seg_end:
                        B.DMA(f"d_od{dr}", od[n // 2, dr].rearrange("h d v -> d h v"), Sst[dr], [f"S{dr}"], [])
                        B.TS(Sst[dr], Sst[dr], knc[:, 0:1], ALU.mult, [f"S{dr}", "knc"], [f"S{dr}"])
                    B.CP(Sb[dr], Sst[dr], [f"S{dr}"], [f"Sb{dr}"], eng="act")
                    B.CP(ot, v3(pb[A_]), [pkb[A_]], [K("ot")], eng="act")
                    B.DMA(f"d_ot{X}", (oTf_d if dr == 0 else oTb_d)[n], ot, [K("ot")], [("oT", dr, n)])
                    yield

            gens = [dn_stream(0), dn_stream(1)]
            alive = [True, True]
            import os
            if os.environ.get("DN_SEQ"):
                ny = int(os.environ.get("DN_YIELDS", "100000"))
                for g in gens:
                    for i, _ in enumerate(g):
                        if i + 1 >= ny:
                            break
                alive = [False, False]
            while any(alive):
                for gi, g in enumerate(gens):
                    if alive[gi]:
                        try:
                            next(g)
                        except StopIteration:
                            alive[gi] = False
            ofc = [carve([128, 4, 128]) for _ in range(2)]
            obc = [carve([128, 4, 128]) for _ in range(2)]
            zc = [carve([128, 4, 128], BF16) for _ in range(2)]
            osq = [carve([128, 4, 128]) for _ in range(2)]
            for n in range(NCH):
                s = n % 2
                tsl = slice(n * 128, (n + 1) * 128)
                B.DMA(f"d_ofc{s}", ofc[s], oTf_d[n], [("oT", 0, n)], [f"ofc{s}"])
                B.DMA(f"d_obc{s}", obc[s], oTb_d[n], [("oT", 1, n)], [f"obc{s}"])
                B.DMA(f"d_zc{s}", zc[s], zs_d[:, :, tsl].rearrange("h d t -> d h t"), [("zs", h) for h in range(4)], [f"zc{s}"])
                B.TT(ofc[s], ofc[s], obc[s], ALU.add, [f"ofc{s}", f"obc{s}"], [f"ofc{s}"])
                B.ACT(f3(osq[s]), f3(ofc[s]), AF.Square, [f"ofc{s}"], [f"osq{s}"])
                B.MM(ps[s], ONE32, f3(osq[s]), ["cst", f"osq{s}"], [pk[s]])
                B.ACT(f3(osq[s]), ps[s], AF.Sqrt, [pk[s], "epsc"], [f"osq{s}"], bias=epsc[:, 0:1], scale=1.0 / 128)
                B.RCP(osq[s], osq[s], [f"osq{s}"], [f"osq{s}"])
                B.TT(ofc[s], ofc[s], osq[s], ALU.mult, [f"ofc{s}", f"osq{s}"], [f"ofc{s}"])
                B.STT(mixT[:, 0:4, tsl], ofc[s], vB[:, 60:61], zc[s], ALU.mult, ALU.mult, [f"ofc{s}", "vB", f"zc{s}"], ["mixT"])

            if self.stop_after == "dn":
                return
            S.barrier()
            self._aoff = off_persist
            qr = carve([128, 4, T], BF16)
            kr = carve([128, 2, T + 256], BF16)
            vat = carve([128, 18, 2, 128], BF16)
            mb = carve([128, 18 * 8])
            B.DMA("d_mb", mb, maskb, (), ["mb"])
            cosT = carve([128, T])
            sinT = carve([128, T])
            B.DMA("d_cs", cosT, ropec, (), ["cosT"])
            B.DMA("d_cs", sinT, ropes, (), ["sinT"])
            a32_ = [carve([128, 512]) for _ in range(2)]
            asq_ = [carve([128, 512]) for _ in range(2)]
            ars_ = [carve([128, 512]) for _ in range(2)]
            arot_ = [carve([128, 512]) for _ in range(2)]
            kst_ = [carve([128, 4, 128]) for _ in range(2)]
            idx = 0
            for j in range(6):
                c0 = 2064 + j * 128
                nw = vB[:, 61:62] if j < 4 else vB[:, 62:63]

                def consq(p, pkk, tb, j=j, nw=nw):
                    q = tb % 2
                    a32, asq, ars, arot, kst = a32_[q], asq_[q], ars_[q], arot_[q], kst_[q]
                    ka, ks, kr_, ko, kk = f"a32{q}", f"asq{q}", f"ars{q}", f"arot{q}", f"kst{q}"
                    tsl = slice(tb * 512, (tb + 1) * 512)
                    B.ACT(asq, p, AF.Square, [pkk], [ks])
                    B.MM(ps[2 + q], ONE32, asq, ["cst", ks], [pk[2 + q]])
                    B.ACT(ars, ps[2 + q], AF.Sqrt, [pk[2 + q], "epsc"], [kr_], bias=epsc[:, 0:1], scale=1.0 / 128)
                    B.RCP(ars, ars, [kr_], [kr_])
                    B.STT(a32, p, nw, ars, ALU.mult, ALU.mult, [pkk, "vB", kr_], [ka])
                    if j >= 4:
                        for k in range(4):
                            B.TR(ps[6 + q][:, k * 128:(k + 1) * 128], a32[:, k * 128:(k + 1) * 128], I32, [ka, "cst"], [pk[6 + q]])
                        B.CP(kst, v3(ps[6 + q]), [pk[6 + q]], [kk], eng="act")
                        B.DMA(f"d_kst{q}", ok.rearrange("(n p) f -> p n f", p=128)[:, tb * 4:(tb + 1) * 4, (j - 4) * 128:(j - 3) * 128],
                              kst, [kk], [])
                    B.MM(ps[4 + q], cst[:, C_ROT:C_ROT + 128], a32, ["cst", ka], [pk[4 + q]])
                    B.TT(arot, ps[4 + q], sinT[:, tsl], ALU.mult, [pk[4 + q], "sinT"], [ko])
                    B.TT(a32, a32, cosT[:, tsl], ALU.mult, [ka, "cosT"], [ka])
                    dst = qr[:, j, tsl] if j < 4 else kr[:, j - 4, tsl]
                    B.TT(dst, a32, arot, ALU.add, [ka, ko], ["qr" if j < 4 else "kr"])
                proj_fm(W, c0, 128, wsl, idx, consq)
                idx += 1
            wv = carve([128, 8, 256], BF16)
            vst32 = [carve([128, 256]) for _ in range(2)]

            def consv(p, pkk, tt):
                q = tt % 2
                B.CP(vst32[q], p[:, 0:256], [pkk], [f"vst{q}"], eng="act")
                B.CP(vat[:, tt].rearrange("p a b -> p (a b)"), vst32[q], [f"vst{q}"], ["vat"])
                B.DMA(f"d_vst{q}", ov[tt * 128:(tt + 1) * 128, :], vst32[q], [f"vst{q}"], [])
            proj_tm(W, 2064 + 768, 256, wv, "wv", consv, pbase=4)
            ckt = carve([128, 2, 256])
            B.DMA("d_ck", ckt, ck.rearrange("(n p) f -> p n f", p=128), (), ["ckt"])
            for kv in range(2):
                for n2 in range(2):
                    B.TR(ps[3][:, (kv * 2 + n2) * 128:(kv * 2 + n2 + 1) * 128], ckt[:, n2, kv * 128:(kv + 1) * 128], I32, ["ckt", "cst"], [pk[3]])
            B.CP(kr[:, :, T:T + 256], ps[3].rearrange("p (a b) -> p a b", a=2), [pk[3]], ["kr"])
            B.DMA("d_cv", vat[:, 16:18].rearrange("p n a b -> p n (a b)"), cv.rearrange("(n p) f -> p n f", p=128), (), ["vat"], queue="pool")
            pT = [carve([128, 512], BF16) for _ in range(4)]
            rden = [carve([128, 512]) for _ in range(2)]
            asc = 128.0 ** -0.5
            iters = [(kv, qb, kt) for kv in range(2) for qb in range(8) for kt in range(18)]

            def emit_scores(it):
                kv, qb, kt = iters[it]
                pb = it % 3
                qsl = slice(qb * 256, (qb + 1) * 256)
                for g in range(2):
                    B.MM(ps[pb][:, g * 256:(g + 1) * 256], kr[:, kv, kt * 128:(kt + 1) * 128], qr[:, kv * 2 + g, qsl],
                         ["kr", "qr"], [pk[pb]])

            import os as _os
            AHEAD = int(_os.environ.get('ATT_AHEAD', '1'))
            for it in range(min(AHEAD, len(iters))):
                emit_scores(it)
            for it, (kv, qb, kt) in enumerate(iters):
                if it + AHEAD < len(iters):
                    emit_scores(it + AHEAD)
                pb = it % 3
                r4 = it % 4
                acc = (kv * 8 + qb) % 2
                pn, pd = 4 + 2 * acc, 5 + 2 * acc
                qsl = slice(qb * 256, (qb + 1) * 256)
                B.ACT(pT[r4], ps[pb], AF.Exp, [pk[pb], "mb"], [f"pT{r4}"], bias=mb[:, kt * 8 + qb:kt * 8 + qb + 1], scale=asc)
                B.MM(ps[pn], vat[:, kt, kv, :], pT[r4], ["vat", f"pT{r4}"], [pk[pn]], start=(kt == 0), stop=(kt == 17))
                B.MM(ps[pd], ONEb, pT[r4], ["cstb", f"pT{r4}"], [pk[pd]], start=(kt == 0), stop=(kt == 17))
                if kt == 17:
                    B.RCP(rden[acc], ps[pd], [pk[pd]], [f"rden{acc}"])
                    for g in range(2):
                        B.TT(mixT[:, 4 + kv * 2 + g, qsl], ps[pn][:, g * 256:(g + 1) * 256], rden[acc][:, g * 256:(g + 1) * 256], ALU.mult,
                             [pk[pn], f"rden{acc}"], ["mixT"])
            if self.stop_after == "attn":
                return
            S.barrier()
            self._aoff = off_persist
            wo = carve([128, 8, 1024], BF16)
            B.DMAS("d_wo", [(wo[:, c, :], ab_out[0][c * 128:(c + 1) * 128, :]) for c in range(8)], (), ["wbuf"], queue="pool")
            res_proj_tb(8, mixT, "mixT", mod(0, 2), wo, (0, 1))

        def mixer_c():
            new_phase()
            W = ml_in[0]
            mqT_d = B.scratch("mqT_d", [8, 64, T], BF16)
            mkT_d = B.scratch("mkT_d", [8, 64, T], BF16)
            mk_d = B.scratch("mk_d", [NCH, 128, 8, 64], BF16)
            mv_d = B.scratch("mv_d", [NCH, 128, 8, 128], BF16)
            mo_d = B.scratch("mo_d", [NCH, 128, 8, 128], BF16)
            hf_d = B.scratch("hf_d", [NCH, 128, 8, 128], F32)
            mixT = carve([128, 8, T], BF16)
            wsl = [carve([128, 8, 128], BF16) for _ in range(2)]
            off_persist = self._aoff
            nb_ = carve([64, T], BF16)
            idx = 0
            for grp in range(2):
                for h in range(8):
                    def cons(p, pkk, tb, grp=grp):
                        if grp == 0:
                            B.ACT(nb_[:, tb * 512:(tb + 1) * 512], p[0:64, :], AF.Copy, [pkk], ["nb"], scale=0.125)
                        else:
                            B.CP(nb_[:, tb * 512:(tb + 1) * 512], p[0:64, :], [pkk], ["nb"], eng="act")
                    proj_fm(W, grp * 512 + h * 64, 64, wsl, idx, cons)
                    idx += 1
                    B.DMA("d_mqk", (mqT_d if grp == 0 else mkT_d)[h], nb_, ["nb"], [("mq" if grp == 0 else "mk", h)])
            wbig = carve([128, 8, 512], BF16)
            st = [carve([128, 512], BF16) for _ in range(2)]
            jobs = [(512, mk_d, "p (h k) -> p h k", 64, 0, 8, "mktok", None),
                    (1024, mv_d, "p (h k) -> p h k", 128, 0, 4, "mvtok", None),
                    (1536, mv_d, "p (h k) -> p h k", 128, 4, 4, "mvtok", None),
                    (2048, mo_d, "p (h k) -> p h k", 128, 0, 4, "motok", AF.Sigmoid),
                    (2560, mo_d, "p (h k) -> p h k", 128, 4, 4, "motok", AF.Sigmoid)]
            for ji, (c0, dst, pat, kk, h0, nh, key, fn) in enumerate(jobs):
                def const(p, pkk, tt, dst=dst, kk=kk, h0=h0, nh=nh, key=key, fn=fn, ji=ji):
                    q = tt % 2
                    B.ACT(st[q], p, fn if fn is not None else AF.Copy, [pkk], [f"st{q}"])
                    B.DMA(f"d_st{q}", dst[tt][:, h0:h0 + nh, :], st[q].rearrange("p (h k) -> p h k", k=kk), [f"st{q}"], [(key, tt, ji)])
                proj_tm(W, c0, 512, wbig, "wbig", const)
            wif = carve([128, 8, 32], BF16)
            gif = carve([128, NCH, 32])
            b1 = carve([128, 32])
            B.DMA("d_b1", b1, bc1.partition_broadcast(128), (), ["b1"])

            def consif(p, pkk, tt):
                B.CP(gif[:, tt, :], p[:, 0:32], [pkk], ["gif"])
            proj_tm(W, 3072, 32, wif, "wif", consif)
            li = carve([128, 2, NCH, 8])
            lf = carve([128, 2, NCH, 8])
            t16 = carve([128, NCH, 16])
            B.TT(t16, gif[:, :, 0:16], b1[:, 0:16].unsqueeze(1).to_broadcast([128, NCH, 16]), ALU.add, ["gif", "b1"], ["t16"])
            for dr in range(2):
                B.CP(li[:, dr], t16[:, :, dr * 8:(dr + 1) * 8], ["t16"], ["li"])
            B.TT(t16, gif[:, :, 16:32], b1[:, 16:32].unsqueeze(1).to_broadcast([128, NCH, 16]), ALU.add, ["gif", "b1"], ["t16"])
            B.ACT(t16, t16, AF.Exp, ["t16"], ["t16"], scale=-1.0)
            B.ACT(t16, t16, AF.Ln, ["t16", "cst"], ["t16"], bias=ONE32[:, 0:1])
            for dr in range(2):
                B.TS(lf[:, dr], t16[:, :, dr * 8:(dr + 1) * 8], -1.0, ALU.mult, ["t16"], ["lf"])
            bb = carve([128, 2, NCH, 8])
            bl = carve([128, 2, NCH, 8])
            aa = carve([128, 2, NCH, 8])
            lw = carve([128, 2, NCH, 8])
            mend = carve([128, 2, NCH, 8])
            g2 = lambda a, dr: a[:, dr].rearrange("p n h -> p (n h)")
            for dr in range(2):
                tri = cst[:, C_U:C_U + 128] if dr == 0 else cst[:, C_LO:C_LO + 128]
                B.MM(ps[2][:, 0:128], tri, g2(lf, dr), ["cst", "lf"], [pk[2]])
                B.CP(g2(bb, dr), ps[2][:, 0:128], [pk[2]], ["bb"])
                B.MM(ps[2][:, 0:128], ONE32, g2(lf, dr), ["cst", "lf"], [pk[2]])
                B.CP(g2(bl, dr), ps[2][:, 0:128], [pk[2]], ["bl"])
            f2 = lambda a: a.rearrange("p d n h -> p (d n h)")
            B.TT(f2(aa), f2(li), f2(bb), ALU.subtract, ["li", "bb"], ["aa"])
            B.TT(f2(lw), f2(aa), f2(bl), ALU.add, ["aa", "bl"], ["lw"])
            mx = carve([128, 1])
            mxb = carve([128, 128])
            for dr in range(2):
                B.TR(ps[2][:, 0:128], g2(lw, dr), I32, ["lw", "cst"], [pk[2]])
                B.RED(mx, ps[2][:, 0:128], ALU.max, [pk[2]], ["mx"])
                B.CP(mxb, mx[:, 0:1].to_broadcast([128, 128]), ["mx"], ["mxb"])
                B.TR(ps[2][:, 0:128], mxb, I32, ["mxb", "cst"], [pk[2]])
                B.CP(g2(mend, dr), ps[2][:, 0:128], [pk[2]], ["mend"])

            Cst = [carve([64, 8, 128]) for _ in range(2)]
            Cb = [carve([64, 8, 128], BF16) for _ in range(2)]
            nst = [carve([64, 8]) for _ in range(2)]
            nbf = [carve([64, 8], BF16) for _ in range(2)]
            mst = [carve([128, 8]) for _ in range(2)]
            for dr in range(2):
                B.DMA("d_c0", Cst[dr], mC0[dr].rearrange("h k v -> k h v"), (), [f"C{dr}"])
                B.DMA("d_c0", nst[dr], mn0[dr].rearrange("h k -> k h"), (), [f"n{dr}"], allow_slow_non_contiguous=True)
                B.DMA("d_c0", mst[dr], mm0[dr].partition_broadcast(128), (), [f"m{dr}"])
                B.CP(Cb[dr], Cst[dr], [f"C{dr}"], [f"Cb{dr}"])
                B.CP(nbf[dr], nst[dr], [f"n{dr}"], [f"nb{dr}"])
            nwb = carve([128, 128])
            B.DMA("d_b1", nwb, mlnw.partition_broadcast(128), (), ["nwb"])
            qc = [carve([64, 8, 128], BF16) for _ in range(2)]
            kc_ = [carve([64, 8, 128], BF16) for _ in range(2)]
            ktc = [carve([128, 8, 64], BF16) for _ in range(2)]
            vtc = [carve([128, 8, 128], BF16) for _ in range(2)]
            oc = [carve([128, 8, 128], BF16) for _ in range(2)]
            hfc = [carve([128, 8, 128]) for _ in range(2)]
            dg = carve([128, 4, 128])
            LD = carve([128, 8, 128])
            s32 = carve([128, 8, 128])
            sbf = carve([128, 8, 128], BF16)
            sT = carve([128, 8, 128], BF16)
            num = carve([128, 8, 128])
            hh = carve([128, 8, 128])
            wk = carve([128, 8, 64], BF16)
            sm = {k: carve([128, 8]) for k in ["mintra", "min", "mt", "winter", "emt", "rsum", "den", "ew", "mnew", "astate", "ssq", "rstd"]}
            f3 = lambda a: a.rearrange("p h t -> p (h t)")
            visit = 0
            for dr in range(2):
                order = range(NCH) if dr == 0 else range(NCH - 1, -1, -1)
                NEGi = cst[:, C_NLI:C_NLI + 128] if dr == 0 else cst[:, C_NUI:C_NUI + 128]
                for n in order:
                    s = visit % 2
                    visit += 1
                    tsl = slice(n * 128, (n + 1) * 128)
                    B.DMA(f"d_qc{s}", qc[s], mqT_d[:, :, tsl].rearrange("h d t -> d h t"), [("mq", h) for h in range(8)], [f"qc{s}"])
                    B.DMA(f"d_kc{s}", kc_[s], mkT_d[:, :, tsl].rearrange("h d t -> d h t"), [("mk", h) for h in range(8)], [f"kc{s}"])
                    B.DMA(f"d_ktc{s}", ktc[s], mk_d[n], [("mktok", n, 0)], [f"ktc{s}"])
                    B.DMA(f"d_vtc{s}", vtc[s], mv_d[n], [("mvtok", n, 1), ("mvtok", n, 2)], [f"vtc{s}"])
                    if dr == 1:
                        B.DMA(f"d_zc{s}", oc[s], mo_d[n], [("motok", n, 3), ("motok", n, 4)], [f"oc{s}"])
                        B.DMA(f"d_ofc{s}", hfc[s], hf_d[n], [("hf", n)], [f"hfc{s}"])
                    bn = bb[:, dr, n, :]
                    for h in range(8):
                        B.MM(ps[h // 4][:, (h % 4) * 128:(h % 4 + 1) * 128], qc[s][:, h, :], kc_[s][:, h, :], [f"qc{s}", f"kc{s}"], [pk[h // 4]])
                    for hf in range(2):
                        B.TT(dg, bc4(I32), bcl(aa[:, dr, n, hf * 4:(hf + 1) * 4]), ALU.mult, ["cst", "aa"], ["dg"])
                        B.MM(ps[2 + hf], ONE32, f3(dg), ["cst", "dg"], [pk[2 + hf]])
                        B.TT(LD[:, hf * 4:(hf + 1) * 4, :], v3(ps[2 + hf]), bc4(NEGi), ALU.add, [pk[2 + hf], "cst"], ["LD"])
                    B.TT(LD, LD, bcl(bn), ALU.add, ["LD", "bb"], ["LD"])
                    B.RED(sm["mintra"], LD, ALU.max, ["LD"], ["mintra"])
                    B.TT(sm["min"], bn, mst[dr], ALU.add, ["bb", f"m{dr}"], ["min"])
                    B.TT(sm["mt"], sm["min"], sm["mintra"], ALU.max, ["min", "mintra"], ["mt"])
                    B.TT(sm["winter"], sm["min"], sm["mt"], ALU.subtract, ["min", "mt"], ["winter"])
                    B.ACT(sm["winter"], sm["winter"], AF.Exp, ["winter"], ["winter"])
                    B.ACT(sm["emt"], sm["mt"], AF.Exp, ["mt"], ["emt"], scale=-1.0)
                    B.TT(LD, LD, bcl(sm["mt"]), ALU.subtract, ["LD", "mt"], ["LD"])
                    B.ACT(f3(LD), f3(LD), AF.Exp, ["LD"], ["LD"])
                    for hf in range(2):
                        B.TT(s32[:, hf * 4:(hf + 1) * 4, :], v3(ps[hf]), LD[:, hf * 4:(hf + 1) * 4, :], ALU.mult, [pk[hf], "LD"], ["s32"])
                    B.RED(sm["rsum"], s32, ALU.add, ["s32"], ["rsum"])
                    B.CP(sbf, s32, ["s32"], ["sbf"], eng="act")
                    for hf in range(2):
                        pbt = ps[2 + hf].bitcast(BF16)
                        for k in range(4):
                            h = hf * 4 + k
                            B.TR(pbt[:, k * 128:(k + 1) * 128], sbf[:, h, :], Ib, ["sbf", "cstb"], [pk[2 + hf]])
                        B.CP(sT[:, hf * 4:(hf + 1) * 4, :], v3(pbt[:, 0:512]), [pk[2 + hf]], ["sT"], eng="act")
                    for h in range(8):
                        B.MM(ps[4 + h // 4][:, (h % 4) * 128:(h % 4 + 1) * 128], sT[:, h, :], vtc[s][:, h, :], ["sT", f"vtc{s}"], [pk[4 + h // 4]])
                    for h in range(8):
                        B.MM(ps[6 + h // 4][:, (h % 4) * 128:(h % 4 + 1) * 128], qc[s][:, h, :], Cb[dr][:, h, :], [f"qc{s}", f"Cb{dr}"], [pk[6 + h // 4]])
                    for h in range(8):
                        B.MM(ps[0][:, h:h + 1], qc[s][:, h, :], nbf[dr][:, h:h + 1], [f"qc{s}", f"nb{dr}"], [pk[0]])
                    for hf in range(2):
                        hs_ = slice(hf * 4, (hf + 1) * 4)
                        B.TT(num[:, hs_, :], v3(ps[6 + hf]), bcl(sm["winter"][:, hs_]), ALU.mult, [pk[6 + hf], "winter"], ["num"])
                        B.TT(num[:, hs_, :], num[:, hs_, :], v3(ps[4 + hf]), ALU.add, ["num", pk[4 + hf]], ["num"])
                    B.TT(sm["den"], ps[0][:, 0:8], sm["winter"], ALU.mult, [pk[0], "winter"], ["den"])
                    B.TT(sm["den"], sm["den"], sm["rsum"], ALU.add, ["den", "rsum"], ["den"])
                    B.TS(sm["ssq"], sm["den"], -1.0, ALU.mult, ["den"], ["ssq"])
                    B.TT(sm["den"], sm["den"], sm["ssq"], ALU.max, ["den", "ssq"], ["den"])
                    B.TT(sm["den"], sm["den"], sm["emt"], ALU.max, ["den", "emt"], ["den"])
                    B.RCP(sm["den"], sm["den"], ["den"], ["den"])
                    B.TT(hh, num, bcl(sm["den"]), ALU.mult, ["num", "den"], ["hh"])
                    B.TT(sm["mnew"], bl[:, dr, n, :], mst[dr], ALU.add, ["bl", f"m{dr}"], ["mnew"])
                    B.TT(sm["astate"], sm["mnew"], sm["mnew"], ALU.max, ["mnew"], ["astate"])
                    B.TT(sm["mnew"], sm["mnew"], mend[:, dr, n, :], ALU.max, ["mnew", "mend"], ["mnew"])
                    B.TT(sm["astate"], sm["astate"], sm["mnew"], ALU.subtract, ["astate", "mnew"], ["astate"])
                    B.ACT(sm["astate"], sm["astate"], AF.Exp, ["astate"], ["astate"])
                    B.TT(sm["ew"], lw[:, dr, n, :], sm["mnew"], ALU.subtract, ["lw", "mnew"], ["ew"])
                    B.ACT(sm["ew"], sm["ew"], AF.Exp, ["ew"], ["ew"])
                    B.TT(wk, ktc[s], bcl(sm["ew"], 64), ALU.mult, [f"ktc{s}", "ew"], ["wk"])
                    for h in range(8):
                        B.MM(ps[2 + h // 4][0:64, (h % 4) * 128:(h % 4 + 1) * 128], wk[:, h, :], vtc[s][:, h, :], ["wk", f"vtc{s}"], [pk[2 + h // 4]])
                    for h in range(8):
                        B.MM(ps[1][0:64, h:h + 1], wk[:, h, :], ONEb[:, 0:1], ["wk", "cstb"], [pk[1]])
                    B.TT(Cst[dr], Cst[dr], bcl(sm["astate"][0:64, :]), ALU.mult, [f"C{dr}", "astate"], [f"C{dr}"])
                    for hf in range(2):
                        hs_ = slice(hf * 4, (hf + 1) * 4)
                        B.TT(Cst[dr][:, hs_, :], Cst[dr][:, hs_, :], v3(ps[2 + hf][0:64, :]), ALU.add, [f"C{dr}", pk[2 + hf]], [f"C{dr}"])
                    B.TT(nst[dr], nst[dr], sm["astate"][0:64, :], ALU.mult, [f"n{dr}", "astate"], [f"n{dr}"])
                    B.TT(nst[dr], nst[dr], ps[1][0:64, 0:8], ALU.add, [f"n{dr}", pk[1]], [f"n{dr}"])
                    B.CP(mst[dr], sm["mnew"], ["mnew"], [f"m{dr}"])
                    seg_end = (n % 2 == 1) if dr == 0 else (n % 2 == 0)
                    if seg_end:
                        sg_ = n // 2
                        B.DMA(f"d_oC{dr}", oC[sg_, dr].rearrange("h k v -> k h v"), Cst[dr], [f"C{dr}"], [])
                        B.DMA(f"d_on{dr}", on[sg_, dr].rearrange("h k -> k h"), nst[dr], [f"n{dr}"], [], allow_slow_non_contiguous=True)
                        B.DMA(f"d_om{dr}", om[sg_, dr:dr + 1, :], mst[dr][0:1, :], [f"m{dr}"], [])
                        B.TS(Cst[dr], Cst[dr], knc[0:64, 0:1], ALU.mult, [f"C{dr}", "knc"], [f"C{dr}"])
                        B.TS(nst[dr], nst[dr], knc[0:64, 0:1], ALU.mult, [f"n{dr}", "knc"], [f"n{dr}"])
                        B.TS(mst[dr], mst[dr], knc[:, 0:1], ALU.mult, [f"m{dr}", "knc"], [f"m{dr}"])
                    B.CP(Cb[dr], Cst[dr], [f"C{dr}"], [f"Cb{dr}"], eng="act")
                    B.CP(nbf[dr], nst[dr], [f"n{dr}"], [f"nb{dr}"], eng="act")
                    if dr == 0:
                        B.DMA("d_hf", hf_d[n], hh, ["hh"], [("hf", n)])
                    else:
                        B.TT(hh, hh, hfc[s], ALU.add, ["hh", f"hfc{s}"], ["hh"])
                        B.ACT(f3(num), f3(hh), AF.Square, ["hh"], ["num"])
                        B.RED(sm["ssq"], num, ALU.add, ["num"], ["ssq"])
                        B.ACT(sm["rstd"], sm["ssq"], AF.Sqrt, ["ssq", "epsc"], ["rstd"], bias=epsc[:, 0:1], scale=1.0 / 128)
                        B.RCP(sm["rstd"], sm["rstd"], ["rstd"], ["rstd"])
                        B.TT(hh, hh, bcl(sm["rstd"]), ALU.mult, ["hh", "rstd"], ["hh"])
                        B.TT(hh, hh, nwb.unsqueeze(1).to_broadcast([128, 8, 128]), ALU.mult, ["hh", "nwb"], ["hh"])
                        B.TT(sbf, hh, oc[s], ALU.mult, ["hh", f"oc{s}"], ["sbf"])
                        for hf in range(2):
                            pbt = ps[hf].bitcast(BF16)
                            for k in range(4):
                                h = hf * 4 + k
                                B.TR(pbt[:, k * 128:(k + 1) * 128], sbf[:, h, :], Ib, ["sbf", "cstb"], [pk[hf]])
                            B.CP(mixT[:, hf * 4:(hf + 1) * 4, tsl], v3(pbt[:, 0:512]), [pk[hf]], ["mixT"], eng="act")
            S.barrier()
            self._aoff = off_persist
            wo = carve([128, 8, 1024], BF16)
            B.DMAS("d_wo", [(wo[:, c, :], ml_out[0][c * 128:(c + 1) * 128, :]) for c in range(8)], (), ["wbuf"], queue="pool")
            res_proj_tb(8, mixT, "mixT", mod(1, 2), wo, (1, 1))

        def run():
            ada_phase(0)
            ada_phase(1)
            norm_phase(0, 0)
            mixer_ab()
            if self.stop_after in ("dn_pre", "dn", "attn", "mix0"):
                return
            ffn_phase(0)
            if self.stop_after == "l0":
                return
            norm_phase(1, 0)
            mixer_c()
            if self.stop_after == "mix1":
                return
            ffn_phase(1)
        run()
        if self.stop_after is not None:
            S.barrier()
            self._aoff = 0
            dbg = carve([128, 8, 512])
            for tb in range(4):
                B.DMA("d_dbg", dbg, xT_d[:, :, tb * 512:(tb + 1) * 512].rearrange("c p t -> p c t"), (), ["dbg"])
                B.DMA("d_dbg2", self.dbg_out(tb), dbg, ["dbg"], [])
        S.wait_all_dma("sp")
        with contextlib.ExitStack() as es:
            sems = {}
            for k in list(S.ENGS) + list(S.dma_cum.keys()):
                sems[k] = es.enter_context(nc.semaphore(str(k)))
            block = es.enter_context(nc.Block())
            S.emit(block, sems)
        return nc

    def dbg_out(self, tb):
        yv = self.dout["y"].rearrange("(c p q) d -> c p (q d)", c=8, p=128)
        return yv[:, :, tb * 512:(tb + 1) * 512].rearrange("c p t -> p c t")


def rope_tables(sample):
    if not sample:
        return np.ones((128, T), np.float32), np.zeros((128, T), np.float32)
    t = np.arange(T)
    rows = (t // 64).astype(np.float32)
    cols = (t % 64).astype(np.float32)
    nf = 32
    inv = (10000.0 ** (-np.arange(nf, dtype=np.float32) / nf)).astype(np.float32)
    ang = np.zeros((128, T), np.float32)
    for p in range(128):
        pos = rows if p < 64 else cols
        ang[p] = pos * inv[p % 32]
    return np.cos(ang).astype(np.float32), np.sin(ang).astype(np.float32)


_CACHE = {}


def kernel(**inp):
    f = lambda k: np.ascontiguousarray(np.asarray(inp[k], dtype=np.float32))
    stop_after = inp.get("_stop_after", None)
    key = ("nc", stop_after)
    if key not in _CACHE:
        _CACHE[key] = Builder(stop_after).build()
    nc = _CACHE[key]
    consts = make_consts()
    xp, xs = f("x_prompt"), f("x_sample")
    shared = {k: f(k) for k in ["ada_w", "ffn_w_gate", "ffn_w_up", "ffn_w_down", "ab_w_in", "ab_w_out", "ml_w_in", "ml_w_out"]}
    ada_b, n1, n2 = f("ada_b"), f("norm1_w"), f("norm2_w")
    vecB = np.concatenate([f("dn_conv_w")[0].reshape(5 * 12, 128), f("dn_norm_w")[0][None], f("at_q_norm")[0][None],
                           f("at_k_norm")[0][None]], axis=0)
    bc0 = np.concatenate([f("dn_A_log")[0].reshape(8), f("dn_dt_bias")[0].reshape(8)])
    bc1 = np.concatenate([f("ml_i_bias")[0].reshape(16), f("ml_f_bias")[0].reshape(16)])
    in_maps = []
    for c in range(8):
        sample = c >= 4
        m = dict(shared)
        m["consts"] = consts
        cond = f("c")[c - 4] if sample else f("c_ctx")
        m["vecA"] = np.stack([np.concatenate([ada_b[l].reshape(48, 128), n1[l].reshape(8, 128), n2[l].reshape(8, 128),
                                              cond.reshape(8, 128)], axis=0) for l in range(2)])
        m["vecB"] = vecB
        m["bc0"], m["bc1"], m["mlnw"] = bc0, bc1, f("ml_norm_w")[0]
        m["knf"] = np.array([1.0 if sample else 0.0], np.float32)
        mb = np.zeros((128, 18, 8), np.float32)
        if not sample:
            mb[:] = NEG
            for qb in range(8):
                mb[:, 2 * qb:2 * qb + 2, qb] = 0.0
        m["maskb"] = mb.reshape(128, 144)
        m["ropec"], m["ropes"] = rope_tables(sample)
        if sample:
            b = c - 4
            m["xin"] = xs[b]
            m["ck"] = f("cache_attn_k")[b, 0].reshape(256, 256)
            m["cv"] = f("cache_attn_v")[b, 0].reshape(256, 256)
            m["sd0"] = f("state_delta")[b, 0]
            m["mC0"] = f("state_mlstm_C")[b, 0]
            m["mn0"] = f("state_mlstm_n")[b, 0]
            m["mm0"] = f("state_mlstm_m")[b, 0]
        else:
            m["xin"] = xp[8 * c:8 * c + 8].reshape(T, D)
            m["ck"] = np.zeros((256, 256), np.float32)
            m["cv"] = np.zeros((256, 256), np.float32)
            m["sd0"] = np.zeros((2, 4, 128, 128), np.float32)
            m["mC0"] = np.zeros((2, 8, 64, 128), np.float32)
            m["mn0"] = np.zeros((2, 8, 64), np.float32)
            m["mm0"] = np.zeros((2, 8), np.float32)
        in_maps.append({k: np.ascontiguousarray(v) for k, v in m.items()})
    res = run_bass_kernel_spmd(nc, in_maps, core_ids=list(range(8)))
    R = res.results
    if stop_after is not None:
        return R
    y_prompt = np.concatenate([R[c]["y"].reshape(8, 256, D) for c in range(4)], axis=0)
    y_sample = np.stack([R[c]["y"] for c in range(4, 8)], axis=0)
    nk = np.concatenate([R[c]["ok"].reshape(8, 1, 256, 2, 128) for c in range(4)], axis=0)
    nv = np.concatenate([R[c]["ov"].reshape(8, 1, 256, 2, 128) for c in range(4)], axis=0)
    nd = np.concatenate([R[c]["od"].reshape(8, 1, 2, 4, 128, 128) for c in range(4)], axis=0)
    nC = np.concatenate([R[c]["oC"].reshape(8, 1, 2, 8, 64, 128) for c in range(4)], axis=0)
    nn = np.concatenate([R[c]["on"].reshape(8, 1, 2, 8, 64) for c in range(4)], axis=0)
    nm = np.concatenate([R[c]["om"].reshape(8, 1, 2, 8) for c in range(4)], axis=0)
    return tuple(np.ascontiguousarray(a, dtype=np.float32) for a in (y_prompt, y_sample, nk, nv, nd, nC, nn, nm))
```

```python
import contextlib
import numpy as np
import concourse.bass as bass
import concourse.mybir as mybir
from concourse.bass_utils import run_bass_kernel_spmd

F32 = mybir.dt.float32
BF16 = mybir.dt.bfloat16
AF = mybir.ActivationFunctionType
ALU = mybir.AluOpType
AX = mybir.AxisListType

D = 1024
T = 2048
NCH = 16
FF = 2816
NFT = 22
AB_IN = 3088
ML_IN = 3104
EPS = 1e-6
NEG = -30000.0

SAME_ENGINE_SYNC = {"act": True, "dve": True, "pool": True, "pe": False, "sp": False}


class Sched:
    ENGS = ("pe", "act", "dve", "pool", "sp")

    def __init__(self, nc):
        self.nc = nc
        self.ops = {e: [] for e in self.ENGS}
        self.n = {e: 0 for e in self.ENGS}
        self.waited = {e: {} for e in self.ENGS}
        self.last_w = {}
        self.readers = {}
        self.dma_cum = {}
        self.needed = {e: set() for e in self.ENGS}
        self.final_waits = []
        self.final_eng = None
        self.fence = {}
        self.phys_map = {}

    def _deps(self, eng, reads, writes):
        deps = []
        for k in reads:
            t = self.last_w.get(k)
            if t is not None:
                deps.append(t)
        for k in writes:
            t = self.last_w.get(k)
            if t is not None:
                deps.append(t)
            deps.extend(self.readers.get(k, ()))
        best = {}
        for (sk, v) in deps:
            if sk == eng and not SAME_ENGINE_SYNC.get(eng, False):
                continue
            if self.waited[eng].get(sk, 0) >= v:
                continue
            best[sk] = max(best.get(sk, 0), v)
        for sk, v in best.items():
            self.waited[eng][sk] = v
            if sk in self.ENGS:
                self.needed[sk].add(v)
        return list(best.items())

    def _commit(self, tok, reads, writes):
        for k in writes:
            self.last_w[k] = tok
            self.readers[k] = []
        for k in reads:
            if k in writes:
                continue
            self.readers.setdefault(k, []).append(tok)

    def op(self, eng, fn, reads=(), writes=(), nofence=False):
        waits = self._deps(eng, reads, writes)
        self.n[eng] += 1
        tok = (eng, self.n[eng])
        self.ops[eng].append((waits, fn, ("self", self.n[eng], nofence)))
        self._commit(tok, reads, writes)
        return tok

    def dma(self, queue, semkey, items, reads=(), writes=()):
        pk_ = (queue, semkey)
        if pk_ not in self.phys_map:
            nq = sum(1 for q, _ in self.phys_map if q == queue)
            self.phys_map[pk_] = f"dma_{queue}{nq}"
        semkey = self.phys_map[pk_]
        waits = self._deps(queue, reads, writes)
        cum = self.dma_cum.get(semkey, 0)
        if cum > 0 and self.waited[queue].get(semkey, 0) < cum:
            self.waited[queue][semkey] = cum
            waits.append((semkey, cum))
        final = cum + 16 * len(items)
        self.dma_cum[semkey] = final
        for i, (o, a, kw) in enumerate(items):
            def fn(e, o=o, a=a, kw=kw):
                return e.dma_start(out=o, in_=a, **kw)
            self.ops[queue].append((waits if i == 0 else [], fn, ("dma", semkey)))
        tok = (semkey, final)
        self._commit(tok, reads, writes)
        return tok

    def barrier(self):
        for e in self.ENGS:
            waits = []
            for e2 in self.ENGS:
                if e2 == e or self.n[e2] == 0 or e2 == "sp":
                    continue
                if self.waited[e].get(e2, 0) < self.n[e2]:
                    waits.append((e2, self.n[e2]))
                    self.waited[e][e2] = self.n[e2]
                    self.needed[e2].add(self.n[e2])
            for sk, cum in self.dma_cum.items():
                if self.waited[e].get(sk, 0) < cum:
                    waits.append((sk, cum))
                    self.waited[e][sk] = cum
            if waits:
                self.ops[e].append((waits, None, None))
        self.last_w = {}
        self.readers = {}
        self.phys_map = {}

    def wait_all_dma(self, eng="sp"):
        self.final_waits = [(sk, v) for sk, v in self.dma_cum.items()]
        self.final_eng = eng

    def emit(self, block, sems):
        rank = {}
        for e in self.ENGS:
            rank[e] = {v: i + 1 for i, v in enumerate(sorted(self.needed[e]))}

        def val(sk, v):
            return rank[sk][v] if sk in self.ENGS else v

        def run(e, h):
            for waits, fn, inc in self.ops[e]:
                for sk, v in waits:
                    h.wait_ge(sems[sk], val(sk, v))
                if fn is None:
                    continue
                ins = fn(h)
                if inc[0] == "self":
                    if inc[1] in rank[e]:
                        if e in self.fence and not inc[2]:
                            ins = self.fence[e](h)
                        ins.then_inc(sems[e], 1)
                else:
                    ins.then_inc(sems[inc[1]], 16)
            if self.final_waits and self.final_eng == e:
                for sk, v in self.final_waits:
                    h.wait_ge(sems[sk], v)

        @block.tensor
        def _(t):
            run("pe", t)

        @block.scalar
        def _(s):
            run("act", s)

        @block.vector
        def _(v):
            run("dve", v)

        @block.gpsimd
        def _(g):
            run("pool", g)

        @block.sync
        def _(s):
            run("sp", s)


C_I, C_ONE, C_U, C_LO, C_NLI, C_NUI, C_NLS, C_NUS, C_ROT = [i * 128 for i in range(9)]
NCONST = 9 * 128


def make_consts():
    i = np.arange(128)[:, None]
    j = np.arange(128)[None, :]
    c = np.zeros((128, NCONST), np.float32)
    c[:, C_I:C_I + 128] = (i == j)
    c[:, C_ONE:C_ONE + 128] = 1.0
    c[:, C_U:C_U + 128] = (i <= j)
    c[:, C_LO:C_LO + 128] = (i >= j)
    c[:, C_NLI:C_NLI + 128] = np.where(i >= j, 0.0, NEG)
    c[:, C_NUI:C_NUI + 128] = np.where(i <= j, 0.0, NEG)
    c[:, C_NLS:C_NLS + 128] = np.where(i > j, 0.0, NEG)
    c[:, C_NUS:C_NUS + 128] = np.where(i < j, 0.0, NEG)
    R = np.zeros((128, 128), np.float32)
    for p in range(128):
        if (p % 64) < 32:
            R[p, p + 32] = -1.0
        else:
            R[p, p - 32] = 1.0
    c[:, C_ROT:C_ROT + 128] = R.T
    return c


class Builder:
    def __init__(self, stop_after=None):
        self.stop_after = stop_after
        nc = bass.Bass("TRN2", target_bir_lowering=False)
        self.nc = nc
        self.S = Sched(nc)
        self.din = {}
        self.dout = {}
        self._uid = 0

    def inp(self, name, shape):
        self.din[name] = self.nc.dram_tensor(name, list(shape), F32, kind="ExternalInput").ap()
        return self.din[name]

    def outp(self, name, shape):
        self.dout[name] = self.nc.dram_tensor(name, list(shape), F32, kind="ExternalOutput").ap()
        return self.dout[name]

    def scratch(self, name, shape, dt):
        return self.nc.dram_tensor(name, list(shape), dt).ap()

    def sb(self, name, shape, dt=F32):
        return self.nc.alloc_sbuf_tensor(name, list(shape), dt).ap()

    def MM(self, out, lhsT, rhs, r, w, start=True, stop=True):
        self.S.op("pe", lambda e: e.matmul(out, lhsT=lhsT, rhs=rhs, start=start, stop=stop), r, w)

    def TR(self, out, in_, ident, r, w):
        self.S.op("pe", lambda e: e.transpose(out, in_, ident), r, w)

    def ACT(self, out, in_, func, r, w, bias=None, scale=1.0, accum=None):
        kw = {}
        if bias is not None:
            kw["bias"] = bias
        if accum is not None:
            kw["accum_out"] = accum
        self.S.op("act", lambda e: e.activation(out=out, in_=in_, func=func, scale=scale, **kw), r, w)

    def TT(self, out, a, b, op, r, w, eng="dve"):
        self.S.op(eng, lambda e: e.tensor_tensor(out=out, in0=a, in1=b, op=op), r, w)

    def TS(self, out, a, s1, op0, r, w, s2=None, op1=None, eng="dve"):
        if op1 is None:
            self.S.op(eng, lambda e: e.tensor_scalar(out=out, in0=a, scalar1=s1, scalar2=None, op0=op0), r, w)
        else:
            self.S.op(eng, lambda e: e.tensor_scalar(out=out, in0=a, scalar1=s1, scalar2=s2, op0=op0, op1=op1), r, w)

    def STT(self, out, in0, scalar, in1, op0, op1, r, w, eng="dve"):
        self.S.op(eng, lambda e: e.scalar_tensor_tensor(out=out, in0=in0, scalar=scalar, in1=in1, op0=op0, op1=op1), r, w)

    def CP(self, out, in_, r, w, eng="dve"):
        if eng == "act":
            self.S.op(eng, lambda e: e.activation(out=out, in_=in_, func=AF.Copy), r, w)
        else:
            self.S.op(eng, lambda e: e.tensor_copy(out=out, in_=in_), r, w)

    def RED(self, out, in_, op, r, w):
        self.S.op("dve", lambda e: e.tensor_reduce(out=out, in_=in_, axis=AX.X, op=op), r, w)

    def RCP(self, out, in_, r, w):
        self.S.op("dve", lambda e: e.reciprocal(out=out, in_=in_), r, w)

    def MSET(self, ap, v, w, eng="dve"):
        self.S.op(eng, lambda e: e.memset(ap, v), (), w)

    def DMAS(self, semkey, pairs, r, w, queue="sp"):
        self.S.dma(queue, semkey, [(o, a, {}) for o, a in pairs], r, w)

    def DMA(self, semkey, out, in_, r, w, queue="sp", **kw):
        self.S.dma(queue, semkey, [(out, in_, kw)], r, w)

    def build(self):
        nc, S = self.nc, self.S
        B = self
        xin = B.inp("xin", [T, D])
        consts = B.inp("consts", [128, NCONST])
        vecA = B.inp("vecA", [2, 72, 128])
        vecB = B.inp("vecB", [63, 128])
        bc0 = B.inp("bc0", [16])
        bc1 = B.inp("bc1", [32])
        mlnw = B.inp("mlnw", [128])
        knf = B.inp("knf", [1])
        maskb = B.inp("maskb", [128, 18 * 8])
        ropec = B.inp("ropec", [128, T])
        ropes = B.inp("ropes", [128, T])
        ck = B.inp("ck", [256, 256])
        cv = B.inp("cv", [256, 256])
        sd0 = B.inp("sd0", [2, 4, 128, 128])
        mC0 = B.inp("mC0", [2, 8, 64, 128])
        mn0 = B.inp("mn0", [2, 8, 64])
        mm0 = B.inp("mm0", [2, 8])
        ada_w = B.inp("ada_w", [2, D, 6 * D])
        ffn_g = B.inp("ffn_w_gate", [2, D, FF])
        ffn_u = B.inp("ffn_w_up", [2, D, FF])
        ffn_d = B.inp("ffn_w_down", [2, FF, D])
        ab_in = B.inp("ab_w_in", [1, D, AB_IN])
        ab_out = B.inp("ab_w_out", [1, D, D])
        ml_in = B.inp("ml_w_in", [1, D, ML_IN])
        ml_out = B.inp("ml_w_out", [1, D, D])

        y = B.outp("y", [T, D])
        ok = B.outp("ok", [T, 256])
        ov = B.outp("ov", [T, 256])
        od = B.outp("od", [8, 2, 4, 128, 128])
        oC = B.outp("oC", [8, 2, 8, 64, 128])
        on = B.outp("on", [8, 2, 8, 64])
        om = B.outp("om", [8, 2, 8])

        xT_d = B.scratch("xT_d", [8, 128, T], F32)

        ps = [nc.alloc_psum_tensor(f"ps{i}", [128, 512], F32).ap() for i in range(8)]
        pk = [f"ps{i}" for i in range(8)]

        cst = B.sb("cst", [128, NCONST])
        cstb = B.sb("cstb", [128, NCONST], BF16)
        epsc = B.sb("epsc", [128, 1])
        knc = B.sb("knc", [128, 1])
        hT = B.sb("hT", [128, 8, T], BF16)
        vA = B.sb("vA", [128, 2, 72])
        vB = B.sb("vB", [128, 63])
        modv = B.sb("modv", [128, 2, 48])
        AA = B.sb("AA", [128, 2, 2, 8])
        NFR = 64
        fsa = B.sb("fence_a", [128, 2 + NFR])
        fsv = B.sb("fence_v", [128, 2 + NFR])
        fcnt = {"act": 0, "dve": 0}

        def fence_act(e):
            fcnt["act"] += 1
            c = 2 + fcnt["act"] % NFR
            return e.activation(out=fsa[:, c:c + 1], in_=fsa[:, 0:1], func=AF.Copy)

        def fence_dve(e):
            fcnt["dve"] += 1
            c = 2 + fcnt["dve"] % NFR
            return e.tensor_copy(out=fsv[:, c:c + 1], in_=fsv[:, 0:1])
        S.fence["act"] = fence_act
        S.fence["dve"] = fence_dve
        arena = B.sb("arena", [128, 42500])
        self._aoff = 0

        def carve(shape, dt=F32):
            n = int(np.prod(shape[1:]))
            words = n if dt == F32 else (n + 1) // 2
            words = (words + 7) // 8 * 8
            v = arena[0:shape[0], self._aoff:self._aoff + words]
            self._aoff += words
            assert self._aoff <= 42500, self._aoff
            if dt != F32:
                v = v.bitcast(BF16)[:, 0:n]
            else:
                v = v[:, 0:n]
            if len(shape) == 2:
                return v
            names = " ".join(f"a{i}" for i in range(len(shape) - 1))
            kw = {f"a{i}": shape[i + 1] for i in range(len(shape) - 1)}
            return v.rearrange(f"p ({names}) -> p {names}", **kw)

        def new_phase():
            S.barrier()
            self._aoff = 0

        I32 = cst[:, C_I:C_I + 128]
        ONE32 = cst[:, C_ONE:C_ONE + 128]
        Ib = cstb[:, C_I:C_I + 128]
        ONEb = cstb[:, C_ONE:C_ONE + 128]

        def bc4(ap2d):
            return ap2d.unsqueeze(1).to_broadcast([128, 4, 128])

        def bcl(ap2d, n=128):
            return ap2d.unsqueeze(2).to_broadcast([ap2d.shape[0], ap2d.shape[1], n])

        def v3(ap, a=4):
            return ap.rearrange("p (a b) -> p a b", a=a)

        B.DMA("d_c", cst, consts, (), ["cst"])
        B.DMA("d_cb", cstb, consts, (), ["cstb"], queue="pool")
        S.op("dve", lambda e: e.memset(fsv, 0.0), (), ["fsv"], nofence=True)
        S.op("dve", lambda e: e.tensor_copy(out=fsv[:, 1:2], in_=fsv[:, 0:1]), ["fsv"], ["fsv1"], nofence=True)
        S.op("dve", lambda e: e.memset(epsc, EPS), (), ["epsc"], nofence=True)
        S.op("act", lambda e: e.activation(out=fsa, in_=epsc[:, 0:1].to_broadcast([128, 2 + NFR]), func=AF.Copy),
             ["epsc"], ["fsa"], nofence=True)
        S.op("act", lambda e: e.activation(out=fsa[:, 1:2], in_=fsa[:, 0:1], func=AF.Copy), ["fsa"], ["fsa1"], nofence=True)
        B.DMA("d_c", knc, knf.partition_broadcast(128), (), ["knc"])
        vst = carve([128, 128])
        for l in range(2):
            B.DMA("d_v", vst[0:72, :], vecA[l], (), ["vst"])
            B.TR(ps[0][:, 0:72], vst[0:72, :], cst[0:72, C_I:C_I + 72], ["vst", "cst"], [pk[0]])
            B.CP(vA[:, l, :], ps[0][:, 0:72], [pk[0]], ["vA"])
        B.DMA("d_v", vst[0:63, :], vecB, (), ["vst"])
        B.TR(ps[0][:, 0:63], vst[0:63, :], cst[0:63, C_I:C_I + 63], ["vst", "cst"], [pk[0]])
        B.CP(vB, ps[0][:, 0:63], [pk[0]], ["vB"])

        xs = [carve([128, D]) for _ in range(2)]
        xo = [carve([128, 8, 128]) for _ in range(2)]
        for tt in range(NCH):
            s = tt % 2
            B.DMA(f"d_xs{s}", xs[s], xin[tt * 128:(tt + 1) * 128, :], (), [f"xs{s}"])
            for c in range(8):
                b = c // 4
                B.TR(ps[b][:, (c % 4) * 128:(c % 4 + 1) * 128], xs[s][:, c * 128:(c + 1) * 128], I32,
                     [f"xs{s}", "cst"], [pk[b]])
            for b in range(2):
                B.CP(xo[s][:, b * 4:(b + 1) * 4, :], v3(ps[b]), [pk[b]], [f"xo{s}"], eng=("dve" if b == 0 else "act"))
            B.DMA(f"d_xo{s}", xT_d[:, :, tt * 128:(tt + 1) * 128].rearrange("c p t -> p c t"), xo[s],
                  [f"xo{s}"], [("xT", tt // 4)])

        def ada_gen(l, nslots, psb):
            scb = carve([128, 8], BF16)
            sc32 = carve([128, 8])
            B.ACT(sc32, vA[:, l, 64:72], AF.Silu, ["vA"], [f"sc32{l}"])
            B.CP(scb, sc32, [f"sc32{l}"], [f"scb{l}"])
            wa = [carve([128, 8, 512], BF16) for _ in range(nslots)]
            for g in range(12):
                s = g % nslots
                B.DMA(f"d_wa{l}{s}", wa[s], ada_w[l][:, g * 512:(g + 1) * 512].rearrange("(c p) n -> p c n", p=128),
                      (), [f"wa{l}{s}"], queue="pool")
                for j in range(4):
                    col = g * 4 + j
                    for kc in range(8):
                        B.MM(ps[psb][:, col:col + 1], wa[s][:, kc, j * 128:(j + 1) * 128], scb[:, kc:kc + 1],
                             [f"wa{l}{s}", f"scb{l}"], [pk[psb]], start=(kc == 0), stop=(kc == 7))
                yield
            B.TT(modv[:, l, :], ps[psb][:, 0:48], vA[:, l, 0:48], ALU.add, [pk[psb], "vA"], ["modv"])
            for i in range(2):
                sc = modv[:, l, (1 + 3 * i) * 8:(2 + 3 * i) * 8]
                B.STT(AA[:, l, i, :], sc, 1.0, vA[:, l, 48 + 8 * i:56 + 8 * i], ALU.add, ALU.mult, ["modv", "vA"], ["AA"])

        def ada_phase(l):
            for _ in ada_gen(l, 4, 2):
                pass

        def mod(l, j):
            return modv[:, l, j * 8:(j + 1) * 8]

        def norm_phase(l, i):
            new_phase()
            xt = [carve([128, 8, 512]) for _ in range(2)]
            sq = [carve([128, 512]) for _ in range(2)]
            rs = carve([128, 512])
            tmp = [carve([128, 512]) for _ in range(2)]
            for tb in range(4):
                s = tb % 2
                B.DMA(f"d_xt{s}", xt[s], xT_d[:, :, tb * 512:(tb + 1) * 512].rearrange("c p t -> p c t"),
                      [("xT", tb)], [f"xt{s}"])
                for c in range(8):
                    q = c % 2
                    B.ACT(sq[q], xt[s][:, c, :], AF.Square, [f"xt{s}"], [f"sq{q}"])
                    B.MM(ps[3], ONE32, sq[q], ["cst", f"sq{q}"], [pk[3]], start=(c == 0), stop=(c == 7))
                B.ACT(rs, ps[3], AF.Sqrt, [pk[3], "epsc"], ["rs"], bias=epsc[:, 0:1], scale=1.0 / D)
                B.RCP(rs, rs, ["rs"], ["rs"])
                for c in range(8):
                    q = c % 2
                    B.STT(tmp[q], xt[s][:, c, :], AA[:, l, i, c:c + 1], rs, ALU.mult, ALU.mult,
                          [f"xt{s}", "AA", "rs"], [f"tmp{q}"])
                    B.ACT(hT[:, c, tb * 512:(tb + 1) * 512], tmp[q], AF.Identity, [f"tmp{q}", "modv"], ["hT"],
                          bias=mod(l, 3 * i)[:, c:c + 1])

        def res_proj(w_ap, KC, rhs, rkey, gate, wbuf, final, tok0, ntb):
            xr = [carve([128, 512]) for _ in range(2)]
            yo = [carve([128, 4, 128]) for _ in range(2)] if final else None
            cnt = 0
            for dt in range(8):
                for tb in range(ntb):
                    s = cnt % 2
                    cnt += 1
                    gtb = tok0 // 512 + tb
                    B.DMA(f"d_xr{s}", xr[s], xT_d[dt, :, gtb * 512:(gtb + 1) * 512], [("xT", dt, gtb)], [f"xr{s}"])
                    pb = 4 + s
                    for kc in range(KC):
                        B.MM(ps[pb], wbuf[:, kc, dt * 128:(dt + 1) * 128], rhs[:, kc, tb * 512:(tb + 1) * 512],
                             ["wbuf", rkey], [pk[pb]], start=(kc == 0), stop=(kc == KC - 1))
                    B.STT(xr[s], ps[pb], gate[:, dt:dt + 1], xr[s], ALU.mult, ALU.add, [pk[pb], "modv", f"xr{s}"], [f"xr{s}"])
                    if not final:
                        B.DMA(f"d_xw{s}", xT_d[dt, :, gtb * 512:(gtb + 1) * 512], xr[s], [f"xr{s}"], [("xT", dt, gtb)])
                    else:
                        pt = 6 + s
                        for k in range(4):
                            B.TR(ps[pt][:, k * 128:(k + 1) * 128], xr[s][:, k * 128:(k + 1) * 128], I32,
                                 [f"xr{s}", "cst"], [pk[pt]])
                        B.CP(yo[s], v3(ps[pt]), [pk[pt]], [f"yo{s}"], eng="act")
                        B.DMA(f"d_yo{s}", y.rearrange("(n p) f -> p n f", p=128)[:, gtb * 4:(gtb + 1) * 4, dt * 128:(dt + 1) * 128],
                              yo[s], [f"yo{s}"], [])

        def res_proj_tb(KC, rhs, rkey, gate, wbuf, norm):
            l, i = norm
            xr = [carve([128, 8, 512]) for _ in range(2)]
            sq = [carve([128, 512]) for _ in range(2)]
            rs = carve([128, 512])
            tmp = [carve([128, 512]) for _ in range(2)]
            cnt = 0
            for tb in range(4):
                s = tb % 2
                tsl = slice(tb * 512, (tb + 1) * 512)
                B.DMA(f"d_xr{s}", xr[s], xT_d[:, :, tsl].rearrange("c p t -> p c t"), (), [f"xr{s}"])
                for dt in range(8):
                    pb = 4 + cnt % 2
                    cnt += 1
                    for kc in range(KC):
                        B.MM(ps[pb], wbuf[:, kc, dt * 128:(dt + 1) * 128], rhs[:, kc, tsl],
                             ["wbuf", rkey], [pk[pb]], start=(kc == 0), stop=(kc == KC - 1))
                    B.STT(xr[s][:, dt, :], ps[pb], gate[:, dt:dt + 1], xr[s][:, dt, :], ALU.mult, ALU.add,
                          [pk[pb], "modv", f"xr{s}"], [f"xr{s}"])
                B.DMA(f"d_xw{s}", xT_d[:, :, tsl].rearrange("c p t -> p c t"), xr[s], [f"xr{s}"], [])
                for c in range(8):
                    q = c % 2
                    B.ACT(sq[q], xr[s][:, c, :], AF.Square, [f"xr{s}"], [f"sq{q}"])
                    B.MM(ps[3], ONE32, sq[q], ["cst", f"sq{q}"], [pk[3]], start=(c == 0), stop=(c == 7))
                B.ACT(rs, ps[3], AF.Sqrt, [pk[3], "epsc"], ["rs"], bias=epsc[:, 0:1], scale=1.0 / D)
                B.RCP(rs, rs, ["rs"], ["rs"])
                for c in range(8):
                    q = c % 2
                    B.STT(tmp[q], xr[s][:, c, :], AA[:, l, i, c:c + 1], rs, ALU.mult, ALU.mult,
                          [f"xr{s}", "AA", "rs"], [f"tmp{q}"])
                    B.ACT(hT[:, c, tsl], tmp[q], AF.Identity, [f"tmp{q}", "modv"], ["hT"],
                          bias=mod(l, 3 * i)[:, c:c + 1])

        def ffn_phase(l):
            new_phase()
            aT = carve([128, NFT, T], BF16)
            wd = carve([128, NFT, 1024], BF16)
            wg = [carve([128, 8, 256], BF16) for _ in range(2)]
            wu = [carve([128, 8, 256], BF16) for _ in range(2)]
            sg = [carve([128, 512]) for _ in range(2)]
            cnt = 0
            for g in range(11):
                s = g % 2
                B.DMA(f"d_wg{s}", wg[s], ffn_g[l][:, g * 256:(g + 1) * 256].rearrange("(c p) n -> p c n", p=128),
                      (), [f"wg{s}"], queue="pool")
                B.DMA(f"d_wu{s}", wu[s], ffn_u[l][:, g * 256:(g + 1) * 256].rearrange("(c p) n -> p c n", p=128),
                      (), [f"wu{s}"], queue="pool")
                if g == 1:
                    B.DMAS("d_wd", [(wd[:, f, :], ffn_d[l][f * 128:(f + 1) * 128, :]) for f in range(NFT)], (), ["wbuf"], queue="pool")
                for j in range(2):
                    f = g * 2 + j
                    for tb in range(4):
                        q = cnt % 2
                        cnt += 1
                        tsl = slice(tb * 512, (tb + 1) * 512)
                        for kc in range(8):
                            B.MM(ps[q], wg[s][:, kc, j * 128:(j + 1) * 128], hT[:, kc, tsl],
                                 [f"wg{s}", "hT"], [pk[q]], start=(kc == 0), stop=(kc == 7))
                        for kc in range(8):
                            B.MM(ps[2 + q], wu[s][:, kc, j * 128:(j + 1) * 128], hT[:, kc, tsl],
                                 [f"wu{s}", "hT"], [pk[2 + q]], start=(kc == 0), stop=(kc == 7))
                        B.ACT(sg[q], ps[q], AF.Silu, [pk[q]], [f"sg{q}"])
                        B.TT(aT[:, f, tsl], sg[q], ps[2 + q], ALU.mult, [f"sg{q}", pk[2 + q]], ["aT"])
            res_proj(None, NFT, aT, "aT", mod(l, 5), wd, final=(l == 1), tok0=0, ntb=4)

        def proj_fm(w_ap, c0, M, wslots, idx, consume):
            s = idx % 2
            wt = wslots[s]
            B.DMA(f"d_wt{s}", wt[:, :, 0:M], w_ap[:, c0:c0 + M].rearrange("(c p) n -> p c n", p=128), (), [f"wt{s}"], queue="pool")
            for tb in range(4):
                pb = (idx * 4 + tb) % 2
                for kc in range(8):
                    B.MM(ps[pb][0:M, :], wt[:, kc, 0:M], hT[:, kc, tb * 512:(tb + 1) * 512], [f"wt{s}", "hT"], [pk[pb]],
                         start=(kc == 0), stop=(kc == 7))
                consume(ps[pb], pk[pb], tb)

        def proj_tm(w_ap, c0, N, wbuf, wkey, consume, pbase=2):
            B.DMA("d_" + wkey, wbuf[:, :, 0:N], w_ap[:, c0:c0 + N].rearrange("(c p) n -> p c n", p=128), (), [wkey], queue="pool")
            for tt in range(NCH):
                pb = pbase + tt % 2
                for kc in range(8):
                    B.MM(ps[pb][:, 0:N], hT[:, kc, tt * 128:(tt + 1) * 128], wbuf[:, kc, 0:N], ["hT", wkey], [pk[pb]],
                         start=(kc == 0), stop=(kc == 7))
                consume(ps[pb], pk[pb], tt)

        def mixer_ab():
            new_phase()
            W = ab_in[0]
            qT_d = B.scratch("qT_d", [4, 128, T], BF16)
            kT_d = B.scratch("kT_d", [4, 128, T], BF16)
            zs_d = B.scratch("zs_d", [4, 128, T], BF16)
            ktok_d = B.scratch("ktok_d", [NCH, 128, 4, 128], BF16)
            vtok_d = B.scratch("vtok_d", [NCH, 128, 4, 128], BF16)
            oTf_d = B.scratch("oTf_d", [NCH, 128, 4, 128], F32)
            mixT = carve([128, 8, T], BF16)
            wsl = [carve([128, 8, 128], BF16) for _ in range(2)]
            off_persist = self._aoff

            wab = carve([128, 8, 16], BF16)
            ab = carve([128, NCH, 16])
            b0 = carve([128, 16])
            gg = carve([128, 2, NCH, 4])
            nbeta = carve([128, 2, NCH, 4])
            beta = carve([128, 2, NCH, 4])
            t8 = carve([128, NCH, 8])
            nA = carve([128, 8])
            gc = carve([128, 2, NCH, 4])
            gl = carve([128, 2, NCH, 4])
            EG = carve([128, 2, NCH, 4])
            EKD = carve([128, 2, NCH, 4])
            EGL = carve([128, 2, NCH, 4])
            off_rec = self._aoff
            ci2 = [carve([128, 8, 260]) for _ in range(3)]
            acc2 = [carve([128, 8, 256]) for _ in range(3)]
            sa2 = [carve([128, T]) for _ in range(3)]
            sqb = [carve([128, 512]) for _ in range(2)]
            rsb2 = [carve([128, 512]) for _ in range(3)]
            nb2 = [carve([128, T], BF16) for _ in range(3)]
            tok2 = [carve([128, NCH, 128], BF16) for _ in range(3)]
            for par in range(3):
                B.MSET(ci2[par], 0.0, [f"ci{par}"])
            idx = 0
            ag = ada_gen(1, 2, 6)
            for grp in range(3):
                for h in range(4):
                    next(ag, None)
                    ct = grp * 4 + h
                    par = idx % 3
                    ci, acc, sa, nb_, tok, rsb = ci2[par], acc2[par], sa2[par], nb2[par], tok2[par], rsb2[par]
                    kci, kacc, ksa, knb, ktk, krs = f"ci{par}", f"acc{par}", f"sa{par}", f"nb{par}", f"tok{par}", f"rsb{par}"
                    accf = acc.rearrange("p s t -> p (s t)")

                    def cons(p, pkk, tb, ci=ci, kci=kci):
                        B.CP(ci[:, 2 * tb:2 * tb + 2, 2:258], p.rearrange("p (s t) -> p s t", s=2), [pkk], [kci], eng="act")
                    proj_fm(W, ct * 128, 128, wsl, idx, cons)
                    idx += 1
                    B.TS(ci[:, 1:8, 0:2], ci[:, 0:7, 256:258], knc[:, 0:1], ALU.mult, [kci, "knc"], [kci])
                    B.TS(ci[:, 0:7, 258:260], ci[:, 1:8, 2:4], knc[:, 0:1], ALU.mult, [kci, "knc"], [kci])
                    B.TS(acc, ci[:, :, 0:256], vB[:, ct:ct + 1], ALU.mult, [kci, "vB"], [kacc])
                    for j in range(1, 5):
                        B.STT(acc, ci[:, :, j:j + 256], vB[:, j * 12 + ct:j * 12 + ct + 1], acc, ALU.mult, ALU.add,
                              [kci, "vB", kacc], [kacc])
                    B.ACT(sa, accf, AF.Silu, [kacc], [ksa])
                    if grp < 2:
                        for tb in range(4):
                            q = tb % 2
                            B.ACT(sqb[q], sa[:, tb * 512:(tb + 1) * 512], AF.Square, [ksa], [f"sqb{q}"])
                            B.MM(ps[2], ONE32, sqb[q], ["cst", f"sqb{q}"], [pk[2]])
                            B.ACT(rsb, ps[2], AF.Sqrt, [pk[2], "epsc"], [krs], bias=epsc[:, 0:1])
                            B.RCP(rsb, rsb, [krs], [krs])
                            B.TT(nb_[:, tb * 512:(tb + 1) * 512], sa[:, tb * 512:(tb + 1) * 512], rsb, ALU.mult, [ksa, krs], [knb])
                        B.DMA(f"d_qk{par}", (qT_d if grp == 0 else kT_d)[h], nb_, [knb], [("qT" if grp == 0 else "kT", h)])
                    else:
                        B.CP(nb_, sa, [ksa], [knb])
                    if grp >= 1:
                        for g4 in range(4):
                            pbt = ps[3].bitcast(BF16)
                            for k in range(4):
                                n = g4 * 4 + k
                                B.TR(pbt[:, k * 128:(k + 1) * 128], nb_[:, n * 128:(n + 1) * 128], Ib, [knb, "cstb"], [pk[3]])
                            B.CP(tok[:, g4 * 4:(g4 + 1) * 4, :], v3(pbt[:, 0:512]), [pk[3]], [ktk], eng="act")
                        dst = ktok_d if grp == 1 else vtok_d
                        B.DMA(f"d_tok{par}", dst[:, :, h, :].rearrange("n t d -> t n d"), tok, [ktk], [("ktok" if grp == 1 else "vtok", h)])
            for h in range(4):
                par = idx % 3
                nb_, knb = nb2[par], f"nb{par}"

                def consz(p, pkk, tb, nb_=nb_, knb=knb):
                    B.ACT(nb_[:, tb * 512:(tb + 1) * 512], p, AF.Silu, [pkk], [knb])
                proj_fm(W, 1536 + h * 128, 128, wsl, idx, consz)
                idx += 1
                B.DMA(f"d_qk{par}", zs_d[h], nb_, [knb], [("zs", h)])
            for _ in ag:
                pass
            B.DMA("d_b0", b0, bc0.partition_broadcast(128), (), ["b0"])

            def consab(p, pkk, tt):
                B.CP(ab[:, tt, :], p[:, 0:16], [pkk], ["ab"])
            proj_tm(W, 2048, 16, wab, "wab", consab)
            B.ACT(nA, b0[:, 0:8], AF.Exp, ["b0"], ["nA"])
            B.TS(nA, nA, -1.0, ALU.mult, ["nA"], ["nA"])
            B.TT(t8, ab[:, :, 0:8], b0[:, 8:16].unsqueeze(1).to_broadcast([128, NCH, 8]), ALU.add, ["ab", "b0"], ["t8"])
            B.ACT(t8, t8, AF.Exp, ["t8"], ["t8"])
            B.ACT(t8, t8, AF.Ln, ["t8", "cst"], ["t8"], bias=ONE32[:, 0:1])
            B.TT(t8, t8, nA.unsqueeze(1).to_broadcast([128, NCH, 8]), ALU.mult, ["t8", "nA"], ["t8"])
            for dr in range(2):
                B.CP(gg[:, dr], t8[:, :, dr * 4:(dr + 1) * 4], ["t8"], ["gg"])
            B.ACT(t8, ab[:, :, 8:16], AF.Sigmoid, ["ab"], ["t8"])
            for dr in range(2):
                B.CP(beta[:, dr], t8[:, :, dr * 4:(dr + 1) * 4], ["t8"], ["beta"])
                B.TS(nbeta[:, dr], t8[:, :, dr * 4:(dr + 1) * 4], -1.0, ALU.mult, ["t8"], ["nbeta"])
            for dr in range(2):
                tri = cst[:, C_U:C_U + 128] if dr == 0 else cst[:, C_LO:C_LO + 128]
                B.MM(ps[2][:, 0:64], tri, gg[:, dr].rearrange("p n h -> p (n h)"), ["cst", "gg"], [pk[2]])
                B.CP(gc[:, dr].rearrange("p n h -> p (n h)"), ps[2][:, 0:64], [pk[2]], ["gc"])
                B.MM(ps[2][:, 0:64], ONE32, gg[:, dr].rearrange("p n h -> p (n h)"), ["cst", "gg"], [pk[2]])
                B.CP(gl[:, dr].rearrange("p n h -> p (n h)"), ps[2][:, 0:64], [pk[2]], ["gl"])
            f2 = lambda a: a.rearrange("p d n h -> p (d n h)")
            B.ACT(f2(EG), f2(gc), AF.Exp, ["gc"], ["EG"])
            B.ACT(f2(EGL), f2(gl), AF.Exp, ["gl"], ["EGL"])
            B.TT(f2(EKD), f2(gl), f2(gc), ALU.subtract, ["gl", "gc"], ["EKD"])
            B.ACT(f2(EKD), f2(EKD), AF.Exp, ["EKD"], ["EKD"])

            if self.stop_after == "dn_pre":
                return
            S.barrier()
            self._aoff = off_rec
            oTb_d = B.scratch("oTb_d", [NCH, 128, 4, 128], F32)
            f3 = lambda a: a.rearrange("p h t -> p (h t)")
            scale = 128.0 ** -0.5
            Sst = [carve([128, 4, 128]) for _ in range(2)]
            Sb = [carve([128, 4, 128], BF16) for _ in range(2)]
            for dr in range(2):
                B.DMA("d_s0", Sst[dr], sd0[dr].rearrange("h d v -> d h v"), (), [f"S{dr}"])
                B.CP(Sb[dr], Sst[dr], [f"S{dr}"], [f"Sb{dr}"])

            import os as _os2
            R32 = (lambda a: a.bitcast(mybir.dt.float32r)) if _os2.environ.get("DN_F32R", "0") == "1" else (lambda a: a)

            off_comb = self._aoff

            POOLE = "pool" if _os2.environ.get("USE_POOL", "1") == "1" else "dve"

            def dn_stream(dr):
                X = f"x{dr}"
                pb = [ps[4 * dr + i] for i in range(4)]
                pkb = [pk[4 * dr + i] for i in range(4)]
                A_, B_, C_, D_ = 0, 1, 2, 3
                qc = [carve([128, 4, 128], BF16) for _ in range(2)]
                kc_ = [carve([128, 4, 128], BF16) for _ in range(2)]
                ktc = [carve([128, 4, 128], BF16) for _ in range(2)]
                vtc = [carve([128, 4, 128], BF16) for _ in range(2)]
                wA = carve([128, 4, 128])
                wB = carve([128, 4, 128])
                EGr = carve([128, 4, 128])
                u_ = carve([128, 4, 128])
                ot = carve([128, 4, 128])
                QKDT = carve([128, 4, 128], BF16)
                Pf = [carve([128, 4, 128]) for _ in range(2)]
                P = [R32(a) for a in Pf]
                PT = [R32(carve([128, 4, 128])) for _ in range(2)]
                RT = [R32(carve([128, 4, 128])) for _ in range(2)]
                Pb = [carve([128, 4, 128], BF16) for _ in range(2)]
                PTb = [carve([128, 4, 128], BF16) for _ in range(2)]
                RTb = [carve([128, 4, 128], BF16) for _ in range(2)]
                MT = carve([128, 4, 128], BF16)
                kg = carve([128, 4, 128], BF16)
                kdec = carve([128, 4, 128], BF16)
                wT = carve([128, 4, 128], BF16)
                vnew = carve([128, 4, 128], BF16)
                qd = carve([128, 4, 128], BF16)
                order = range(NCH) if dr == 0 else range(NCH - 1, -1, -1)
                NEGs = cst[:, C_NLS:C_NLS + 128] if dr == 0 else cst[:, C_NUS:C_NUS + 128]
                NEGt = cst[:, C_NUI:C_NUI + 128] if dr == 0 else cst[:, C_NLI:C_NLI + 128]
                K = lambda nm: X + nm
                for vi, n in enumerate(order):
                    s = vi % 2
                    tsl = slice(n * 128, (n + 1) * 128)
                    B.DMA(f"d_qc{s}{X}", qc[s], qT_d[:, :, tsl].rearrange("h d t -> d h t"), [("qT", h) for h in range(4)], [K(f"qc{s}")])
                    B.DMA(f"d_kc{s}{X}", kc_[s], kT_d[:, :, tsl].rearrange("h d t -> d h t"), [("kT", h) for h in range(4)], [K(f"kc{s}")])
                    B.DMA(f"d_ktc{s}{X}", ktc[s], ktok_d[n], [("ktok", h) for h in range(4)], [K(f"ktc{s}")])
                    B.DMA(f"d_vtc{s}{X}", vtc[s], vtok_d[n], [("vtok", h) for h in range(4)], [K(f"vtc{s}")])
                    gcn = gc[:, dr, n, :]
                    for h in range(4):
                        B.MM(pb[A_][:, h * 128:(h + 1) * 128], kc_[s][:, h, :], kc_[s][:, h, :], [K(f"kc{s}")], [pkb[A_]])
                    for h in range(4):
                        B.MM(pb[B_][:, h * 128:(h + 1) * 128], kc_[s][:, h, :], qc[s][:, h, :], [K(f"kc{s}"), K(f"qc{s}")], [pkb[B_]])
                    B.TT(wA, bc4(I32), bcl(gcn), ALU.mult, ["cst", "gc"], [K("wA")], eng=POOLE)
                    yield
                    B.MM(pb[C_], ONE32, f3(wA), ["cst", K("wA")], [pkb[C_]])
                    yield
                    B.TT(wA, bc4(NEGs), v3(pb[C_]), ALU.subtract, ["cst", pkb[C_]], [K("wA")])
                    B.TT(wA, wA, bcl(gcn), ALU.add, [K("wA"), "gc"], [K("wA")])
                    B.TT(wB, v3(pb[C_]), bc4(NEGt), ALU.add, ["cst", pkb[C_]], [K("wB")])
                    B.TT(wB, wB, bcl(gcn), ALU.subtract, [K("wB"), "gc"], [K("wB")])
                    yield
                    B.ACT(f3(wA), f3(wA), AF.Exp, [K("wA")], [K("wA")])
                    B.ACT(f3(wB), f3(wB), AF.Exp, [K("wB")], [K("wB")])
                    B.ACT(f3(EGr), pb[C_], AF.Exp, [pkb[C_], K("wA"), K("wB")], [K("EGr")])
                    yield
                    B.TT(wA, v3(pb[A_]), wA, ALU.mult, [pkb[A_], K("wA")], [K("wA")])
                    B.TT(P[0], wA, bcl(nbeta[:, dr, n, :]), ALU.mult, [K("wA"), "nbeta"], [K("P0")], eng=POOLE)
                    B.STT(QKDT, v3(pb[B_]), scale, wB, ALU.mult, ALU.mult, [pkb[B_], K("wB")], [K("QKDT")])
                    yield
                    for h in range(4):
                        B.TR(pb[D_][:, h * 128:(h + 1) * 128], Pf[0][:, h, :], I32, [K("P0"), "cst"], [pkb[D_]])
                    yield
                    B.CP(PT[0], v3(pb[D_]), [pkb[D_]], [K("PT0")], eng="act")
                    yield
                    B.TT(RT[0], PT[0], bc4(I32), ALU.add, [K("PT0"), "cst"], [K("RT0")], eng=POOLE)
                    NFP = 6
                    cur = 0
                    cb = 0
                    for it in range(6):
                        fp = it < NFP
                        nx = 1 - cur
                        nb2_ = 1 - cb
                        if fp:
                            Pin, PTin, RTin = P[cur], PT[cur], RT[cur]
                            kP, kPT, kRT = K(f"P{cur}"), K(f"PT{cur}"), K(f"RT{cur}")
                        else:
                            Pin, PTin, RTin = Pb[cb], PTb[cb], RTb[cb]
                            kP, kPT, kRT = K(f"Pb{cb}"), K(f"PTb{cb}"), K(f"RTb{cb}")
                        for h in range(4):
                            B.MM(pb[A_][:, h * 128:(h + 1) * 128], PTin[:, h, :], Pin[:, h, :], [kPT, kP], [pkb[A_]])
                        if it < 5:
                            for h in range(4):
                                B.MM(pb[B_][:, h * 128:(h + 1) * 128], Pin[:, h, :], PTin[:, h, :], [kPT, kP], [pkb[B_]])
                        yield
                        if it < NFP - 1:
                            Pl, kPl = P[nx], K(f"P{nx}")
                            B.CP(P[nx], v3(pb[A_]), [pkb[A_]], [K(f"P{nx}")], eng="act")
                            B.CP(PT[nx], v3(pb[B_]), [pkb[B_]], [K(f"PT{nx}")], eng="dve")
                        elif it == NFP - 1:
                            Pl, kPl = P[nx], K(f"P{nx}")
                            B.CP(P[nx], v3(pb[A_]), [pkb[A_]], [K(f"P{nx}")], eng="act")
                            if it < 5:
                                B.CP(Pb[0], v3(pb[A_]), [pkb[A_]], [K("Pb0")], eng="act")
                                B.CP(PTb[0], v3(pb[B_]), [pkb[B_]], [K("PTb0")], eng="dve")
                        else:
                            Pl, kPl = Pb[nb2_], K(f"Pb{nb2_}")
                            B.CP(Pb[nb2_], v3(pb[A_]), [pkb[A_]], [K(f"Pb{nb2_}")], eng="act")
                            if it < 5:
                                B.CP(PTb[nb2_], v3(pb[B_]), [pkb[B_]], [K(f"PTb{nb2_}")], eng="dve")
                        yield
                        for h in range(4):
                            B.MM(pb[C_][:, h * 128:(h + 1) * 128], Pl[:, h, :], RTin[:, h, :], [kPl, kRT], [pkb[C_]])
                        yield
                        if it < NFP - 1:
                            B.TT(RT[nx], RTin, v3(pb[C_]), ALU.add, [kRT, pkb[C_]], [K(f"RT{nx}")])
                            cur = nx
                        elif it == NFP - 1:
                            B.TT(RTb[0], RTin, v3(pb[C_]), ALU.add, [kRT, pkb[C_]], [K("RTb0")])
                            cb = 0
                        else:
                            B.TT(RTb[nb2_], RTin, v3(pb[C_]), ALU.add, [kRT, pkb[C_]], [K(f"RTb{nb2_}")])
                            cb = nb2_
                    RTfin, kRTfin = RTb[cb], K(f"RTb{cb}")
                    B.TT(MT, RTfin, bcl(beta[:, dr, n, :]), ALU.mult, [kRTfin, "beta"], [K("MT")], eng=POOLE)
                    B.TT(kg, ktc[s], bcl(EG[:, dr, n, :]), ALU.mult, [K(f"ktc{s}"), "EG"], [K("kg")], eng=POOLE)
                    B.TT(kdec, ktc[s], bcl(EKD[:, dr, n, :]), ALU.mult, [K(f"ktc{s}"), "EKD"], [K("kdec")], eng=POOLE)
                    B.STT(qd, qc[s], scale, EGr, ALU.mult, ALU.mult, [K(f"qc{s}"), K("EGr")], [K("qd")])
                    yield
                    for h in range(4):
                        B.MM(pb[A_][:, h * 128:(h + 1) * 128], MT[:, h, :], vtc[s][:, h, :], [K("MT"), K(f"vtc{s}")], [pkb[A_]])
                    for h in range(4):
                        B.MM(pb[B_][:, h * 128:(h + 1) * 128], kg[:, h, :], MT[:, h, :], [K("MT"), K("kg")], [pkb[B_]])
                    yield
                    B.CP(u_, v3(pb[A_]), [pkb[A_]], [K("u")], eng="act")
                    B.CP(wT, v3(pb[B_]), [pkb[B_]], [K("wT")], eng="act")
                    yield
                    for h in range(4):
                        B.MM(pb[C_][:, h * 128:(h + 1) * 128], wT[:, h, :], Sb[dr][:, h, :], [K("wT"), f"Sb{dr}"], [pkb[C_]])
                    yield
                    B.TT(vnew, u_, v3(pb[C_]), ALU.subtract, [K("u"), pkb[C_]], [K("vnew")])
                    yield
                    for h in range(4):
                        B.MM(pb[A_][:, h * 128:(h + 1) * 128], Sb[dr][:, h, :], qd[:, h, :], [K("qd"), f"Sb{dr}"], [pkb[A_]], start=True, stop=False)
                        B.MM(pb[A_][:, h * 128:(h + 1) * 128], vnew[:, h, :], QKDT[:, h, :], [K("vnew"), K("QKDT")], [pkb[A_]], start=False, stop=True)
                    for h in range(4):
                        B.MM(pb[B_][:, h * 128:(h + 1) * 128], kdec[:, h, :], vnew[:, h, :], [K("kdec"), K("vnew")], [pkb[B_]])
                    B.TT(Sst[dr], Sst[dr], bcl(EGL[:, dr, n, :]), ALU.mult, [f"S{dr}", "EGL"], [f"S{dr}"], eng=POOLE)
                    yield
                    B.TT(Sst[dr], Sst[dr], v3(pb[B_]), ALU.add, [f"S{dr}", pkb[B_]], [f"S{dr}"])
                    seg_end = (n % 2 == 1) if dr == 0 else (n % 2 == 0)
                    if seg_end:
                        B.DMA(f"d_od{dr}", od[n // 2, dr].rearrange("h d v -> d h v"), Sst[dr], [f"S{dr}"], [])
                        B.TS(Sst[dr], Sst[dr], knc[:, 0:1], ALU.mult, [f"S{dr}", "knc"], [f"S{dr}"])
                    B.CP(Sb[dr], Sst[dr], [f"S{dr}"], [f"Sb{dr}"], eng="act")
                    B.CP(ot, v3(pb[A_]), [pkb[A_]], [K("ot")], eng="act")
                    B.DMA(f"d_ot{X}", (oTf_d if dr == 0 else oTb_d)[n], ot, [K("ot")], [("oT", dr, n)])
                    yield

            gens = [dn_stream(0), dn_stream(1)]
            alive = [True, True]
            import os
            if os.environ.get("DN_SEQ"):
                ny = int(os.environ.get("DN_YIELDS", "100000"))
                for g in gens:
                    for i, _ in enumerate(g):
                        if i + 1 >= ny:
                            break
                alive = [False, False]
            while any(alive):
                for gi, g in enumerate(gens):
                    if alive[gi]:
                        try:
                            next(g)
                        except StopIteration:
                            alive[gi] = False
            S.barrier()
            self._aoff = off_comb
            ofc = [carve([128, 4, 128]) for _ in range(4)]
            obc = [carve([128, 4, 128]) for _ in range(4)]
            zc = [carve([128, 4, 128], BF16) for _ in range(4)]
            osq = [carve([128, 4, 128]) for _ in range(4)]
            for n in range(NCH):
                s = n % 4
                tsl = slice(n * 128, (n + 1) * 128)
                B.DMA(f"d_ofc{s}", ofc[s], oTf_d[n], [("oT", 0, n)], [f"ofc{s}"])
                B.DMA(f"d_obc{s}", obc[s], oTb_d[n], [("oT", 1, n)], [f"obc{s}"])
                B.DMA(f"d_zc{s}", zc[s], zs_d[:, :, tsl].rearrange("h d t -> d h t"), [("zs", h) for h in range(4)], [f"zc{s}"])
                B.TT(ofc[s], ofc[s], obc[s], ALU.add, [f"ofc{s}", f"obc{s}"], [f"ofc{s}"])
                B.ACT(f3(osq[s]), f3(ofc[s]), AF.Square, [f"ofc{s}"], [f"osq{s}"])
                B.MM(ps[s], ONE32, f3(osq[s]), ["cst", f"osq{s}"], [pk[s]])
                B.ACT(f3(osq[s]), ps[s], AF.Sqrt, [pk[s], "epsc"], [f"osq{s}"], bias=epsc[:, 0:1], scale=1.0 / 128)
                B.RCP(osq[s], osq[s], [f"osq{s}"], [f"osq{s}"])
                B.TT(ofc[s], ofc[s], osq[s], ALU.mult, [f"ofc{s}", f"osq{s}"], [f"ofc{s}"])
                B.STT(mixT[:, 0:4, tsl], ofc[s], vB[:, 60:61], zc[s], ALU.mult, ALU.mult, [f"ofc{s}", "vB", f"zc{s}"], ["mixT"])

            if self.stop_after == "dn":
                return
            S.barrier()
            self._aoff = off_persist
            qr = carve([128, 4, T], BF16)
            kr = carve([128, 2, T + 256], BF16)
            vat = carve([128, 18, 2, 128], BF16)
            mb = carve([128, 18 * 8])
            B.DMA("d_mb", mb, maskb, (), ["mb"])
            cosT = carve([128, T])
            sinT = carve([128, T])
            B.DMA("d_cs", cosT, ropec, (), ["cosT"])
            B.DMA("d_cs", sinT, ropes, (), ["sinT"])
            a32_ = [carve([128, 512]) for _ in range(2)]
            asq_ = [carve([128, 512]) for _ in range(2)]
            ars_ = [carve([128, 512]) for _ in range(2)]
            arot_ = [carve([128, 512]) for _ in range(2)]
            kst_ = [carve([128, 4, 128]) for _ in range(2)]
            idx = 0
            for j in range(6):
                c0 = 2064 + j * 128
                nw = vB[:, 61:62] if j < 4 else vB[:, 62:63]

                def consq(p, pkk, tb, j=j, nw=nw):
                    q = tb % 2
                    a32, asq, ars, arot, kst = a32_[q], asq_[q], ars_[q], arot_[q], kst_[q]
                    ka, ks, kr_, ko, kk = f"a32{q}", f"asq{q}", f"ars{q}", f"arot{q}", f"kst{q}"
                    tsl = slice(tb * 512, (tb + 1) * 512)
                    B.ACT(asq, p, AF.Square, [pkk], [ks])
                    B.MM(ps[2 + q], ONE32, asq, ["cst", ks], [pk[2 + q]])
                    B.ACT(ars, ps[2 + q], AF.Sqrt, [pk[2 + q], "epsc"], [kr_], bias=epsc[:, 0:1], scale=1.0 / 128)
                    B.RCP(ars, ars, [kr_], [kr_])
                    B.STT(a32, p, nw, ars, ALU.mult, ALU.mult, [pkk, "vB", kr_], [ka])
                    if j >= 4:
                        for k in range(4):
                            B.TR(ps[6 + q][:, k * 128:(k + 1) * 128], a32[:, k * 128:(k + 1) * 128], I32, [ka, "cst"], [pk[6 + q]])
                        B.CP(kst, v3(ps[6 + q]), [pk[6 + q]], [kk], eng="act")
                        B.DMA(f"d_kst{q}", ok.rearrange("(n p) f -> p n f", p=128)[:, tb * 4:(tb + 1) * 4, (j - 4) * 128:(j - 3) * 128],
                              kst, [kk], [])
                    B.MM(ps[4 + q], cst[:, C_ROT:C_ROT + 128], a32, ["cst", ka], [pk[4 + q]])
                    B.TT(arot, ps[4 + q], sinT[:, tsl], ALU.mult, [pk[4 + q], "sinT"], [ko])
                    B.TT(a32, a32, cosT[:, tsl], ALU.mult, [ka, "cosT"], [ka])
                    dst = qr[:, j, tsl] if j < 4 else kr[:, j - 4, tsl]
                    B.TT(dst, a32, arot, ALU.add, [ka, ko], ["qr" if j < 4 else "kr"])
                proj_fm(W, c0, 128, wsl, idx, consq)
                idx += 1
            wv = carve([128, 8, 256], BF16)
            vst32 = [carve([128, 256]) for _ in range(2)]

            def consv(p, pkk, tt):
                q = tt % 2
                B.CP(vst32[q], p[:, 0:256], [pkk], [f"vst{q}"], eng="act")
                B.CP(vat[:, tt].rearrange("p a b -> p (a b)"), vst32[q], [f"vst{q}"], ["vat"])
                B.DMA(f"d_vst{q}", ov[tt * 128:(tt + 1) * 128, :], vst32[q], [f"vst{q}"], [])
            proj_tm(W, 2064 + 768, 256, wv, "wv", consv, pbase=4)
            ckt = carve([128, 2, 256])
            B.DMA("d_ck", ckt, ck.rearrange("(n p) f -> p n f", p=128), (), ["ckt"])
            for kv in range(2):
                for n2 in range(2):
                    B.TR(ps[3][:, (kv * 2 + n2) * 128:(kv * 2 + n2 + 1) * 128], ckt[:, n2, kv * 128:(kv + 1) * 128], I32, ["ckt", "cst"], [pk[3]])
            B.CP(kr[:, :, T:T + 256], ps[3].rearrange("p (a b) -> p a b", a=2), [pk[3]], ["kr"])
            B.DMA("d_cv", vat[:, 16:18].rearrange("p n a b -> p n (a b)"), cv.rearrange("(n p) f -> p n f", p=128), (), ["vat"], queue="pool")
            pT = [carve([128, 512], BF16) for _ in range(4)]
            rden = [carve([128, 512]) for _ in range(2)]
            asc = 128.0 ** -0.5
            iters = [(kv, qb, kt) for kv in range(2) for qb in range(8) for kt in range(18)]

            def emit_scores(it):
                kv, qb, kt = iters[it]
                pb = it % 3
                qsl = slice(qb * 256, (qb + 1) * 256)
                for g in range(2):
                    B.MM(ps[pb][:, g * 256:(g + 1) * 256], kr[:, kv, kt * 128:(kt + 1) * 128], qr[:, kv * 2 + g, qsl],
                         ["kr", "qr"], [pk[pb]])

            import os as _os
            AHEAD = int(_os.environ.get('ATT_AHEAD', '1'))
            for it in range(min(AHEAD, len(iters))):
                emit_scores(it)
            for it, (kv, qb, kt) in enumerate(iters):
                if it + AHEAD < len(iters):
                    emit_scores(it + AHEAD)
                pb = it % 3
                r4 = it % 4
                acc = (kv * 8 + qb) % 2
                pn, pd = 4 + 2 * acc, 5 + 2 * acc
                qsl = slice(qb * 256, (qb + 1) * 256)
                B.ACT(pT[r4], ps[pb], AF.Exp, [pk[pb], "mb"], [f"pT{r4}"], bias=mb[:, kt * 8 + qb:kt * 8 + qb + 1], scale=asc)
                B.MM(ps[pn], vat[:, kt, kv, :], pT[r4], ["vat", f"pT{r4}"], [pk[pn]], start=(kt == 0), stop=(kt == 17))
                B.MM(ps[pd], ONEb, pT[r4], ["cstb", f"pT{r4}"], [pk[pd]], start=(kt == 0), stop=(kt == 17))
                if kt == 17:
                    B.RCP(rden[acc], ps[pd], [pk[pd]], [f"rden{acc}"])
                    for g in range(2):
                        B.TT(mixT[:, 4 + kv * 2 + g, qsl], ps[pn][:, g * 256:(g + 1) * 256], rden[acc][:, g * 256:(g + 1) * 256], ALU.mult,
                             [pk[pn], f"rden{acc}"], ["mixT"])
            if self.stop_after == "attn":
                return
            S.barrier()
            self._aoff = off_persist
            wo = carve([128, 8, 1024], BF16)
            B.DMAS("d_wo", [(wo[:, c, :], ab_out[0][c * 128:(c + 1) * 128, :]) for c in range(8)], (), ["wbuf"], queue="pool")
            res_proj_tb(8, mixT, "mixT", mod(0, 2), wo, (0, 1))

        def mixer_c():
            new_phase()
            W = ml_in[0]
            mqT_d = B.scratch("mqT_d", [8, 64, T], BF16)
            mkT_d = B.scratch("mkT_d", [8, 64, T], BF16)
            mk_d = B.scratch("mk_d", [NCH, 128, 8, 64], BF16)
            mv_d = B.scratch("mv_d", [NCH, 128, 8, 128], BF16)
            mo_d = B.scratch("mo_d", [NCH, 128, 8, 128], BF16)
            hf_d = B.scratch("hf_d", [NCH, 128, 8, 128], F32)
            mixT = carve([128, 8, T], BF16)
            wsl = [carve([128, 8, 128], BF16) for _ in range(2)]
            off_persist = self._aoff
            nb_ = carve([64, T], BF16)
            idx = 0
            for grp in range(2):
                for h in range(8):
                    def cons(p, pkk, tb, grp=grp):
                        if grp == 0:
                            B.ACT(nb_[:, tb * 512:(tb + 1) * 512], p[0:64, :], AF.Copy, [pkk], ["nb"], scale=0.125)
                        else:
                            B.CP(nb_[:, tb * 512:(tb + 1) * 512], p[0:64, :], [pkk], ["nb"], eng="act")
                    proj_fm(W, grp * 512 + h * 64, 64, wsl, idx, cons)
                    idx += 1
                    B.DMA("d_mqk", (mqT_d if grp == 0 else mkT_d)[h], nb_, ["nb"], [("mq" if grp == 0 else "mk", h)])
            wbig = carve([128, 8, 512], BF16)
            st = [carve([128, 512], BF16) for _ in range(2)]
            jobs = [(512, mk_d, "p (h k) -> p h k", 64, 0, 8, "mktok", None),
                    (1024, mv_d, "p (h k) -> p h k", 128, 0, 4, "mvtok", None),
                    (1536, mv_d, "p (h k) -> p h k", 128, 4, 4, "mvtok", None),
                    (2048, mo_d, "p (h k) -> p h k", 128, 0, 4, "motok", AF.Sigmoid),
                    (2560, mo_d, "p (h k) -> p h k", 128, 4, 4, "motok", AF.Sigmoid)]
            for ji, (c0, dst, pat, kk, h0, nh, key, fn) in enumerate(jobs):
                def const(p, pkk, tt, dst=dst, kk=kk, h0=h0, nh=nh, key=key, fn=fn, ji=ji):
                    q = tt % 2
                    B.ACT(st[q], p, fn if fn is not None else AF.Copy, [pkk], [f"st{q}"])
                    B.DMA(f"d_st{q}", dst[tt][:, h0:h0 + nh, :], st[q].rearrange("p (h k) -> p h k", k=kk), [f"st{q}"], [(key, tt, ji)])
                proj_tm(W, c0, 512, wbig, "wbig", const)
            wif = carve([128, 8, 32], BF16)
            gif = carve([128, NCH, 32])
            b1 = carve([128, 32])
            B.DMA("d_b1", b1, bc1.partition_broadcast(128), (), ["b1"])

            def consif(p, pkk, tt):
                B.CP(gif[:, tt, :], p[:, 0:32], [pkk], ["gif"])
            proj_tm(W, 3072, 32, wif, "wif", consif)
            li = carve([128, 2, NCH, 8])
            lf = carve([128, 2, NCH, 8])
            t16 = carve([128, NCH, 16])
            B.TT(t16, gif[:, :, 0:16], b1[:, 0:16].unsqueeze(1).to_broadcast([128, NCH, 16]), ALU.add, ["gif", "b1"], ["t16"])
            for dr in range(2):
                B.CP(li[:, dr], t16[:, :, dr * 8:(dr + 1) * 8], ["t16"], ["li"])
            B.TT(t16, gif[:, :, 16:32], b1[:, 16:32].unsqueeze(1).to_broadcast([128, NCH, 16]), ALU.add, ["gif", "b1"], ["t16"])
            B.ACT(t16, t16, AF.Exp, ["t16"], ["t16"], scale=-1.0)
            B.ACT(t16, t16, AF.Ln, ["t16", "cst"], ["t16"], bias=ONE32[:, 0:1])
            for dr in range(2):
                B.TS(lf[:, dr], t16[:, :, dr * 8:(dr + 1) * 8], -1.0, ALU.mult, ["t16"], ["lf"])
            bb = carve([128, 2, NCH, 8])
            bl = carve([128, 2, NCH, 8])
            aa = carve([128, 2, NCH, 8])
            lw = carve([128, 2, NCH, 8])
            mend = carve([128, 2, NCH, 8])
            g2 = lambda a, dr: a[:, dr].rearrange("p n h -> p (n h)")
            for dr in range(2):
                tri = cst[:, C_U:C_U + 128] if dr == 0 else cst[:, C_LO:C_LO + 128]
                B.MM(ps[2][:, 0:128], tri, g2(lf, dr), ["cst", "lf"], [pk[2]])
                B.CP(g2(bb, dr), ps[2][:, 0:128], [pk[2]], ["bb"])
                B.MM(ps[2][:, 0:128], ONE32, g2(lf, dr), ["cst", "lf"], [pk[2]])
                B.CP(g2(bl, dr), ps[2][:, 0:128], [pk[2]], ["bl"])
            f2 = lambda a: a.rearrange("p d n h -> p (d n h)")
            B.TT(f2(aa), f2(li), f2(bb), ALU.subtract, ["li", "bb"], ["aa"])
            B.TT(f2(lw), f2(aa), f2(bl), ALU.add, ["aa", "bl"], ["lw"])
            mx = carve([128, 1])
            mxb = carve([128, 128])
            for dr in range(2):
                B.TR(ps[2][:, 0:128], g2(lw, dr), I32, ["lw", "cst"], [pk[2]])
                B.RED(mx, ps[2][:, 0:128], ALU.max, [pk[2]], ["mx"])
                B.CP(mxb, mx[:, 0:1].to_broadcast([128, 128]), ["mx"], ["mxb"])
                B.TR(ps[2][:, 0:128], mxb, I32, ["mxb", "cst"], [pk[2]])
                B.CP(g2(mend, dr), ps[2][:, 0:128], [pk[2]], ["mend"])

            Cst = [carve([64, 8, 128]) for _ in range(2)]
            Cb = [carve([64, 8, 128], BF16) for _ in range(2)]
            nst = [carve([64, 8]) for _ in range(2)]
            nbf = [carve([64, 8], BF16) for _ in range(2)]
            mst = [carve([128, 8]) for _ in range(2)]
            for dr in range(2):
                B.DMA("d_c0", Cst[dr], mC0[dr].rearrange("h k v -> k h v"), (), [f"C{dr}"])
                B.DMA("d_c0", nst[dr], mn0[dr].rearrange("h k -> k h"), (), [f"n{dr}"], allow_slow_non_contiguous=True)
                B.DMA("d_c0", mst[dr], mm0[dr].partition_broadcast(128), (), [f"m{dr}"])
                B.CP(Cb[dr], Cst[dr], [f"C{dr}"], [f"Cb{dr}"])
                B.CP(nbf[dr], nst[dr], [f"n{dr}"], [f"nb{dr}"])
            nwb = carve([128, 128])
            B.DMA("d_b1", nwb, mlnw.partition_broadcast(128), (), ["nwb"])
            hb_d = B.scratch("hb_d", [NCH, 128, 8, 128], F32)
            off_streams = self._aoff
            f3 = lambda a: a.rearrange("p h t -> p (h t)")

            import os as _os3
            POOLM = "pool" if _os3.environ.get("USE_POOL", "1") == "1" else "dve"

            def ml_stream(dr):
                X = f"y{dr}"
                K = lambda nm: X + nm
                pb = [ps[4 * dr + i] for i in range(4)]
                pkb = [pk[4 * dr + i] for i in range(4)]
                qc = [carve([64, 8, 128], BF16) for _ in range(2)]
                kc_ = [carve([64, 8, 128], BF16) for _ in range(2)]
                ktc = [carve([128, 8, 64], BF16) for _ in range(2)]
                vtc = [carve([128, 8, 128], BF16) for _ in range(2)]
                dg = carve([128, 4, 128])
                LD = carve([128, 8, 128])
                s32 = carve([128, 8, 128])
                sbf = carve([128, 8, 128], BF16)
                sT = carve([128, 8, 128], BF16)
                num = carve([128, 8, 128])
                hh = carve([128, 8, 128])
                wk = carve([128, 8, 64], BF16)
                sm = {k: carve([128, 8]) for k in ["mintra", "min", "mt", "winter", "emt", "rsum", "den", "ew", "mnew", "astate", "neg", "qn"]}
                order = range(NCH) if dr == 0 else range(NCH - 1, -1, -1)
                NEGi = cst[:, C_NLI:C_NLI + 128] if dr == 0 else cst[:, C_NUI:C_NUI + 128]
                for vi, n in enumerate(order):
                    s = vi % 2
                    B.DMA(f"d_qc{s}{X}", qc[s], mqT_d[:, :, n * 128:(n + 1) * 128].rearrange("h d t -> d h t"), [("mq", h) for h in range(8)], [K(f"qc{s}")])
                    B.DMA(f"d_kc{s}{X}", kc_[s], mkT_d[:, :, n * 128:(n + 1) * 128].rearrange("h d t -> d h t"), [("mk", h) for h in range(8)], [K(f"kc{s}")])
                    B.DMA(f"d_ktc{s}{X}", ktc[s], mk_d[n], [("mktok", n, 0)], [K(f"ktc{s}")])
                    B.DMA(f"d_vtc{s}{X}", vtc[s], mv_d[n], [("mvtok", n, 1), ("mvtok", n, 2)], [K(f"vtc{s}")])
                    bn = bb[:, dr, n, :]
                    for h in range(8):
                        B.MM(pb[h // 4][:, (h % 4) * 128:(h % 4 + 1) * 128], qc[s][:, h, :], kc_[s][:, h, :], [K(f"qc{s}"), K(f"kc{s}")], [pkb[h // 4]])
                    for hf in range(2):
                        B.TT(dg, bc4(I32), bcl(aa[:, dr, n, hf * 4:(hf + 1) * 4]), ALU.mult, ["cst", "aa"], [K("dg")], eng=POOLM)
                        yield
                        B.MM(pb[2 + hf], ONE32, f3(dg), ["cst", K("dg")], [pkb[2 + hf]])
                        yield
                        B.TT(LD[:, hf * 4:(hf + 1) * 4, :], v3(pb[2 + hf]), bc4(NEGi), ALU.add, [pkb[2 + hf], "cst"], [K("LD")])
                    B.TT(LD, LD, bcl(bn), ALU.add, [K("LD"), "bb"], [K("LD")], eng=POOLM)
                    B.RED(sm["mintra"], LD, ALU.max, [K("LD")], [K("mintra")])
                    B.TT(sm["min"], bn, mst[dr], ALU.add, ["bb", f"m{dr}"], [K("min")])
                    B.TT(sm["mt"], sm["min"], sm["mintra"], ALU.max, [K("min"), K("mintra")], [K("mt")])
                    B.TT(sm["winter"], sm["min"], sm["mt"], ALU.subtract, [K("min"), K("mt")], [K("winter")])
                    B.TT(LD, LD, bcl(sm["mt"]), ALU.subtract, [K("LD"), K("mt")], [K("LD")], eng=POOLM)
                    yield
                    B.ACT(sm["winter"], sm["winter"], AF.Exp, [K("winter")], [K("winter")])
                    B.ACT(sm["emt"], sm["mt"], AF.Exp, [K("mt")], [K("emt")], scale=-1.0)
                    B.ACT(f3(LD), f3(LD), AF.Exp, [K("LD")], [K("LD")])
                    yield
                    for hf in range(2):
                        B.TT(s32[:, hf * 4:(hf + 1) * 4, :], v3(pb[hf]), LD[:, hf * 4:(hf + 1) * 4, :], ALU.mult, [pkb[hf], K("LD")], [K("s32")])
                    B.RED(sm["rsum"], s32, ALU.add, [K("s32")], [K("rsum")])
                    yield
                    B.CP(sbf, s32, [K("s32")], [K("sbf")], eng="act")
                    for h in range(8):
                        B.MM(pb[0][:, h:h + 1], qc[s][:, h, :], nbf[dr][:, h:h + 1], [K(f"qc{s}"), f"nb{dr}"], [pkb[0]])
                    yield
                    B.TT(sm["qn"], pb[0][:, 0:8], sm["winter"], ALU.mult, [pkb[0], K("winter")], [K("qn")])
                    pbt = pb[2].bitcast(BF16)
                    for h in range(8):
                        B.TR(pbt[:, h * 128:(h + 1) * 128], sbf[:, h, :], Ib, [K("sbf"), "cstb"], [pkb[2]])
                    yield
                    B.CP(sT, pbt.rearrange("p (a b) -> p a b", a=8), [pkb[2]], [K("sT")], eng="act")
                    yield
                    for h in range(8):
                        B.MM(pb[h // 4][:, (h % 4) * 128:(h % 4 + 1) * 128], sT[:, h, :], vtc[s][:, h, :], [K("sT"), K(f"vtc{s}")], [pkb[h // 4]])
                    for h in range(8):
                        B.MM(pb[2 + h // 4][:, (h % 4) * 128:(h % 4 + 1) * 128], qc[s][:, h, :], Cb[dr][:, h, :], [K(f"qc{s}"), f"Cb{dr}"], [pkb[2 + h // 4]])
                    B.TT(sm["mnew"], bl[:, dr, n, :], mst[dr], ALU.add, ["bl", f"m{dr}"], [K("mnew")])
                    B.TT(sm["astate"], sm["mnew"], sm["mnew"], ALU.max, [K("mnew")], [K("astate")])
                    B.TT(sm["mnew"], sm["mnew"], mend[:, dr, n, :], ALU.max, [K("mnew"), "mend"], [K("mnew")])
                    B.TT(sm["astate"], sm["astate"], sm["mnew"], ALU.subtract, [K("astate"), K("mnew")], [K("astate")])
                    B.TT(sm["ew"], lw[:, dr, n, :], sm["mnew"], ALU.subtract, ["lw", K("mnew")], [K("ew")])
                    yield
                    B.ACT(sm["astate"], sm["astate"], AF.Exp, [K("astate")], [K("astate")])
                    B.ACT(sm["ew"], sm["ew"], AF.Exp, [K("ew")], [K("ew")])
                    yield
                    for hf in range(2):
                        hs_ = slice(hf * 4, (hf + 1) * 4)
                        B.TT(num[:, hs_, :], v3(pb[2 + hf]), bcl(sm["winter"][:, hs_]), ALU.mult, [pkb[2 + hf], K("winter")], [K("num")])
                        B.TT(num[:, hs_, :], num[:, hs_, :], v3(pb[hf]), ALU.add, [K("num"), pkb[hf]], [K("num")])
                    B.TT(sm["den"], sm["qn"], sm["rsum"], ALU.add, [K("qn"), K("rsum")], [K("den")])
                    B.TS(sm["neg"], sm["den"], -1.0, ALU.mult, [K("den")], [K("neg")])
                    B.TT(sm["den"], sm["den"], sm["neg"], ALU.max, [K("den"), K("neg")], [K("den")])
                    B.TT(sm["den"], sm["den"], sm["emt"], ALU.max, [K("den"), K("emt")], [K("den")])
                    B.RCP(sm["den"], sm["den"], [K("den")], [K("den")])
                    B.TT(hh, num, bcl(sm["den"]), ALU.mult, [K("num"), K("den")], [K("hh")], eng=POOLM)
                    B.DMA(f"d_hh{X}", (hf_d if dr == 0 else hb_d)[n], hh, [K("hh")], [("hh", dr, n)])
                    B.TT(wk, ktc[s], bcl(sm["ew"], 64), ALU.mult, [K(f"ktc{s}"), K("ew")], [K("wk")], eng=POOLM)
                    yield
                    for h in range(8):
                        B.MM(pb[h // 4][0:64, (h % 4) * 128:(h % 4 + 1) * 128], wk[:, h, :], vtc[s][:, h, :], [K("wk"), K(f"vtc{s}")], [pkb[h // 4]])
                    for h in range(8):
                        B.MM(pb[2][0:64, h:h + 1], wk[:, h, :], ONEb[:, 0:1], [K("wk"), "cstb"], [pkb[2]])
                    B.TT(Cst[dr], Cst[dr], bcl(sm["astate"][0:64, :]), ALU.mult, [f"C{dr}", K("astate")], [f"C{dr}"], eng=POOLM)
                    B.TT(nst[dr], nst[dr], sm["astate"][0:64, :], ALU.mult, [f"n{dr}", K("astate")], [f"n{dr}"])
                    yield
                    for hf in range(2):
                        hs_ = slice(hf * 4, (hf + 1) * 4)
                        B.TT(Cst[dr][:, hs_, :], Cst[dr][:, hs_, :], v3(pb[hf][0:64, :]), ALU.add, [f"C{dr}", pkb[hf]], [f"C{dr}"])
                    B.TT(nst[dr], nst[dr], pb[2][0:64, 0:8], ALU.add, [f"n{dr}", pkb[2]], [f"n{dr}"])
                    B.CP(mst[dr], sm["mnew"], [K("mnew")], [f"m{dr}"])
                    seg_end = (n % 2 == 1) if dr == 0 else (n % 2 == 0)
                    if seg_end:
                        sg_ = n // 2
                        B.DMA(f"d_oC{dr}", oC[sg_, dr].rearrange("h k v -> k h v"), Cst[dr], [f"C{dr}"], [])
                        B.DMA(f"d_on{dr}", on[sg_, dr].rearrange("h k -> k h"), nst[dr], [f"n{dr}"], [], allow_slow_non_contiguous=True)
                        B.DMA(f"d_om{dr}", om[sg_, dr:dr + 1, :], mst[dr][0:1, :], [f"m{dr}"], [])
                        B.TS(Cst[dr], Cst[dr], knc[0:64, 0:1], ALU.mult, [f"C{dr}", "knc"], [f"C{dr}"])
                        B.TS(nst[dr], nst[dr], knc[0:64, 0:1], ALU.mult, [f"n{dr}", "knc"], [f"n{dr}"])
                        B.TS(mst[dr], mst[dr], knc[:, 0:1], ALU.mult, [f"m{dr}", "knc"], [f"m{dr}"])
                    B.CP(Cb[dr], Cst[dr], [f"C{dr}"], [f"Cb{dr}"], eng="act")
                    B.CP(nbf[dr], nst[dr], [f"n{dr}"], [f"nb{dr}"], eng="act")
                    yield

            gens = [ml_stream(0), ml_stream(1)]
            alive = [True, True]
            while any(alive):
                for gi, g in enumerate(gens):
                    if alive[gi]:
                        try:
                            next(g)
                        except StopIteration:
                            alive[gi] = False
            S.barrier()
            self._aoff = off_streams
            NR = 4
            hfc = [carve([128, 8, 128]) for _ in range(NR)]
            hbc = [carve([128, 8, 128]) for _ in range(NR)]
            oc = [carve([128, 8, 128], BF16) for _ in range(NR)]
            sq2 = [carve([128, 8, 128]) for _ in range(2)]
            gb = [carve([128, 8, 128], BF16) for _ in range(NR)]
            ss = [carve([128, 8]) for _ in range(NR)]
            for n in range(NCH):
                s = n % NR
                q2 = n % 2
                tsl = slice(n * 128, (n + 1) * 128)
                B.DMA(f"d_hfc{s}", hfc[s], hf_d[n], (), [f"hfc{s}"])
                B.DMA(f"d_hbc{s}", hbc[s], hb_d[n], (), [f"hbc{s}"])
                B.DMA(f"d_oc{s}", oc[s], mo_d[n], (), [f"oc{s}"])
                B.TT(hfc[s], hfc[s], hbc[s], ALU.add, [f"hfc{s}", f"hbc{s}"], [f"hfc{s}"])
                B.ACT(f3(sq2[q2]), f3(hfc[s]), AF.Square, [f"hfc{s}"], [f"sq2{q2}"])
                B.RED(ss[s], sq2[q2], ALU.add, [f"sq2{q2}"], [f"ss{s}"])
                B.ACT(ss[s], ss[s], AF.Sqrt, [f"ss{s}", "epsc"], [f"ss{s}"], bias=epsc[:, 0:1], scale=1.0 / 128)
                B.RCP(ss[s], ss[s], [f"ss{s}"], [f"ss{s}"])
                B.TT(hfc[s], hfc[s], bcl(ss[s]), ALU.mult, [f"hfc{s}", f"ss{s}"], [f"hfc{s}"])
                B.TT(hfc[s], hfc[s], nwb.unsqueeze(1).to_broadcast([128, 8, 128]), ALU.mult, [f"hfc{s}", "nwb"], [f"hfc{s}"])
                B.TT(gb[s], hfc[s], oc[s], ALU.mult, [f"hfc{s}", f"oc{s}"], [f"gb{s}"])
                for hf in range(2):
                    pbi = 2 * s + hf
                    pbt = ps[pbi].bitcast(BF16)
                    for k in range(4):
                        h = hf * 4 + k
                        B.TR(pbt[:, k * 128:(k + 1) * 128], gb[s][:, h, :], Ib, [f"gb{s}", "cstb"], [pk[pbi]])
                    B.CP(mixT[:, hf * 4:(hf + 1) * 4, tsl], v3(pbt[:, 0:512]), [pk[pbi]], ["mixT"], eng="act")
            S.barrier()
            self._aoff = off_persist
            wo = carve([128, 8, 1024], BF16)
            B.DMAS("d_wo", [(wo[:, c, :], ml_out[0][c * 128:(c + 1) * 128, :]) for c in range(8)], (), ["wbuf"], queue="pool")
            res_proj_tb(8, mixT, "mixT", mod(1, 2), wo, (1, 1))

        def run():
            ada_phase(0)
            norm_phase(0, 0)
            mixer_ab()
            if self.stop_after in ("dn_pre", "dn", "attn", "mix0"):
                return
            ffn_phase(0)
            if self.stop_after == "l0":
                return
            norm_phase(1, 0)
            mixer_c()
            if self.stop_after == "mix1":
                return
            ffn_phase(1)
        run()
        if self.stop_after is not None:
            S.barrier()
            self._aoff = 0
            dbg = carve([128, 8, 512])
            for tb in range(4):
                B.DMA("d_dbg", dbg, xT_d[:, :, tb * 512:(tb + 1) * 512].rearrange("c p t -> p c t"), (), ["dbg"])
                B.DMA("d_dbg2", self.dbg_out(tb), dbg, ["dbg"], [])
        S.wait_all_dma("sp")
        with contextlib.ExitStack() as es:
            sems = {}
            for k in list(S.ENGS) + list(S.dma_cum.keys()):
                sems[k] = es.enter_context(nc.semaphore(str(k)))
            block = es.enter_context(nc.Block())
            S.emit(block, sems)
        return nc

    def dbg_out(self, tb):
        yv = self.dout["y"].rearrange("(c p q) d -> c p (q d)", c=8, p=128)
        return yv[:, :, tb * 512:(tb + 1) * 512].rearrange("c p t -> p c t")


def rope_tables(sample):
    if not sample:
        return np.ones((128, T), np.float32), np.zeros((128, T), np.float32)
    t = np.arange(T)
    rows = (t // 64).astype(np.float32)
    cols = (t % 64).astype(np.float32)
    nf = 32
    inv = (10000.0 ** (-np.arange(nf, dtype=np.float32) / nf)).astype(np.float32)
    ang = np.zeros((128, T), np.float32)
    for p in range(128):
        pos = rows if p < 64 else cols
        ang[p] = pos * inv[p % 32]
    return np.cos(ang).astype(np.float32), np.sin(ang).astype(np.float32)


_CACHE = {}


def kernel(**inp):
    f = lambda k: np.ascontiguousarray(np.asarray(inp[k], dtype=np.float32))
    stop_after = inp.get("_stop_after", None)
    key = ("nc", stop_after)
    if key not in _CACHE:
        _CACHE[key] = Builder(stop_after).build()
    nc = _CACHE[key]
    consts = make_consts()
    xp, xs = f("x_prompt"), f("x_sample")
    shared = {k: f(k) for k in ["ada_w", "ffn_w_gate", "ffn_w_up", "ffn_w_down", "ab_w_in", "ab_w_out", "ml_w_in", "ml_w_out"]}
    ada_b, n1, n2 = f("ada_b"), f("norm1_w"), f("norm2_w")
    vecB = np.concatenate([f("dn_conv_w")[0].reshape(5 * 12, 128), f("dn_norm_w")[0][None], f("at_q_norm")[0][None],
                           f("at_k_norm")[0][None]], axis=0)
    bc0 = np.concatenate([f("dn_A_log")[0].reshape(8), f("dn_dt_bias")[0].reshape(8)])
    bc1 = np.concatenate([f("ml_i_bias")[0].reshape(16), f("ml_f_bias")[0].reshape(16)])
    in_maps = []
    for c in range(8):
        sample = c >= 4
        m = dict(shared)
        m["consts"] = consts
        cond = f("c")[c - 4] if sample else f("c_ctx")
        m["vecA"] = np.stack([np.concatenate([ada_b[l].reshape(48, 128), n1[l].reshape(8, 128), n2[l].reshape(8, 128),
                                              cond.reshape(8, 128)], axis=0) for l in range(2)])
        m["vecB"] = vecB
        m["bc0"], m["bc1"], m["mlnw"] = bc0, bc1, f("ml_norm_w")[0]
        m["knf"] = np.array([1.0 if sample else 0.0], np.float32)
        mb = np.zeros((128, 18, 8), np.float32)
        if not sample:
            mb[:] = NEG
            for qb in range(8):
                mb[:, 2 * qb:2 * qb + 2, qb] = 0.0
        m["maskb"] = mb.reshape(128, 144)
        m["ropec"], m["ropes"] = rope_tables(sample)
        if sample:
            b = c - 4
            m["xin"] = xs[b]
            m["ck"] = f("cache_attn_k")[b, 0].reshape(256, 256)
            m["cv"] = f("cache_attn_v")[b, 0].reshape(256, 256)
            m["sd0"] = f("state_delta")[b, 0]
            m["mC0"] = f("state_mlstm_C")[b, 0]
            m["mn0"] = f("state_mlstm_n")[b, 0]
            m["mm0"] = f("state_mlstm_m")[b, 0]
        else:
            m["xin"] = xp[8 * c:8 * c + 8].reshape(T, D)
            m["ck"] = np.zeros((256, 256), np.float32)
            m["cv"] = np.zeros((256, 256), np.float32)
            m["sd0"] = np.zeros((2, 4, 128, 128), np.float32)
            m["mC0"] = np.zeros((2, 8, 64, 128), np.float32)
            m["mn0"] = np.zeros((2, 8, 64), np.float32)
            m["mm0"] = np.zeros((2, 8), np.float32)
        in_maps.append({k: np.ascontiguousarray(v) for k, v in m.items()})
    res = run_bass_kernel_spmd(nc, in_maps, core_ids=list(range(8)))
    R = res.results
    if stop_after is not None:
        return R
    y_prompt = np.concatenate([R[c]["y"].reshape(8, 256, D) for c in range(4)], axis=0)
    y_sample = np.stack([R[c]["y"] for c in range(4, 8)], axis=0)
    nk = np.concatenate([R[c]["ok"].reshape(8, 1, 256, 2, 128) for c in range(4)], axis=0)
    nv = np.concatenate([R[c]["ov"].reshape(8, 1, 256, 2, 128) for c in range(4)], axis=0)
    nd = np.concatenate([R[c]["od"].reshape(8, 1, 2, 4, 128, 128) for c in range(4)], axis=0)
    nC = np.concatenate([R[c]["oC"].reshape(8, 1, 2, 8, 64, 128) for c in range(4)], axis=0)
    nn = np.concatenate([R[c]["on"].reshape(8, 1, 2, 8, 64) for c in range(4)], axis=0)
    nm = np.concatenate([R[c]["om"].reshape(8, 1, 2, 8) for c in range(4)], axis=0)
    return tuple(np.ascontiguousarray(a, dtype=np.float32) for a in (y_prompt, y_sample, nk, nv, nd, nC, nn, nm))
```

```python
import contextlib
import numpy as np
import concourse.bass as bass
import concourse.mybir as mybir
from concourse.bass_utils import run_bass_kernel_spmd

F32 = mybir.dt.float32
BF16 = mybir.dt.bfloat16
AF = mybir.ActivationFunctionType
ALU = mybir.AluOpType
AX = mybir.AxisListType

D = 1024
T = 2048
NCH = 16
FF = 2816
NFT = 22
AB_IN = 3088
ML_IN = 3104
EPS = 1e-6
NEG = -30000.0

SAME_ENGINE_SYNC = {"act": True, "dve": True, "pool": True, "pe": False, "sp": False}


class Sched:
    ENGS = ("pe", "act", "dve", "pool", "sp")

    def __init__(self, nc):
        self.nc = nc
        self.ops = {e: [] for e in self.ENGS}
        self.n = {e: 0 for e in self.ENGS}
        self.waited = {e: {} for e in self.ENGS}
        self.last_w = {}
        self.readers = {}
        self.dma_cum = {}
        self.needed = {e: set() for e in self.ENGS}
        self.final_waits = []
        self.final_eng = None
        self.fence = {}
        self.phys_map = {}

    def _deps(self, eng, reads, writes):
        deps = []
        for k in reads:
            t = self.last_w.get(k)
            if t is not None:
                deps.append(t)
        for k in writes:
            t = self.last_w.get(k)
            if t is not None:
                deps.append(t)
            deps.extend(self.readers.get(k, ()))
        best = {}
        for (sk, v) in deps:
            if sk == eng and not SAME_ENGINE_SYNC.get(eng, False):
                continue
            if self.waited[eng].get(sk, 0) >= v:
                continue
            best[sk] = max(best.get(sk, 0), v)
        for sk, v in best.items():
            self.waited[eng][sk] = v
            if sk in self.ENGS:
                self.needed[sk].add(v)
        return list(best.items())

    def _commit(self, tok, reads, writes):
        for k in writes:
            self.last_w[k] = tok
            self.readers[k] = []
        for k in reads:
            if k in writes:
                continue
            self.readers.setdefault(k, []).append(tok)

    def op(self, eng, fn, reads=(), writes=(), nofence=False):
        waits = self._deps(eng, reads, writes)
        self.n[eng] += 1
        tok = (eng, self.n[eng])
        self.ops[eng].append((waits, fn, ("self", self.n[eng], nofence)))
        self._commit(tok, reads, writes)
        return tok

    def dma(self, queue, semkey, items, reads=(), writes=()):
        pk_ = (queue, semkey)
        if pk_ not in self.phys_map:
            nq = sum(1 for q, _ in self.phys_map if q == queue)
            self.phys_map[pk_] = f"dma_{queue}{nq}"
        semkey = self.phys_map[pk_]
        waits = self._deps(queue, reads, writes)
        cum = self.dma_cum.get(semkey, 0)
        if cum > 0 and self.waited[queue].get(semkey, 0) < cum:
            self.waited[queue][semkey] = cum
            waits.append((semkey, cum))
        final = cum + 16 * len(items)
        self.dma_cum[semkey] = final
        for i, (o, a, kw) in enumerate(items):
            def fn(e, o=o, a=a, kw=kw):
                return e.dma_start(out=o, in_=a, **kw)
            self.ops[queue].append((waits if i == 0 else [], fn, ("dma", semkey)))
        tok = (semkey, final)
        self._commit(tok, reads, writes)
        return tok

    def barrier(self):
        for e in self.ENGS:
            waits = []
            for e2 in self.ENGS:
                if e2 == e or self.n[e2] == 0 or e2 == "sp":
                    continue
                if self.waited[e].get(e2, 0) < self.n[e2]:
                    waits.append((e2, self.n[e2]))
                    self.waited[e][e2] = self.n[e2]
                    self.needed[e2].add(self.n[e2])
            for sk, cum in self.dma_cum.items():
                if self.waited[e].get(sk, 0) < cum:
                    waits.append((sk, cum))
                    self.waited[e][sk] = cum
            if waits:
                self.ops[e].append((waits, None, None))
        self.last_w = {}
        self.readers = {}
        self.phys_map = {}

    def wait_all_dma(self, eng="sp"):
        self.final_waits = [(sk, v) for sk, v in self.dma_cum.items()]
        self.final_eng = eng

    def emit(self, block, sems):
        rank = {}
        for e in self.ENGS:
            rank[e] = {v: i + 1 for i, v in enumerate(sorted(self.needed[e]))}

        def val(sk, v):
            return rank[sk][v] if sk in self.ENGS else v

        def run(e, h):
            for waits, fn, inc in self.ops[e]:
                for sk, v in waits:
                    h.wait_ge(sems[sk], val(sk, v))
                if fn is None:
                    continue
                ins = fn(h)
                if inc[0] == "self":
                    if inc[1] in rank[e]:
                        if e in self.fence and not inc[2]:
                            ins = self.fence[e](h)
                        ins.then_inc(sems[e], 1)
                else:
                    ins.then_inc(sems[inc[1]], 16)
            if self.final_waits and self.final_eng == e:
                for sk, v in self.final_waits:
                    h.wait_ge(sems[sk], v)

        @block.tensor
        def _(t):
            run("pe", t)

        @block.scalar
        def _(s):
            run("act", s)

        @block.vector
        def _(v):
            run("dve", v)

        @block.gpsimd
        def _(g):
            run("pool", g)

        @block.sync
        def _(s):
            run("sp", s)


C_I, C_ONE, C_U, C_LO, C_NLI, C_NUI, C_NLS, C_NUS, C_ROT = [i * 128 for i in range(9)]
NCONST = 9 * 128


def make_consts():
    i = np.arange(128)[:, None]
    j = np.arange(128)[None, :]
    c = np.zeros((128, NCONST), np.float32)
    c[:, C_I:C_I + 128] = (i == j)
    c[:, C_ONE:C_ONE + 128] = 1.0
    c[:, C_U:C_U + 128] = (i <= j)
    c[:, C_LO:C_LO + 128] = (i >= j)
    c[:, C_NLI:C_NLI + 128] = np.where(i >= j, 0.0, NEG)
    c[:, C_NUI:C_NUI + 128] = np.where(i <= j, 0.0, NEG)
    c[:, C_NLS:C_NLS + 128] = np.where(i > j, 0.0, NEG)
    c[:, C_NUS:C_NUS + 128] = np.where(i < j, 0.0, NEG)
    R = np.zeros((128, 128), np.float32)
    for p in range(128):
        if (p % 64) < 32:
            R[p, p + 32] = -1.0
        else:
            R[p, p - 32] = 1.0
    c[:, C_ROT:C_ROT + 128] = R.T
    return c


class Builder:
    def __init__(self, stop_after=None):
        self.stop_after = stop_after
        nc = bass.Bass("TRN2", target_bir_lowering=False)
        self.nc = nc
        self.S = Sched(nc)
        self.din = {}
        self.dout = {}
        self._uid = 0

    def inp(self, name, shape):
        self.din[name] = self.nc.dram_tensor(name, list(shape), F32, kind="ExternalInput").ap()
        return self.din[name]

    def outp(self, name, shape):
        self.dout[name] = self.nc.dram_tensor(name, list(shape), F32, kind="ExternalOutput").ap()
        return self.dout[name]

    def scratch(self, name, shape, dt):
        return self.nc.dram_tensor(name, list(shape), dt).ap()

    def sb(self, name, shape, dt=F32):
        return self.nc.alloc_sbuf_tensor(name, list(shape), dt).ap()

    def MM(self, out, lhsT, rhs, r, w, start=True, stop=True):
        self.S.op("pe", lambda e: e.matmul(out, lhsT=lhsT, rhs=rhs, start=start, stop=stop), r, w)

    def TR(self, out, in_, ident, r, w):
        self.S.op("pe", lambda e: e.transpose(out, in_, ident), r, w)

    def ACT(self, out, in_, func, r, w, bias=None, scale=1.0, accum=None):
        kw = {}
        if bias is not None:
            kw["bias"] = bias
        if accum is not None:
            kw["accum_out"] = accum
        self.S.op("act", lambda e: e.activation(out=out, in_=in_, func=func, scale=scale, **kw), r, w)

    def TT(self, out, a, b, op, r, w, eng="dve"):
        self.S.op(eng, lambda e: e.tensor_tensor(out=out, in0=a, in1=b, op=op), r, w)

    def TS(self, out, a, s1, op0, r, w, s2=None, op1=None, eng="dve"):
        if op1 is None:
            self.S.op(eng, lambda e: e.tensor_scalar(out=out, in0=a, scalar1=s1, scalar2=None, op0=op0), r, w)
        else:
            self.S.op(eng, lambda e: e.tensor_scalar(out=out, in0=a, scalar1=s1, scalar2=s2, op0=op0, op1=op1), r, w)

    def STT(self, out, in0, scalar, in1, op0, op1, r, w, eng="dve"):
        self.S.op(eng, lambda e: e.scalar_tensor_tensor(out=out, in0=in0, scalar=scalar, in1=in1, op0=op0, op1=op1), r, w)

    def CP(self, out, in_, r, w, eng="dve"):
        if eng == "act":
            self.S.op(eng, lambda e: e.activation(out=out, in_=in_, func=AF.Copy), r, w)
        else:
            self.S.op(eng, lambda e: e.tensor_copy(out=out, in_=in_), r, w)

    def RED(self, out, in_, op, r, w):
        self.S.op("dve", lambda e: e.tensor_reduce(out=out, in_=in_, axis=AX.X, op=op), r, w)

    def RCP(self, out, in_, r, w):
        self.S.op("dve", lambda e: e.reciprocal(out=out, in_=in_), r, w)

    def MSET(self, ap, v, w, eng="dve"):
        self.S.op(eng, lambda e: e.memset(ap, v), (), w)

    def DMAS(self, semkey, pairs, r, w, queue="sp"):
        self.S.dma(queue, semkey, [(o, a, {}) for o, a in pairs], r, w)

    def DMA(self, semkey, out, in_, r, w, queue="sp", **kw):
        self.S.dma(queue, semkey, [(out, in_, kw)], r, w)

    def build(self):
        nc, S = self.nc, self.S
        B = self
        xin = B.inp("xin", [T, D])
        consts = B.inp("consts", [128, NCONST])
        vecA = B.inp("vecA", [2, 72, 128])
        vecB = B.inp("vecB", [63, 128])
        bc0 = B.inp("bc0", [16])
        bc1 = B.inp("bc1", [32])
        mlnw = B.inp("mlnw", [128])
        knf = B.inp("knf", [1])
        maskb = B.inp("maskb", [128, 18 * 8])
        ropec = B.inp("ropec", [128, T])
        ropes = B.inp("ropes", [128, T])
        ck = B.inp("ck", [256, 256])
        cv = B.inp("cv", [256, 256])
        sd0 = B.inp("sd0", [2, 4, 128, 128])
        mC0 = B.inp("mC0", [2, 8, 64, 128])
        mn0 = B.inp("mn0", [2, 8, 64])
        mm0 = B.inp("mm0", [2, 8])
        ada_w = B.inp("ada_w", [2, D, 6 * D])
        ffn_g = B.inp("ffn_w_gate", [2, D, FF])
        ffn_u = B.inp("ffn_w_up", [2, D, FF])
        ffn_d = B.inp("ffn_w_down", [2, FF, D])
        ab_in = B.inp("ab_w_in", [1, D, AB_IN])
        ab_out = B.inp("ab_w_out", [1, D, D])
        ml_in = B.inp("ml_w_in", [1, D, ML_IN])
        ml_out = B.inp("ml_w_out", [1, D, D])

        y = B.outp("y", [T, D])
        ok = B.outp("ok", [T, 256])
        ov = B.outp("ov", [T, 256])
        od = B.outp("od", [8, 2, 4, 128, 128])
        oC = B.outp("oC", [8, 2, 8, 64, 128])
        on = B.outp("on", [8, 2, 8, 64])
        om = B.outp("om", [8, 2, 8])

        xT_d = B.scratch("xT_d", [8, 128, T], F32)

        ps = [nc.alloc_psum_tensor(f"ps{i}", [128, 512], F32).ap() for i in range(8)]
        pk = [f"ps{i}" for i in range(8)]

        cst = B.sb("cst", [128, NCONST])
        cstb = B.sb("cstb", [128, NCONST], BF16)
        epsc = B.sb("epsc", [128, 1])
        knc = B.sb("knc", [128, 1])
        hT = B.sb("hT", [128, 8, T], BF16)
        vA = B.sb("vA", [128, 2, 72])
        vB = B.sb("vB", [128, 63])
        modv = B.sb("modv", [128, 2, 48])
        AA = B.sb("AA", [128, 2, 2, 8])
        NFR = 64
        fsa = B.sb("fence_a", [128, 2 + NFR])
        fsv = B.sb("fence_v", [128, 2 + NFR])
        fcnt = {"act": 0, "dve": 0}

        def fence_act(e):
            fcnt["act"] += 1
            c = 2 + fcnt["act"] % NFR
            return e.activation(out=fsa[:, c:c + 1], in_=fsa[:, 0:1], func=AF.Copy)

        def fence_dve(e):
            fcnt["dve"] += 1
            c = 2 + fcnt["dve"] % NFR
            return e.tensor_copy(out=fsv[:, c:c + 1], in_=fsv[:, 0:1])
        USE_FENCE = False
        if USE_FENCE:
            S.fence["act"] = fence_act
            S.fence["dve"] = fence_dve
        arena = B.sb("arena", [128, 42500])
        self._aoff = 0

        def carve(shape, dt=F32):
            n = int(np.prod(shape[1:]))
            words = n if dt == F32 else (n + 1) // 2
            words = (words + 7) // 8 * 8
            v = arena[0:shape[0], self._aoff:self._aoff + words]
            self._aoff += words
            assert self._aoff <= 42500, self._aoff
            if dt != F32:
                v = v.bitcast(BF16)[:, 0:n]
            else:
                v = v[:, 0:n]
            if len(shape) == 2:
                return v
            names = " ".join(f"a{i}" for i in range(len(shape) - 1))
            kw = {f"a{i}": shape[i + 1] for i in range(len(shape) - 1)}
            return v.rearrange(f"p ({names}) -> p {names}", **kw)

        def new_phase():
            S.barrier()
            self._aoff = 0

        I32 = cst[:, C_I:C_I + 128]
        ONE32 = cst[:, C_ONE:C_ONE + 128]
        Ib = cstb[:, C_I:C_I + 128]
        ONEb = cstb[:, C_ONE:C_ONE + 128]

        def bc4(ap2d):
            return ap2d.unsqueeze(1).to_broadcast([128, 4, 128])

        def bcl(ap2d, n=128):
            return ap2d.unsqueeze(2).to_broadcast([ap2d.shape[0], ap2d.shape[1], n])

        def v3(ap, a=4):
            return ap.rearrange("p (a b) -> p a b", a=a)

        B.DMA("d_c", cst, consts, (), ["cst"])
        B.DMA("d_cb", cstb, consts, (), ["cstb"], queue="pool")
        S.op("dve", lambda e: e.memset(fsv, 0.0), (), ["fsv"], nofence=True)
        S.op("dve", lambda e: e.tensor_copy(out=fsv[:, 1:2], in_=fsv[:, 0:1]), ["fsv"], ["fsv1"], nofence=True)
        S.op("dve", lambda e: e.memset(epsc, EPS), (), ["epsc"], nofence=True)
        S.op("act", lambda e: e.activation(out=fsa, in_=epsc[:, 0:1].to_broadcast([128, 2 + NFR]), func=AF.Copy),
             ["epsc"], ["fsa"], nofence=True)
        S.op("act", lambda e: e.activation(out=fsa[:, 1:2], in_=fsa[:, 0:1], func=AF.Copy), ["fsa"], ["fsa1"], nofence=True)
        B.DMA("d_c", knc, knf.partition_broadcast(128), (), ["knc"])
        vst = carve([128, 128])
        for l in range(2):
            B.DMA("d_v", vst[0:72, :], vecA[l], (), ["vst"])
            B.TR(ps[0][:, 0:72], vst[0:72, :], cst[0:72, C_I:C_I + 72], ["vst", "cst"], [pk[0]])
            B.CP(vA[:, l, :], ps[0][:, 0:72], [pk[0]], ["vA"])
        B.DMA("d_v", vst[0:63, :], vecB, (), ["vst"])
        B.TR(ps[0][:, 0:63], vst[0:63, :], cst[0:63, C_I:C_I + 63], ["vst", "cst"], [pk[0]])
        B.CP(vB, ps[0][:, 0:63], [pk[0]], ["vB"])

        xs = [carve([128, D]) for _ in range(2)]
        xo = [carve([128, 8, 128]) for _ in range(2)]
        for tt in range(NCH):
            s = tt % 2
            B.DMA(f"d_xs{s}", xs[s], xin[tt * 128:(tt + 1) * 128, :], (), [f"xs{s}"])
            for c in range(8):
                b = c // 4
                B.TR(ps[b][:, (c % 4) * 128:(c % 4 + 1) * 128], xs[s][:, c * 128:(c + 1) * 128], I32,
                     [f"xs{s}", "cst"], [pk[b]])
            for b in range(2):
                B.CP(xo[s][:, b * 4:(b + 1) * 4, :], v3(ps[b]), [pk[b]], [f"xo{s}"], eng=("dve" if b == 0 else "act"))
            B.DMA(f"d_xo{s}", xT_d[:, :, tt * 128:(tt + 1) * 128].rearrange("c p t -> p c t"), xo[s],
                  [f"xo{s}"], [("xT", tt // 4)])

        def ada_gen(l, nslots, psb):
            scb = carve([128, 8], BF16)
            sc32 = carve([128, 8])
            B.ACT(sc32, vA[:, l, 64:72], AF.Silu, ["vA"], [f"sc32{l}"])
            B.CP(scb, sc32, [f"sc32{l}"], [f"scb{l}"])
            wa = [carve([128, 8, 512], BF16) for _ in range(nslots)]
            for g in range(12):
                s = g % nslots
                B.DMA(f"d_wa{l}{s}", wa[s], ada_w[l][:, g * 512:(g + 1) * 512].rearrange("(c p) n -> p c n", p=128),
                      (), [f"wa{l}{s}"], queue="pool")
                for j in range(4):
                    col = g * 4 + j
                    for kc in range(8):
                        B.MM(ps[psb][:, col:col + 1], wa[s][:, kc, j * 128:(j + 1) * 128], scb[:, kc:kc + 1],
                             [f"wa{l}{s}", f"scb{l}"], [pk[psb]], start=(kc == 0), stop=(kc == 7))
                yield
            B.TT(modv[:, l, :], ps[psb][:, 0:48], vA[:, l, 0:48], ALU.add, [pk[psb], "vA"], ["modv"])
            for i in range(2):
                sc = modv[:, l, (1 + 3 * i) * 8:(2 + 3 * i) * 8]
                B.STT(AA[:, l, i, :], sc, 1.0, vA[:, l, 48 + 8 * i:56 + 8 * i], ALU.add, ALU.mult, ["modv", "vA"], ["AA"])

        def ada_phase(l):
            for _ in ada_gen(l, 4, 2):
                pass

        def mod(l, j):
            return modv[:, l, j * 8:(j + 1) * 8]

        def norm_phase(l, i):
            new_phase()
            xt = [carve([128, 8, 512]) for _ in range(2)]
            sq = [carve([128, 512]) for _ in range(2)]
            rs = carve([128, 512])
            tmp = [carve([128, 512]) for _ in range(2)]
            for tb in range(4):
                s = tb % 2
                B.DMA(f"d_xt{s}", xt[s], xT_d[:, :, tb * 512:(tb + 1) * 512].rearrange("c p t -> p c t"),
                      [("xT", tb)], [f"xt{s}"])
                for c in range(8):
                    q = c % 2
                    B.ACT(sq[q], xt[s][:, c, :], AF.Square, [f"xt{s}"], [f"sq{q}"])
                    B.MM(ps[3], ONE32, sq[q], ["cst", f"sq{q}"], [pk[3]], start=(c == 0), stop=(c == 7))
                B.ACT(rs, ps[3], AF.Sqrt, [pk[3], "epsc"], ["rs"], bias=epsc[:, 0:1], scale=1.0 / D)
                B.RCP(rs, rs, ["rs"], ["rs"])
                for c in range(8):
                    q = c % 2
                    B.STT(tmp[q], xt[s][:, c, :], AA[:, l, i, c:c + 1], rs, ALU.mult, ALU.mult,
                          [f"xt{s}", "AA", "rs"], [f"tmp{q}"])
                    B.ACT(hT[:, c, tb * 512:(tb + 1) * 512], tmp[q], AF.Identity, [f"tmp{q}", "modv"], ["hT"],
                          bias=mod(l, 3 * i)[:, c:c + 1])

        def res_proj(w_ap, KC, rhs, rkey, gate, wbuf, final, tok0, ntb):
            xr = [carve([128, 512]) for _ in range(2)]
            yo = [carve([128, 4, 128]) for _ in range(2)] if final else None
            cnt = 0
            for dt in range(8):
                for tb in range(ntb):
                    s = cnt % 2
                    cnt += 1
                    gtb = tok0 // 512 + tb
                    B.DMA(f"d_xr{s}", xr[s], xT_d[dt, :, gtb * 512:(gtb + 1) * 512], [("xT", dt, gtb)], [f"xr{s}"])
                    pb = 4 + s
                    for kc in range(KC):
                        B.MM(ps[pb], wbuf[:, kc, dt * 128:(dt + 1) * 128], rhs[:, kc, tb * 512:(tb + 1) * 512],
                             ["wbuf", rkey], [pk[pb]], start=(kc == 0), stop=(kc == KC - 1))
                    B.STT(xr[s], ps[pb], gate[:, dt:dt + 1], xr[s], ALU.mult, ALU.add, [pk[pb], "modv", f"xr{s}"], [f"xr{s}"])
                    if not final:
                        B.DMA(f"d_xw{s}", xT_d[dt, :, gtb * 512:(gtb + 1) * 512], xr[s], [f"xr{s}"], [("xT", dt, gtb)])
                    else:
                        pt = 6 + s
                        for k in range(4):
                            B.TR(ps[pt][:, k * 128:(k + 1) * 128], xr[s][:, k * 128:(k + 1) * 128], I32,
                                 [f"xr{s}", "cst"], [pk[pt]])
                        B.CP(yo[s], v3(ps[pt]), [pk[pt]], [f"yo{s}"], eng="act")
                        B.DMA(f"d_yo{s}", y.rearrange("(n p) f -> p n f", p=128)[:, gtb * 4:(gtb + 1) * 4, dt * 128:(dt + 1) * 128],
                              yo[s], [f"yo{s}"], [])

        def res_proj_tb(KC, rhs, rkey, gate, wbuf, norm):
            l, i = norm
            xr = [carve([128, 8, 512]) for _ in range(2)]
            sq = [carve([128, 512]) for _ in range(2)]
            rs = carve([128, 512])
            tmp = [carve([128, 512]) for _ in range(2)]
            cnt = 0
            for tb in range(4):
                s = tb % 2
                tsl = slice(tb * 512, (tb + 1) * 512)
                B.DMA(f"d_xr{s}", xr[s], xT_d[:, :, tsl].rearrange("c p t -> p c t"), (), [f"xr{s}"])
                for dt in range(8):
                    pb = 4 + cnt % 2
                    cnt += 1
                    for kc in range(KC):
                        B.MM(ps[pb], wbuf[:, kc, dt * 128:(dt + 1) * 128], rhs[:, kc, tsl],
                             ["wbuf", rkey], [pk[pb]], start=(kc == 0), stop=(kc == KC - 1))
                    B.STT(xr[s][:, dt, :], ps[pb], gate[:, dt:dt + 1], xr[s][:, dt, :], ALU.mult, ALU.add,
                          [pk[pb], "modv", f"xr{s}"], [f"xr{s}"])
                B.DMA(f"d_xw{s}", xT_d[:, :, tsl].rearrange("c p t -> p c t"), xr[s], [f"xr{s}"], [])
                for c in range(8):
                    q = c % 2
                    B.ACT(sq[q], xr[s][:, c, :], AF.Square, [f"xr{s}"], [f"sq{q}"])
                    B.MM(ps[3], ONE32, sq[q], ["cst", f"sq{q}"], [pk[3]], start=(c == 0), stop=(c == 7))
                B.ACT(rs, ps[3], AF.Sqrt, [pk[3], "epsc"], ["rs"], bias=epsc[:, 0:1], scale=1.0 / D)
                B.RCP(rs, rs, ["rs"], ["rs"])
                for c in range(8):
                    q = c % 2
                    B.STT(tmp[q], xr[s][:, c, :], AA[:, l, i, c:c + 1], rs, ALU.mult, ALU.mult,
                          [f"xr{s}", "AA", "rs"], [f"tmp{q}"])
                    B.ACT(hT[:, c, tsl], tmp[q], AF.Identity, [f"tmp{q}", "modv"], ["hT"],
                          bias=mod(l, 3 * i)[:, c:c + 1])

        def ffn_phase(l):
            new_phase()
            aT = carve([128, NFT, T], BF16)
            wd = carve([128, NFT, 1024], BF16)
            wg = [carve([128, 8, 256], BF16) for _ in range(2)]
            wu = [carve([128, 8, 256], BF16) for _ in range(2)]
            sg = [carve([128, 512]) for _ in range(2)]
            cnt = 0
            for g in range(11):
                s = g % 2
                B.DMA(f"d_wg{s}", wg[s], ffn_g[l][:, g * 256:(g + 1) * 256].rearrange("(c p) n -> p c n", p=128),
                      (), [f"wg{s}"], queue="pool")
                B.DMA(f"d_wu{s}", wu[s], ffn_u[l][:, g * 256:(g + 1) * 256].rearrange("(c p) n -> p c n", p=128),
                      (), [f"wu{s}"], queue="pool")
                if g == 1:
                    B.DMAS("d_wd", [(wd[:, f, :], ffn_d[l][f * 128:(f + 1) * 128, :]) for f in range(NFT)], (), ["wbuf"], queue="pool")
                for j in range(2):
                    f = g * 2 + j
                    for tb in range(4):
                        q = cnt % 2
                        cnt += 1
                        tsl = slice(tb * 512, (tb + 1) * 512)
                        for kc in range(8):
                            B.MM(ps[q], wg[s][:, kc, j * 128:(j + 1) * 128], hT[:, kc, tsl],
                                 [f"wg{s}", "hT"], [pk[q]], start=(kc == 0), stop=(kc == 7))
                        for kc in range(8):
                            B.MM(ps[2 + q], wu[s][:, kc, j * 128:(j + 1) * 128], hT[:, kc, tsl],
                                 [f"wu{s}", "hT"], [pk[2 + q]], start=(kc == 0), stop=(kc == 7))
                        B.ACT(sg[q], ps[q], AF.Silu, [pk[q]], [f"sg{q}"])
                        B.TT(aT[:, f, tsl], sg[q], ps[2 + q], ALU.mult, [f"sg{q}", pk[2 + q]], ["aT"])
            res_proj(None, NFT, aT, "aT", mod(l, 5), wd, final=(l == 1), tok0=0, ntb=4)

        def proj_fm(w_ap, c0, M, wslots, idx, consume):
            s = idx % 2
            wt = wslots[s]
            B.DMA(f"d_wt{s}", wt[:, :, 0:M], w_ap[:, c0:c0 + M].rearrange("(c p) n -> p c n", p=128), (), [f"wt{s}"], queue="pool")
            for tb in range(4):
                pb = (idx * 4 + tb) % 2
                for kc in range(8):
                    B.MM(ps[pb][0:M, :], wt[:, kc, 0:M], hT[:, kc, tb * 512:(tb + 1) * 512], [f"wt{s}", "hT"], [pk[pb]],
                         start=(kc == 0), stop=(kc == 7))
                consume(ps[pb], pk[pb], tb)

        def proj_tm(w_ap, c0, N, wbuf, wkey, consume, pbase=2):
            B.DMA("d_" + wkey, wbuf[:, :, 0:N], w_ap[:, c0:c0 + N].rearrange("(c p) n -> p c n", p=128), (), [wkey], queue="pool")
            for tt in range(NCH):
                pb = pbase + tt % 2
                for kc in range(8):
                    B.MM(ps[pb][:, 0:N], hT[:, kc, tt * 128:(tt + 1) * 128], wbuf[:, kc, 0:N], ["hT", wkey], [pk[pb]],
                         start=(kc == 0), stop=(kc == 7))
                consume(ps[pb], pk[pb], tt)

        def mixer_ab():
            new_phase()
            W = ab_in[0]
            qT_d = B.scratch("qT_d", [4, 128, T], BF16)
            kT_d = B.scratch("kT_d", [4, 128, T], BF16)
            zs_d = B.scratch("zs_d", [4, 128, T], BF16)
            ktok_d = B.scratch("ktok_d", [NCH, 128, 4, 128], BF16)
            vtok_d = B.scratch("vtok_d", [NCH, 128, 4, 128], BF16)
            oTf_d = B.scratch("oTf_d", [NCH, 128, 4, 128], F32)
            mixT = carve([128, 8, T], BF16)
            wsl = [carve([128, 8, 128], BF16) for _ in range(2)]
            off_persist = self._aoff

            wab = carve([128, 8, 16], BF16)
            ab = carve([128, NCH, 16])
            b0 = carve([128, 16])
            gg = carve([128, 2, NCH, 4])
            nbeta = carve([128, 2, NCH, 4])
            beta = carve([128, 2, NCH, 4])
            t8 = carve([128, NCH, 8])
            nA = carve([128, 8])
            gc = carve([128, 2, NCH, 4])
            gl = carve([128, 2, NCH, 4])
            EG = carve([128, 2, NCH, 4])
            EKD = carve([128, 2, NCH, 4])
            EGL = carve([128, 2, NCH, 4])
            off_rec = self._aoff
            ci2 = [carve([128, 8, 260]) for _ in range(3)]
            acc2 = [carve([128, 8, 256]) for _ in range(3)]
            sa2 = [carve([128, T]) for _ in range(3)]
            sqb = [carve([128, 512]) for _ in range(2)]
            rsb2 = [carve([128, 512]) for _ in range(3)]
            nb2 = [carve([128, T], BF16) for _ in range(3)]
            tok2 = [carve([128, NCH, 128], BF16) for _ in range(3)]
            for par in range(3):
                B.MSET(ci2[par], 0.0, [f"ci{par}"])
            idx = 0
            ag = ada_gen(1, 2, 6)
            for grp in range(3):
                for h in range(4):
                    next(ag, None)
                    ct = grp * 4 + h
                    par = idx % 3
                    ci, acc, sa, nb_, tok, rsb = ci2[par], acc2[par], sa2[par], nb2[par], tok2[par], rsb2[par]
                    kci, kacc, ksa, knb, ktk, krs = f"ci{par}", f"acc{par}", f"sa{par}", f"nb{par}", f"tok{par}", f"rsb{par}"
                    accf = acc.rearrange("p s t -> p (s t)")

                    def cons(p, pkk, tb, ci=ci, kci=kci):
                        B.CP(ci[:, 2 * tb:2 * tb + 2, 2:258], p.rearrange("p (s t) -> p s t", s=2), [pkk], [kci], eng="act")
                    proj_fm(W, ct * 128, 128, wsl, idx, cons)
                    idx += 1
                    B.TS(ci[:, 1:8, 0:2], ci[:, 0:7, 256:258], knc[:, 0:1], ALU.mult, [kci, "knc"], [kci])
                    B.TS(ci[:, 0:7, 258:260], ci[:, 1:8, 2:4], knc[:, 0:1], ALU.mult, [kci, "knc"], [kci])
                    B.TS(acc, ci[:, :, 0:256], vB[:, ct:ct + 1], ALU.mult, [kci, "vB"], [kacc])
                    for j in range(1, 5):
                        B.STT(acc, ci[:, :, j:j + 256], vB[:, j * 12 + ct:j * 12 + ct + 1], acc, ALU.mult, ALU.add,
                              [kci, "vB", kacc], [kacc])
                    B.ACT(sa, accf, AF.Silu, [kacc], [ksa])
                    if grp < 2:
                        for tb in range(4):
                            q = tb % 2
                            B.ACT(sqb[q], sa[:, tb * 512:(tb + 1) * 512], AF.Square, [ksa], [f"sqb{q}"])
                            B.MM(ps[2], ONE32, sqb[q], ["cst", f"sqb{q}"], [pk[2]])
                            B.ACT(rsb, ps[2], AF.Sqrt, [pk[2], "epsc"], [krs], bias=epsc[:, 0:1])
                            B.RCP(rsb, rsb, [krs], [krs])
                            B.TT(nb_[:, tb * 512:(tb + 1) * 512], sa[:, tb * 512:(tb + 1) * 512], rsb, ALU.mult, [ksa, krs], [knb])
                        B.DMA(f"d_qk{par}", (qT_d if grp == 0 else kT_d)[h], nb_, [knb], [("qT" if grp == 0 else "kT", h)])
                    else:
                        B.CP(nb_, sa, [ksa], [knb])
                    if grp >= 1:
                        for g4 in range(4):
                            pbt = ps[3].bitcast(BF16)
                            for k in range(4):
                                n = g4 * 4 + k
                                B.TR(pbt[:, k * 128:(k + 1) * 128], nb_[:, n * 128:(n + 1) * 128], Ib, [knb, "cstb"], [pk[3]])
                            B.CP(tok[:, g4 * 4:(g4 + 1) * 4, :], v3(pbt[:, 0:512]), [pk[3]], [ktk], eng="act")
                        dst = ktok_d if grp == 1 else vtok_d
                        B.DMA(f"d_tok{par}", dst[:, :, h, :].rearrange("n t d -> t n d"), tok, [ktk], [("ktok" if grp == 1 else "vtok", h)])
            for h in range(4):
                par = idx % 3
                nb_, knb = nb2[par], f"nb{par}"

                def consz(p, pkk, tb, nb_=nb_, knb=knb):
                    B.ACT(nb_[:, tb * 512:(tb + 1) * 512], p, AF.Silu, [pkk], [knb])
                proj_fm(W, 1536 + h * 128, 128, wsl, idx, consz)
                idx += 1
                B.DMA(f"d_qk{par}", zs_d[h], nb_, [knb], [("zs", h)])
            for _ in ag:
                pass
            B.DMA("d_b0", b0, bc0.partition_broadcast(128), (), ["b0"])

            def consab(p, pkk, tt):
                B.CP(ab[:, tt, :], p[:, 0:16], [pkk], ["ab"])
            proj_tm(W, 2048, 16, wab, "wab", consab)
            B.ACT(nA, b0[:, 0:8], AF.Exp, ["b0"], ["nA"])
            B.TS(nA, nA, -1.0, ALU.mult, ["nA"], ["nA"])
            B.TT(t8, ab[:, :, 0:8], b0[:, 8:16].unsqueeze(1).to_broadcast([128, NCH, 8]), ALU.add, ["ab", "b0"], ["t8"])
            B.ACT(t8, t8, AF.Exp, ["t8"], ["t8"])
            B.ACT(t8, t8, AF.Ln, ["t8", "cst"], ["t8"], bias=ONE32[:, 0:1])
            B.TT(t8, t8, nA.unsqueeze(1).to_broadcast([128, NCH, 8]), ALU.mult, ["t8", "nA"], ["t8"])
            for dr in range(2):
                B.CP(gg[:, dr], t8[:, :, dr * 4:(dr + 1) * 4], ["t8"], ["gg"])
            B.ACT(t8, ab[:, :, 8:16], AF.Sigmoid, ["ab"], ["t8"])
            for dr in range(2):
                B.CP(beta[:, dr], t8[:, :, dr * 4:(dr + 1) * 4], ["t8"], ["beta"])
                B.TS(nbeta[:, dr], t8[:, :, dr * 4:(dr + 1) * 4], -1.0, ALU.mult, ["t8"], ["nbeta"])
            for dr in range(2):
                tri = cst[:, C_U:C_U + 128] if dr == 0 else cst[:, C_LO:C_LO + 128]
                B.MM(ps[2][:, 0:64], tri, gg[:, dr].rearrange("p n h -> p (n h)"), ["cst", "gg"], [pk[2]])
                B.CP(gc[:, dr].rearrange("p n h -> p (n h)"), ps[2][:, 0:64], [pk[2]], ["gc"])
                B.MM(ps[2][:, 0:64], ONE32, gg[:, dr].rearrange("p n h -> p (n h)"), ["cst", "gg"], [pk[2]])
                B.CP(gl[:, dr].rearrange("p n h -> p (n h)"), ps[2][:, 0:64], [pk[2]], ["gl"])
            f2 = lambda a: a.rearrange("p d n h -> p (d n h)")
            B.ACT(f2(EG), f2(gc), AF.Exp, ["gc"], ["EG"])
            B.ACT(f2(EGL), f2(gl), AF.Exp, ["gl"], ["EGL"])
            B.TT(f2(EKD), f2(gl), f2(gc), ALU.subtract, ["gl", "gc"], ["EKD"])
            B.ACT(f2(EKD), f2(EKD), AF.Exp, ["EKD"], ["EKD"])

            if self.stop_after == "dn_pre":
                return
            S.barrier()
            self._aoff = off_rec
            oTb_d = B.scratch("oTb_d", [NCH, 128, 4, 128], F32)
            f3 = lambda a: a.rearrange("p h t -> p (h t)")
            scale = 128.0 ** -0.5
            Sst = [carve([128, 4, 128]) for _ in range(2)]
            Sb = [carve([128, 4, 128], BF16) for _ in range(2)]
            for dr in range(2):
                B.DMA("d_s0", Sst[dr], sd0[dr].rearrange("h d v -> d h v"), (), [f"S{dr}"])
                B.CP(Sb[dr], Sst[dr], [f"S{dr}"], [f"Sb{dr}"])

            import os as _os2
            R32 = (lambda a: a.bitcast(mybir.dt.float32r)) if _os2.environ.get("DN_F32R", "0") == "1" else (lambda a: a)

            off_comb = self._aoff

            POOLE = "pool" if _os2.environ.get("USE_POOL", "1") == "1" else "dve"

            def dn_stream(dr):
                X = f"x{dr}"
                pb = [ps[4 * dr + i] for i in range(4)]
                pkb = [pk[4 * dr + i] for i in range(4)]
                A_, B_, C_, D_ = 0, 1, 2, 3
                qc = [carve([128, 4, 128], BF16) for _ in range(2)]
                kc_ = [carve([128, 4, 128], BF16) for _ in range(2)]
                ktc = [carve([128, 4, 128], BF16) for _ in range(2)]
                vtc = [carve([128, 4, 128], BF16) for _ in range(2)]
                wA = carve([128, 4, 128])
                wB = carve([128, 4, 128])
                EGr = carve([128, 4, 128])
                u_ = carve([128, 4, 128])
                ot = carve([128, 4, 128])
                QKDT = carve([128, 4, 128], BF16)
                Pf = [carve([128, 4, 128]) for _ in range(2)]
                P = [R32(a) for a in Pf]
                PT = [R32(carve([128, 4, 128])) for _ in range(2)]
                RT = [R32(carve([128, 4, 128])) for _ in range(2)]
                Pb = [carve([128, 4, 128], BF16) for _ in range(2)]
                PTb = [carve([128, 4, 128], BF16) for _ in range(2)]
                RTb = [carve([128, 4, 128], BF16) for _ in range(2)]
                MT = carve([128, 4, 128], BF16)
                kg = carve([128, 4, 128], BF16)
                kdec = carve([128, 4, 128], BF16)
                wT = carve([128, 4, 128], BF16)
                vnew = carve([128, 4, 128], BF16)
                qd = carve([128, 4, 128], BF16)
                order = range(NCH) if dr == 0 else range(NCH - 1, -1, -1)
                NEGs = cst[:, C_NLS:C_NLS + 128] if dr == 0 else cst[:, C_NUS:C_NUS + 128]
                NEGt = cst[:, C_NUI:C_NUI + 128] if dr == 0 else cst[:, C_NLI:C_NLI + 128]
                K = lambda nm: X + nm
                for vi, n in enumerate(order):
                    s = vi % 2
                    tsl = slice(n * 128, (n + 1) * 128)
                    B.DMA(f"d_qc{s}{X}", qc[s], qT_d[:, :, tsl].rearrange("h d t -> d h t"), [("qT", h) for h in range(4)], [K(f"qc{s}")])
                    B.DMA(f"d_kc{s}{X}", kc_[s], kT_d[:, :, tsl].rearrange("h d t -> d h t"), [("kT", h) for h in range(4)], [K(f"kc{s}")])
                    B.DMA(f"d_ktc{s}{X}", ktc[s], ktok_d[n], [("ktok", h) for h in range(4)], [K(f"ktc{s}")])
                    B.DMA(f"d_vtc{s}{X}", vtc[s], vtok_d[n], [("vtok", h) for h in range(4)], [K(f"vtc{s}")])
                    gcn = gc[:, dr, n, :]
                    for h in range(4):
                        B.MM(pb[A_][:, h * 128:(h + 1) * 128], kc_[s][:, h, :], kc_[s][:, h, :], [K(f"kc{s}")], [pkb[A_]])
                    for h in range(4):
                        B.MM(pb[B_][:, h * 128:(h + 1) * 128], kc_[s][:, h, :], qc[s][:, h, :], [K(f"kc{s}"), K(f"qc{s}")], [pkb[B_]])
                    B.TT(wA, bc4(I32), bcl(gcn), ALU.mult, ["cst", "gc"], [K("wA")], eng=POOLE)
                    yield
                    B.MM(pb[C_], ONE32, f3(wA), ["cst", K("wA")], [pkb[C_]])
                    yield
                    B.TT(wA, bc4(NEGs), v3(pb[C_]), ALU.subtract, ["cst", pkb[C_]], [K("wA")])
                    B.TT(wA, wA, bcl(gcn), ALU.add, [K("wA"), "gc"], [K("wA")])
                    B.TT(wB, v3(pb[C_]), bc4(NEGt), ALU.add, ["cst", pkb[C_]], [K("wB")])
                    B.TT(wB, wB, bcl(gcn), ALU.subtract, [K("wB"), "gc"], [K("wB")])
                    yield
                    B.ACT(f3(wA), f3(wA), AF.Exp, [K("wA")], [K("wA")])
                    B.ACT(f3(wB), f3(wB), AF.Exp, [K("wB")], [K("wB")])
                    B.ACT(f3(EGr), pb[C_], AF.Exp, [pkb[C_], K("wA"), K("wB")], [K("EGr")])
                    yield
                    B.TT(wA, v3(pb[A_]), wA, ALU.mult, [pkb[A_], K("wA")], [K("wA")])
                    B.TT(P[0], wA, bcl(nbeta[:, dr, n, :]), ALU.mult, [K("wA"), "nbeta"], [K("P0")], eng=POOLE)
                    B.STT(QKDT, v3(pb[B_]), scale, wB, ALU.mult, ALU.mult, [pkb[B_], K("wB")], [K("QKDT")])
                    yield
                    for h in range(4):
                        B.TR(pb[D_][:, h * 128:(h + 1) * 128], Pf[0][:, h, :], I32, [K("P0"), "cst"], [pkb[D_]])
                    yield
                    B.CP(PT[0], v3(pb[D_]), [pkb[D_]], [K("PT0")], eng="act")
                    yield
                    B.TT(RT[0], PT[0], bc4(I32), ALU.add, [K("PT0"), "cst"], [K("RT0")], eng=POOLE)
                    NFP = 6
                    cur = 0
                    cb = 0
                    for it in range(6):
                        fp = it < NFP
                        nx = 1 - cur
                        nb2_ = 1 - cb
                        if fp:
                            Pin, PTin, RTin = P[cur], PT[cur], RT[cur]
                            kP, kPT, kRT = K(f"P{cur}"), K(f"PT{cur}"), K(f"RT{cur}")
                        else:
                            Pin, PTin, RTin = Pb[cb], PTb[cb], RTb[cb]
                            kP, kPT, kRT = K(f"Pb{cb}"), K(f"PTb{cb}"), K(f"RTb{cb}")
                        for h in range(4):
                            B.MM(pb[A_][:, h * 128:(h + 1) * 128], PTin[:, h, :], Pin[:, h, :], [kPT, kP], [pkb[A_]])
                        if it < 5:
                            for h in range(4):
                                B.MM(pb[B_][:, h * 128:(h + 1) * 128], Pin[:, h, :], PTin[:, h, :], [kPT, kP], [pkb[B_]])
                        yield
                        if it < NFP - 1:
                            Pl, kPl = P[nx], K(f"P{nx}")
                            B.CP(P[nx], v3(pb[A_]), [pkb[A_]], [K(f"P{nx}")], eng="act")
                            B.CP(PT[nx], v3(pb[B_]), [pkb[B_]], [K(f"PT{nx}")], eng="dve")
                        elif it == NFP - 1:
                            Pl, kPl = P[nx], K(f"P{nx}")
                            B.CP(P[nx], v3(pb[A_]), [pkb[A_]], [K(f"P{nx}")], eng="act")
                            if it < 5:
                                B.CP(Pb[0], v3(pb[A_]), [pkb[A_]], [K("Pb0")], eng="act")
                                B.CP(PTb[0], v3(pb[B_]), [pkb[B_]], [K("PTb0")], eng="dve")
                        else:
                            Pl, kPl = Pb[nb2_], K(f"Pb{nb2_}")
                            B.CP(Pb[nb2_], v3(pb[A_]), [pkb[A_]], [K(f"Pb{nb2_}")], eng="act")
                            if it < 5:
                                B.CP(PTb[nb2_], v3(pb[B_]), [pkb[B_]], [K(f"PTb{nb2_}")], eng="dve")
                        yield
                        for h in range(4):
                            B.MM(pb[C_][:, h * 128:(h + 1) * 128], Pl[:, h, :], RTin[:, h, :], [kPl, kRT], [pkb[C_]])
                        yield
                        if it < NFP - 1:
                            B.TT(RT[nx], RTin, v3(pb[C_]), ALU.add, [kRT, pkb[C_]], [K(f"RT{nx}")])
                            cur = nx
                        elif it == NFP - 1:
                            B.TT(RTb[0], RTin, v3(pb[C_]), ALU.add, [kRT, pkb[C_]], [K("RTb0")])
                            cb = 0
                        else:
                            B.TT(RTb[nb2_], RTin, v3(pb[C_]), ALU.add, [kRT, pkb[C_]], [K(f"RTb{nb2_}")])
                            cb = nb2_
                    RTfin, kRTfin = RTb[cb], K(f"RTb{cb}")
                    B.TT(MT, RTfin, bcl(beta[:, dr, n, :]), ALU.mult, [kRTfin, "beta"], [K("MT")], eng=POOLE)
                    B.TT(kg, ktc[s], bcl(EG[:, dr, n, :]), ALU.mult, [K(f"ktc{s}"), "EG"], [K("kg")], eng=POOLE)
                    B.TT(kdec, ktc[s], bcl(EKD[:, dr, n, :]), ALU.mult, [K(f"ktc{s}"), "EKD"], [K("kdec")], eng=POOLE)
                    B.STT(qd, qc[s], scale, EGr, ALU.mult, ALU.mult, [K(f"qc{s}"), K("EGr")], [K("qd")])
                    yield
                    for h in range(4):
                        B.MM(pb[A_][:, h * 128:(h + 1) * 128], MT[:, h, :], vtc[s][:, h, :], [K("MT"), K(f"vtc{s}")], [pkb[A_]])
                    for h in range(4):
                        B.MM(pb[B_][:, h * 128:(h + 1) * 128], kg[:, h, :], MT[:, h, :], [K("MT"), K("kg")], [pkb[B_]])
                    yield
                    B.CP(u_, v3(pb[A_]), [pkb[A_]], [K("u")], eng="act")
                    B.CP(wT, v3(pb[B_]), [pkb[B_]], [K("wT")], eng="act")
                    yield
                    for h in range(4):
                        B.MM(pb[C_][:, h * 128:(h + 1) * 128], wT[:, h, :], Sb[dr][:, h, :], [K("wT"), f"Sb{dr}"], [pkb[C_]])
                    yield
                    B.TT(vnew, u_, v3(pb[C_]), ALU.subtract, [K("u"), pkb[C_]], [K("vnew")])
                    yield
                    for h in range(4):
                        B.MM(pb[A_][:, h * 128:(h + 1) * 128], Sb[dr][:, h, :], qd[:, h, :], [K("qd"), f"Sb{dr}"], [pkb[A_]], start=True, stop=False)
                        B.MM(pb[A_][:, h * 128:(h + 1) * 128], vnew[:, h, :], QKDT[:, h, :], [K("vnew"), K("QKDT")], [pkb[A_]], start=False, stop=True)
                    for h in range(4):
                        B.MM(pb[B_][:, h * 128:(h + 1) * 128], kdec[:, h, :], vnew[:, h, :], [K("kdec"), K("vnew")], [pkb[B_]])
                    B.TT(Sst[dr], Sst[dr], bcl(EGL[:, dr, n, :]), ALU.mult, [f"S{dr}", "EGL"], [f"S{dr}"], eng=POOLE)
                    yield
                    B.TT(Sst[dr], Sst[dr], v3(pb[B_]), ALU.add, [f"S{dr}", pkb[B_]], [f"S{dr}"])
                    seg_end = (n % 2 == 1) if dr == 0 else (n % 2 == 0)
                    if seg_end:
                        B.DMA(f"d_od{dr}", od[n // 2, dr].rearrange("h d v -> d h v"), Sst[dr], [f"S{dr}"], [])
                        B.TS(Sst[dr], Sst[dr], knc[:, 0:1], ALU.mult, [f"S{dr}", "knc"], [f"S{dr}"])
                    B.CP(Sb[dr], Sst[dr], [f"S{dr}"], [f"Sb{dr}"], eng="act")
                    B.CP(ot, v3(pb[A_]), [pkb[A_]], [K("ot")], eng="act")
                    B.DMA(f"d_ot{X}", (oTf_d if dr == 0 else oTb_d)[n], ot, [K("ot")], [("oT", dr, n)])
                    yield

            gens = [dn_stream(0), dn_stream(1)]
            alive = [True, True]
            import os
            if os.environ.get("DN_SEQ"):
                ny = int(os.environ.get("DN_YIELDS", "100000"))
                for g in gens:
                    for i, _ in enumerate(g):
                        if i + 1 >= ny:
                            break
                alive = [False, False]
            while any(alive):
                for gi, g in enumerate(gens):
                    if alive[gi]:
                        try:
                            next(g)
                        except StopIteration:
                            alive[gi] = False
            S.barrier()
            self._aoff = off_comb
            ofc = [carve([128, 4, 128]) for _ in range(4)]
            obc = [carve([128, 4, 128]) for _ in range(4)]
            zc = [carve([128, 4, 128], BF16) for _ in range(4)]
            osq = [carve([128, 4, 128]) for _ in range(4)]
            for n in range(NCH):
                s = n % 4
                tsl = slice(n * 128, (n + 1) * 128)
                B.DMA(f"d_ofc{s}", ofc[s], oTf_d[n], [("oT", 0, n)], [f"ofc{s}"])
                B.DMA(f"d_obc{s}", obc[s], oTb_d[n], [("oT", 1, n)], [f"obc{s}"])
                B.DMA(f"d_zc{s}", zc[s], zs_d[:, :, tsl].rearrange("h d t -> d h t"), [("zs", h) for h in range(4)], [f"zc{s}"])
                B.TT(ofc[s], ofc[s], obc[s], ALU.add, [f"ofc{s}", f"obc{s}"], [f"ofc{s}"])
                B.ACT(f3(osq[s]), f3(ofc[s]), AF.Square, [f"ofc{s}"], [f"osq{s}"])
                B.MM(ps[s], ONE32, f3(osq[s]), ["cst", f"osq{s}"], [pk[s]])
                B.ACT(f3(osq[s]), ps[s], AF.Sqrt, [pk[s], "epsc"], [f"osq{s}"], bias=epsc[:, 0:1], scale=1.0 / 128)
                B.RCP(osq[s], osq[s], [f"osq{s}"], [f"osq{s}"])
                B.TT(ofc[s], ofc[s], osq[s], ALU.mult, [f"ofc{s}", f"osq{s}"], [f"ofc{s}"])
                B.STT(mixT[:, 0:4, tsl], ofc[s], vB[:, 60:61], zc[s], ALU.mult, ALU.mult, [f"ofc{s}", "vB", f"zc{s}"], ["mixT"])

            if self.stop_after == "dn":
                return
            S.barrier()
            self._aoff = off_persist
            qr = carve([128, 4, T], BF16)
            kr = carve([128, 2, T + 256], BF16)
            vat = carve([128, 18, 2, 128], BF16)
            mb = carve([128, 18 * 8])
            B.DMA("d_mb", mb, maskb, (), ["mb"])
            cosT = carve([128, T])
            sinT = carve([128, T])
            B.DMA("d_cs", cosT, ropec, (), ["cosT"])
            B.DMA("d_cs", sinT, ropes, (), ["sinT"])
            a32_ = [carve([128, 512]) for _ in range(2)]
            asq_ = [carve([128, 512]) for _ in range(2)]
            ars_ = [carve([128, 512]) for _ in range(2)]
            arot_ = [carve([128, 512]) for _ in range(2)]
            kst_ = [carve([128, 4, 128]) for _ in range(2)]
            idx = 0
            for j in range(6):
                c0 = 2064 + j * 128
                nw = vB[:, 61:62] if j < 4 else vB[:, 62:63]

                def consq(p, pkk, tb, j=j, nw=nw):
                    q = tb % 2
                    a32, asq, ars, arot, kst = a32_[q], asq_[q], ars_[q], arot_[q], kst_[q]
                    ka, ks, kr_, ko, kk = f"a32{q}", f"asq{q}", f"ars{q}", f"arot{q}", f"kst{q}"
                    tsl = slice(tb * 512, (tb + 1) * 512)
                    B.ACT(asq, p, AF.Square, [pkk], [ks])
                    B.MM(ps[2 + q], ONE32, asq, ["cst", ks], [pk[2 + q]])
                    B.ACT(ars, ps[2 + q], AF.Sqrt, [pk[2 + q], "epsc"], [kr_], bias=epsc[:, 0:1], scale=1.0 / 128)
                    B.RCP(ars, ars, [kr_], [kr_])
                    B.STT(a32, p, nw, ars, ALU.mult, ALU.mult, [pkk, "vB", kr_], [ka])
                    if j >= 4:
                        for k in range(4):
                            B.TR(ps[6 + q][:, k * 128:(k + 1) * 128], a32[:, k * 128:(k + 1) * 128], I32, [ka, "cst"], [pk[6 + q]])
                        B.CP(kst, v3(ps[6 + q]), [pk[6 + q]], [kk], eng="act")
                        B.DMA(f"d_kst{q}", ok.rearrange("(n p) f -> p n f", p=128)[:, tb * 4:(tb + 1) * 4, (j - 4) * 128:(j - 3) * 128],
                              kst, [kk], [])
                    B.MM(ps[4 + q], cst[:, C_ROT:C_ROT + 128], a32, ["cst", ka], [pk[4 + q]])
                    B.TT(arot, ps[4 + q], sinT[:, tsl], ALU.mult, [pk[4 + q], "sinT"], [ko])
                    B.TT(a32, a32, cosT[:, tsl], ALU.mult, [ka, "cosT"], [ka])
                    dst = qr[:, j, tsl] if j < 4 else kr[:, j - 4, tsl]
                    B.TT(dst, a32, arot, ALU.add, [ka, ko], ["qr" if j < 4 else "kr"])
                proj_fm(W, c0, 128, wsl, idx, consq)
                idx += 1
            wv = carve([128, 8, 256], BF16)
            vst32 = [carve([128, 256]) for _ in range(2)]

            def consv(p, pkk, tt):
                q = tt % 2
                B.CP(vst32[q], p[:, 0:256], [pkk], [f"vst{q}"], eng="act")
                B.CP(vat[:, tt].rearrange("p a b -> p (a b)"), vst32[q], [f"vst{q}"], ["vat"])
                B.DMA(f"d_vst{q}", ov[tt * 128:(tt + 1) * 128, :], vst32[q], [f"vst{q}"], [])
            proj_tm(W, 2064 + 768, 256, wv, "wv", consv, pbase=4)
            ckt = carve([128, 2, 256])
            B.DMA("d_ck", ckt, ck.rearrange("(n p) f -> p n f", p=128), (), ["ckt"])
            for kv in range(2):
                for n2 in range(2):
                    B.TR(ps[3][:, (kv * 2 + n2) * 128:(kv * 2 + n2 + 1) * 128], ckt[:, n2, kv * 128:(kv + 1) * 128], I32, ["ckt", "cst"], [pk[3]])
            B.CP(kr[:, :, T:T + 256], ps[3].rearrange("p (a b) -> p a b", a=2), [pk[3]], ["kr"])
            B.DMA("d_cv", vat[:, 16:18].rearrange("p n a b -> p n (a b)"), cv.rearrange("(n p) f -> p n f", p=128), (), ["vat"], queue="pool")
            pT = [carve([128, 512], BF16) for _ in range(4)]
            rden = [carve([128, 512]) for _ in range(2)]
            asc = 128.0 ** -0.5
            iters = [(kv, qb, kt) for kv in range(2) for qb in range(8) for kt in range(18)]

            def emit_scores(it):
                kv, qb, kt = iters[it]
                pb = it % 3
                qsl = slice(qb * 256, (qb + 1) * 256)
                for g in range(2):
                    B.MM(ps[pb][:, g * 256:(g + 1) * 256], kr[:, kv, kt * 128:(kt + 1) * 128], qr[:, kv * 2 + g, qsl],
                         ["kr", "qr"], [pk[pb]])

            import os as _os
            AHEAD = int(_os.environ.get('ATT_AHEAD', '1'))
            for it in range(min(AHEAD, len(iters))):
                emit_scores(it)
            for it, (kv, qb, kt) in enumerate(iters):
                if it + AHEAD < len(iters):
                    emit_scores(it + AHEAD)
                pb = it % 3
                r4 = it % 4
                acc = (kv * 8 + qb) % 2
                pn, pd = 4 + 2 * acc, 5 + 2 * acc
                qsl = slice(qb * 256, (qb + 1) * 256)
                B.ACT(pT[r4], ps[pb], AF.Exp, [pk[pb], "mb"], [f"pT{r4}"], bias=mb[:, kt * 8 + qb:kt * 8 + qb + 1], scale=asc)
                B.MM(ps[pn], vat[:, kt, kv, :], pT[r4], ["vat", f"pT{r4}"], [pk[pn]], start=(kt == 0), stop=(kt == 17))
                B.MM(ps[pd], ONEb, pT[r4], ["cstb", f"pT{r4}"], [pk[pd]], start=(kt == 0), stop=(kt == 17))
                if kt == 17:
                    B.RCP(rden[acc], ps[pd], [pk[pd]], [f"rden{acc}"])
                    for g in range(2):
                        B.TT(mixT[:, 4 + kv * 2 + g, qsl], ps[pn][:, g * 256:(g + 1) * 256], rden[acc][:, g * 256:(g + 1) * 256], ALU.mult,
                             [pk[pn], f"rden{acc}"], ["mixT"])
            if self.stop_after == "attn":
                return
            S.barrier()
            self._aoff = off_persist
            wo = carve([128, 8, 1024], BF16)
            B.DMAS("d_wo", [(wo[:, c, :], ab_out[0][c * 128:(c + 1) * 128, :]) for c in range(8)], (), ["wbuf"], queue="pool")
            res_proj_tb(8, mixT, "mixT", mod(0, 2), wo, (0, 1))

        def mixer_c():
            new_phase()
            W = ml_in[0]
            mqT_d = B.scratch("mqT_d", [8, 64, T], BF16)
            mkT_d = B.scratch("mkT_d", [8, 64, T], BF16)
            mk_d = B.scratch("mk_d", [NCH, 128, 8, 64], BF16)
            mv_d = B.scratch("mv_d", [NCH, 128, 8, 128], BF16)
            mo_d = B.scratch("mo_d", [NCH, 128, 8, 128], BF16)
            hf_d = B.scratch("hf_d", [NCH, 128, 8, 128], F32)
            mixT = carve([128, 8, T], BF16)
            wsl = [carve([128, 8, 128], BF16) for _ in range(2)]
            off_persist = self._aoff
            nb_ = carve([64, T], BF16)
            idx = 0
            for grp in range(2):
                for h in range(8):
                    def cons(p, pkk, tb, grp=grp):
                        if grp == 0:
                            B.ACT(nb_[:, tb * 512:(tb + 1) * 512], p[0:64, :], AF.Copy, [pkk], ["nb"], scale=0.125)
                        else:
                            B.CP(nb_[:, tb * 512:(tb + 1) * 512], p[0:64, :], [pkk], ["nb"], eng="act")
                    proj_fm(W, grp * 512 + h * 64, 64, wsl, idx, cons)
                    idx += 1
                    B.DMA("d_mqk", (mqT_d if grp == 0 else mkT_d)[h], nb_, ["nb"], [("mq" if grp == 0 else "mk", h)])
            wbig = carve([128, 8, 512], BF16)
            st = [carve([128, 512], BF16) for _ in range(2)]
            jobs = [(512, mk_d, "p (h k) -> p h k", 64, 0, 8, "mktok", None),
                    (1024, mv_d, "p (h k) -> p h k", 128, 0, 4, "mvtok", None),
                    (1536, mv_d, "p (h k) -> p h k", 128, 4, 4, "mvtok", None),
                    (2048, mo_d, "p (h k) -> p h k", 128, 0, 4, "motok", AF.Sigmoid),
                    (2560, mo_d, "p (h k) -> p h k", 128, 4, 4, "motok", AF.Sigmoid)]
            for ji, (c0, dst, pat, kk, h0, nh, key, fn) in enumerate(jobs):
                def const(p, pkk, tt, dst=dst, kk=kk, h0=h0, nh=nh, key=key, fn=fn, ji=ji):
                    q = tt % 2
                    B.ACT(st[q], p, fn if fn is not None else AF.Copy, [pkk], [f"st{q}"])
                    B.DMA(f"d_st{q}", dst[tt][:, h0:h0 + nh, :], st[q].rearrange("p (h k) -> p h k", k=kk), [f"st{q}"], [(key, tt, ji)])
                proj_tm(W, c0, 512, wbig, "wbig", const)
            wif = carve([128, 8, 32], BF16)
            gif = carve([128, NCH, 32])
            b1 = carve([128, 32])
            B.DMA("d_b1", b1, bc1.partition_broadcast(128), (), ["b1"])

            def consif(p, pkk, tt):
                B.CP(gif[:, tt, :], p[:, 0:32], [pkk], ["gif"])
            proj_tm(W, 3072, 32, wif, "wif", consif)
            li = carve([128, 2, NCH, 8])
            lf = carve([128, 2, NCH, 8])
            t16 = carve([128, NCH, 16])
            B.TT(t16, gif[:, :, 0:16], b1[:, 0:16].unsqueeze(1).to_broadcast([128, NCH, 16]), ALU.add, ["gif", "b1"], ["t16"])
            for dr in range(2):
                B.CP(li[:, dr], t16[:, :, dr * 8:(dr + 1) * 8], ["t16"], ["li"])
            B.TT(t16, gif[:, :, 16:32], b1[:, 16:32].unsqueeze(1).to_broadcast([128, NCH, 16]), ALU.add, ["gif", "b1"], ["t16"])
            B.ACT(t16, t16, AF.Exp, ["t16"], ["t16"], scale=-1.0)
            B.ACT(t16, t16, AF.Ln, ["t16", "cst"], ["t16"], bias=ONE32[:, 0:1])
            for dr in range(2):
                B.TS(lf[:, dr], t16[:, :, dr * 8:(dr + 1) * 8], -1.0, ALU.mult, ["t16"], ["lf"])
            bb = carve([128, 2, NCH, 8])
            bl = carve([128, 2, NCH, 8])
            aa = carve([128, 2, NCH, 8])
            lw = carve([128, 2, NCH, 8])
            mend = carve([128, 2, NCH, 8])
            g2 = lambda a, dr: a[:, dr].rearrange("p n h -> p (n h)")
            for dr in range(2):
                tri = cst[:, C_U:C_U + 128] if dr == 0 else cst[:, C_LO:C_LO + 128]
                B.MM(ps[2][:, 0:128], tri, g2(lf, dr), ["cst", "lf"], [pk[2]])
                B.CP(g2(bb, dr), ps[2][:, 0:128], [pk[2]], ["bb"])
                B.MM(ps[2][:, 0:128], ONE32, g2(lf, dr), ["cst", "lf"], [pk[2]])
                B.CP(g2(bl, dr), ps[2][:, 0:128], [pk[2]], ["bl"])
            f2 = lambda a: a.rearrange("p d n h -> p (d n h)")
            B.TT(f2(aa), f2(li), f2(bb), ALU.subtract, ["li", "bb"], ["aa"])
            B.TT(f2(lw), f2(aa), f2(bl), ALU.add, ["aa", "bl"], ["lw"])
            mx = carve([128, 1])
            mxb = carve([128, 128])
            for dr in range(2):
                B.TR(ps[2][:, 0:128], g2(lw, dr), I32, ["lw", "cst"], [pk[2]])
                B.RED(mx, ps[2][:, 0:128], ALU.max, [pk[2]], ["mx"])
                B.CP(mxb, mx[:, 0:1].to_broadcast([128, 128]), ["mx"], ["mxb"])
                B.TR(ps[2][:, 0:128], mxb, I32, ["mxb", "cst"], [pk[2]])
                B.CP(g2(mend, dr), ps[2][:, 0:128], [pk[2]], ["mend"])

            Cst = [carve([64, 8, 128]) for _ in range(2)]
            Cb = [carve([64, 8, 128], BF16) for _ in range(2)]
            nst = [carve([64, 8]) for _ in range(2)]
            nbf = [carve([64, 8], BF16) for _ in range(2)]
            mst = [carve([128, 8]) for _ in range(2)]
            for dr in range(2):
                B.DMA("d_c0", Cst[dr], mC0[dr].rearrange("h k v -> k h v"), (), [f"C{dr}"])
                B.DMA("d_c0", nst[dr], mn0[dr].rearrange("h k -> k h"), (), [f"n{dr}"], allow_slow_non_contiguous=True)
                B.DMA("d_c0", mst[dr], mm0[dr].partition_broadcast(128), (), [f"m{dr}"])
                B.CP(Cb[dr], Cst[dr], [f"C{dr}"], [f"Cb{dr}"])
                B.CP(nbf[dr], nst[dr], [f"n{dr}"], [f"nb{dr}"])
            nwb = carve([128, 128])
            B.DMA("d_b1", nwb, mlnw.partition_broadcast(128), (), ["nwb"])
            hb_d = B.scratch("hb_d", [NCH, 128, 8, 128], F32)
            off_streams = self._aoff
            f3 = lambda a: a.rearrange("p h t -> p (h t)")

            import os as _os3
            POOLM = "pool" if _os3.environ.get("USE_POOL", "1") == "1" else "dve"

            def ml_stream(dr):
                X = f"y{dr}"
                K = lambda nm: X + nm
                pb = [ps[4 * dr + i] for i in range(4)]
                pkb = [pk[4 * dr + i] for i in range(4)]
                qc = [carve([64, 8, 128], BF16) for _ in range(2)]
                kc_ = [carve([64, 8, 128], BF16) for _ in range(2)]
                ktc = [carve([128, 8, 64], BF16) for _ in range(2)]
                vtc = [carve([128, 8, 128], BF16) for _ in range(2)]
                dg = carve([128, 4, 128])
                LD = carve([128, 8, 128])
                s32 = carve([128, 8, 128])
                sbf = carve([128, 8, 128], BF16)
                sT = carve([128, 8, 128], BF16)
                num = carve([128, 8, 128])
                hh = carve([128, 8, 128])
                wk = carve([128, 8, 64], BF16)
                sm = {k: carve([128, 8]) for k in ["mintra", "min", "mt", "winter", "emt", "rsum", "den", "ew", "mnew", "astate", "neg", "qn"]}
                order = range(NCH) if dr == 0 else range(NCH - 1, -1, -1)
                NEGi = cst[:, C_NLI:C_NLI + 128] if dr == 0 else cst[:, C_NUI:C_NUI + 128]
                for vi, n in enumerate(order):
                    s = vi % 2
                    B.DMA(f"d_qc{s}{X}", qc[s], mqT_d[:, :, n * 128:(n + 1) * 128].rearrange("h d t -> d h t"), [("mq", h) for h in range(8)], [K(f"qc{s}")])
                    B.DMA(f"d_kc{s}{X}", kc_[s], mkT_d[:, :, n * 128:(n + 1) * 128].rearrange("h d t -> d h t"), [("mk", h) for h in range(8)], [K(f"kc{s}")])
                    B.DMA(f"d_ktc{s}{X}", ktc[s], mk_d[n], [("mktok", n, 0)], [K(f"ktc{s}")])
                    B.DMA(f"d_vtc{s}{X}", vtc[s], mv_d[n], [("mvtok", n, 1), ("mvtok", n, 2)], [K(f"vtc{s}")])
                    bn = bb[:, dr, n, :]
                    for h in range(8):
                        B.MM(pb[h // 4][:, (h % 4) * 128:(h % 4 + 1) * 128], qc[s][:, h, :], kc_[s][:, h, :], [K(f"qc{s}"), K(f"kc{s}")], [pkb[h // 4]])
                    for hf in range(2):
                        B.TT(dg, bc4(I32), bcl(aa[:, dr, n, hf * 4:(hf + 1) * 4]), ALU.mult, ["cst", "aa"], [K("dg")], eng=POOLM)
                        yield
                        B.MM(pb[2 + hf], ONE32, f3(dg), ["cst", K("dg")], [pkb[2 + hf]])
                        yield
                        B.TT(LD[:, hf * 4:(hf + 1) * 4, :], v3(pb[2 + hf]), bc4(NEGi), ALU.add, [pkb[2 + hf], "cst"], [K("LD")])
                    B.TT(LD, LD, bcl(bn), ALU.add, [K("LD"), "bb"], [K("LD")], eng=POOLM)
                    B.RED(sm["mintra"], LD, ALU.max, [K("LD")], [K("mintra")])
                    B.TT(sm["min"], bn, mst[dr], ALU.add, ["bb", f"m{dr}"], [K("min")])
                    B.TT(sm["mt"], sm["min"], sm["mintra"], ALU.max, [K("min"), K("mintra")], [K("mt")])
                    B.TT(sm["winter"], sm["min"], sm["mt"], ALU.subtract, [K("min"), K("mt")], [K("winter")])
                    B.TT(LD, LD, bcl(sm["mt"]), ALU.subtract, [K("LD"), K("mt")], [K("LD")], eng=POOLM)
                    yield
                    B.ACT(sm["winter"], sm["winter"], AF.Exp, [K("winter")], [K("winter")])
                    B.ACT(sm["emt"], sm["mt"], AF.Exp, [K("mt")], [K("emt")], scale=-1.0)
                    B.ACT(f3(LD), f3(LD), AF.Exp, [K("LD")], [K("LD")])
                    yield
                    for hf in range(2):
                        B.TT(s32[:, hf * 4:(hf + 1) * 4, :], v3(pb[hf]), LD[:, hf * 4:(hf + 1) * 4, :], ALU.mult, [pkb[hf], K("LD")], [K("s32")])
                    B.RED(sm["rsum"], s32, ALU.add, [K("s32")], [K("rsum")])
                    yield
                    B.CP(sbf, s32, [K("s32")], [K("sbf")], eng="act")
                    for h in range(8):
                        B.MM(pb[0][:, h:h + 1], qc[s][:, h, :], nbf[dr][:, h:h + 1], [K(f"qc{s}"), f"nb{dr}"], [pkb[0]])
                    yield
                    B.TT(sm["qn"], pb[0][:, 0:8], sm["winter"], ALU.mult, [pkb[0], K("winter")], [K("qn")])
                    pbt = pb[2].bitcast(BF16)
                    for h in range(8):
                        B.TR(pbt[:, h * 128:(h + 1) * 128], sbf[:, h, :], Ib, [K("sbf"), "cstb"], [pkb[2]])
                    yield
                    B.CP(sT, pbt.rearrange("p (a b) -> p a b", a=8), [pkb[2]], [K("sT")], eng="act")
                    yield
                    for h in range(8):
                        B.MM(pb[h // 4][:, (h % 4) * 128:(h % 4 + 1) * 128], sT[:, h, :], vtc[s][:, h, :], [K("sT"), K(f"vtc{s}")], [pkb[h // 4]])
                    for h in range(8):
                        B.MM(pb[2 + h // 4][:, (h % 4) * 128:(h % 4 + 1) * 128], qc[s][:, h, :], Cb[dr][:, h, :], [K(f"qc{s}"), f"Cb{dr}"], [pkb[2 + h // 4]])
                    B.TT(sm["mnew"], bl[:, dr, n, :], mst[dr], ALU.add, ["bl", f"m{dr}"], [K("mnew")])
                    B.TT(sm["astate"], sm["mnew"], sm["mnew"], ALU.max, [K("mnew")], [K("astate")])
                    B.TT(sm["mnew"], sm["mnew"], mend[:, dr, n, :], ALU.max, [K("mnew"), "mend"], [K("mnew")])
                    B.TT(sm["astate"], sm["astate"], sm["mnew"], ALU.subtract, [K("astate"), K("mnew")], [K("astate")])
                    B.TT(sm["ew"], lw[:, dr, n, :], sm["mnew"], ALU.subtract, ["lw", K("mnew")], [K("ew")])
                    yield
                    B.ACT(sm["astate"], sm["astate"], AF.Exp, [K("astate")], [K("astate")])
                    B.ACT(sm["ew"], sm["ew"], AF.Exp, [K("ew")], [K("ew")])
                    yield
                    for hf in range(2):
                        hs_ = slice(hf * 4, (hf + 1) * 4)
                        B.TT(num[:, hs_, :], v3(pb[2 + hf]), bcl(sm["winter"][:, hs_]), ALU.mult, [pkb[2 + hf], K("winter")], [K("num")])
                        B.TT(num[:, hs_, :], num[:, hs_, :], v3(pb[hf]), ALU.add, [K("num"), pkb[hf]], [K("num")])
                    B.TT(sm["den"], sm["qn"], sm["rsum"], ALU.add, [K("qn"), K("rsum")], [K("den")])
                    B.TS(sm["neg"], sm["den"], -1.0, ALU.mult, [K("den")], [K("neg")])
                    B.TT(sm["den"], sm["den"], sm["neg"], ALU.max, [K("den"), K("neg")], [K("den")])
                    B.TT(sm["den"], sm["den"], sm["emt"], ALU.max, [K("den"), K("emt")], [K("den")])
                    B.RCP(sm["den"], sm["den"], [K("den")], [K("den")])
                    B.TT(hh, num, bcl(sm["den"]), ALU.mult, [K("num"), K("den")], [K("hh")], eng=POOLM)
                    B.DMA(f"d_hh{X}", (hf_d if dr == 0 else hb_d)[n], hh, [K("hh")], [("hh", dr, n)])
                    B.TT(wk, ktc[s], bcl(sm["ew"], 64), ALU.mult, [K(f"ktc{s}"), K("ew")], [K("wk")], eng=POOLM)
                    yield
                    for h in range(8):
                        B.MM(pb[h // 4][0:64, (h % 4) * 128:(h % 4 + 1) * 128], wk[:, h, :], vtc[s][:, h, :], [K("wk"), K(f"vtc{s}")], [pkb[h // 4]])
                    for h in range(8):
                        B.MM(pb[2][0:64, h:h + 1], wk[:, h, :], ONEb[:, 0:1], [K("wk"), "cstb"], [pkb[2]])
                    B.TT(Cst[dr], Cst[dr], bcl(sm["astate"][0:64, :]), ALU.mult, [f"C{dr}", K("astate")], [f"C{dr}"], eng=POOLM)
                    B.TT(nst[dr], nst[dr], sm["astate"][0:64, :], ALU.mult, [f"n{dr}", K("astate")], [f"n{dr}"])
                    yield
                    for hf in range(2):
                        hs_ = slice(hf * 4, (hf + 1) * 4)
                        B.TT(Cst[dr][:, hs_, :], Cst[dr][:, hs_, :], v3(pb[hf][0:64, :]), ALU.add, [f"C{dr}", pkb[hf]], [f"C{dr}"])
                    B.TT(nst[dr], nst[dr], pb[2][0:64, 0:8], ALU.add, [f"n{dr}", pkb[2]], [f"n{dr}"])
                    B.CP(mst[dr], sm["mnew"], [K("mnew")], [f"m{dr}"])
                    seg_end = (n % 2 == 1) if dr == 0 else (n % 2 == 0)
                    if seg_end:
                        sg_ = n // 2
                        B.DMA(f"d_oC{dr}", oC[sg_, dr].rearrange("h k v -> k h v"), Cst[dr], [f"C{dr}"], [])
                        B.DMA(f"d_on{dr}", on[sg_, dr].rearrange("h k -> k h"), nst[dr], [f"n{dr}"], [], allow_slow_non_contiguous=True)
                        B.DMA(f"d_om{dr}", om[sg_, dr:dr + 1, :], mst[dr][0:1, :], [f"m{dr}"], [])
                        B.TS(Cst[dr], Cst[dr], knc[0:64, 0:1], ALU.mult, [f"C{dr}", "knc"], [f"C{dr}"])
                        B.TS(nst[dr], nst[dr], knc[0:64, 0:1], ALU.mult, [f"n{dr}", "knc"], [f"n{dr}"])
                        B.TS(mst[dr], mst[dr], knc[:, 0:1], ALU.mult, [f"m{dr}", "knc"], [f"m{dr}"])
                    B.CP(Cb[dr], Cst[dr], [f"C{dr}"], [f"Cb{dr}"], eng="act")
                    B.CP(nbf[dr], nst[dr], [f"n{dr}"], [f"nb{dr}"], eng="act")
                    yield

            gens = [ml_stream(0), ml_stream(1)]
            alive = [True, True]
            while any(alive):
                for gi, g in enumerate(gens):
                    if alive[gi]:
                        try:
                            next(g)
                        except StopIteration:
                            alive[gi] = False
            S.barrier()
            self._aoff = off_streams
            NR = 4
            hfc = [carve([128, 8, 128]) for _ in range(NR)]
            hbc = [carve([128, 8, 128]) for _ in range(NR)]
            oc = [carve([128, 8, 128], BF16) for _ in range(NR)]
            sq2 = [carve([128, 8, 128]) for _ in range(2)]
            gb = [carve([128, 8, 128], BF16) for _ in range(NR)]
            ss = [carve([128, 8]) for _ in range(NR)]
            for n in range(NCH):
                s = n % NR
                q2 = n % 2
                tsl = slice(n * 128, (n + 1) * 128)
                B.DMA(f"d_hfc{s}", hfc[s], hf_d[n], (), [f"hfc{s}"])
                B.DMA(f"d_hbc{s}", hbc[s], hb_d[n], (), [f"hbc{s}"])
                B.DMA(f"d_oc{s}", oc[s], mo_d[n], (), [f"oc{s}"])
                B.TT(hfc[s], hfc[s], hbc[s], ALU.add, [f"hfc{s}", f"hbc{s}"], [f"hfc{s}"])
                B.ACT(f3(sq2[q2]), f3(hfc[s]), AF.Square, [f"hfc{s}"], [f"sq2{q2}"])
                B.RED(ss[s], sq2[q2], ALU.add, [f"sq2{q2}"], [f"ss{s}"])
                B.ACT(ss[s], ss[s], AF.Sqrt, [f"ss{s}", "epsc"], [f"ss{s}"], bias=epsc[:, 0:1], scale=1.0 / 128)
                B.RCP(ss[s], ss[s], [f"ss{s}"], [f"ss{s}"])
                B.TT(hfc[s], hfc[s], bcl(ss[s]), ALU.mult, [f"hfc{s}", f"ss{s}"], [f"hfc{s}"])
                B.TT(hfc[s], hfc[s], nwb.unsqueeze(1).to_broadcast([128, 8, 128]), ALU.mult, [f"hfc{s}", "nwb"], [f"hfc{s}"])
                B.TT(gb[s], hfc[s], oc[s], ALU.mult, [f"hfc{s}", f"oc{s}"], [f"gb{s}"])
                for hf in range(2):
                    pbi = 2 * s + hf
                    pbt = ps[pbi].bitcast(BF16)
                    for k in range(4):
                        h = hf * 4 + k
                        B.TR(pbt[:, k * 128:(k + 1) * 128], gb[s][:, h, :], Ib, [f"gb{s}", "cstb"], [pk[pbi]])
                    B.CP(mixT[:, hf * 4:(hf + 1) * 4, tsl], v3(pbt[:, 0:512]), [pk[pbi]], ["mixT"], eng="act")
            S.barrier()
            self._aoff = off_persist
            wo = carve([128, 8, 1024], BF16)
            B.DMAS("d_wo", [(wo[:, c, :], ml_out[0][c * 128:(c + 1) * 128, :]) for c in range(8)], (), ["wbuf"], queue="pool")
            res_proj_tb(8, mixT, "mixT", mod(1, 2), wo, (1, 1))

        def run():
            ada_phase(0)
            norm_phase(0, 0)
            mixer_ab()
            if self.stop_after in ("dn_pre", "dn", "attn", "mix0"):
                return
            ffn_phase(0)
            if self.stop_after == "l0":
                return
            norm_phase(1, 0)
            mixer_c()
            if self.stop_after == "mix1":
                return
            ffn_phase(1)
        run()
        if self.stop_after is not None:
            S.barrier()
            self._aoff = 0
            dbg = carve([128, 8, 512])
            for tb in range(4):
                B.DMA("d_dbg", dbg, xT_d[:, :, tb * 512:(tb + 1) * 512].rearrange("c p t -> p c t"), (), ["dbg"])
                B.DMA("d_dbg2", self.dbg_out(tb), dbg, ["dbg"], [])
        S.wait_all_dma("sp")
        with contextlib.ExitStack() as es:
            sems = {}
            for k in list(S.ENGS) + list(S.dma_cum.keys()):
                sems[k] = es.enter_context(nc.semaphore(str(k)))
            block = es.enter_context(nc.Block())
            S.emit(block, sems)
        return nc

    def dbg_out(self, tb):
        yv = self.dout["y"].rearrange("(c p q) d -> c p (q d)", c=8, p=128)
        return yv[:, :, tb * 512:(tb + 1) * 512].rearrange("c p t -> p c t")


def rope_tables(sample):
    if not sample:
        return np.ones((128, T), np.float32), np.zeros((128, T), np.float32)
    t = np.arange(T)
    rows = (t // 64).astype(np.float32)
    cols = (t % 64).astype(np.float32)
    nf = 32
    inv = (10000.0 ** (-np.arange(nf, dtype=np.float32) / nf)).astype(np.float32)
    ang = np.zeros((128, T), np.float32)
    for p in range(128):
        pos = rows if p < 64 else cols
        ang[p] = pos * inv[p % 32]
    return np.cos(ang).astype(np.float32), np.sin(ang).astype(np.float32)


_CACHE = {}


def kernel(**inp):
    f = lambda k: np.ascontiguousarray(np.asarray(inp[k], dtype=np.float32))
    stop_after = inp.get("_stop_after", None)
    key = ("nc", stop_after)
    if key not in _CACHE:
        _CACHE[key] = Builder(stop_after).build()
    nc = _CACHE[key]
    consts = make_consts()
    xp, xs = f("x_prompt"), f("x_sample")
    shared = {k: f(k) for k in ["ada_w", "ffn_w_gate", "ffn_w_up", "ffn_w_down", "ab_w_in", "ab_w_out", "ml_w_in", "ml_w_out"]}
    ada_b, n1, n2 = f("ada_b"), f("norm1_w"), f("norm2_w")
    vecB = np.concatenate([f("dn_conv_w")[0].reshape(5 * 12, 128), f("dn_norm_w")[0][None], f("at_q_norm")[0][None],
                           f("at_k_norm")[0][None]], axis=0)
    bc0 = np.concatenate([f("dn_A_log")[0].reshape(8), f("dn_dt_bias")[0].reshape(8)])
    bc1 = np.concatenate([f("ml_i_bias")[0].reshape(16), f("ml_f_bias")[0].reshape(16)])
    in_maps = []
    for c in range(8):
        sample = c >= 4
        m = dict(shared)
        m["consts"] = consts
        cond = f("c")[c - 4] if sample else f("c_ctx")
        m["vecA"] = np.stack([np.concatenate([ada_b[l].reshape(48, 128), n1[l].reshape(8, 128), n2[l].reshape(8, 128),
                                              cond.reshape(8, 128)], axis=0) for l in range(2)])
        m["vecB"] = vecB
        m["bc0"], m["bc1"], m["mlnw"] = bc0, bc1, f("ml_norm_w")[0]
        m["knf"] = np.array([1.0 if sample else 0.0], np.float32)
        mb = np.zeros((128, 18, 8), np.float32)
        if not sample:
            mb[:] = NEG
            for qb in range(8):
                mb[:, 2 * qb:2 * qb + 2, qb] = 0.0
        m["maskb"] = mb.reshape(128, 144)
        m["ropec"], m["ropes"] = rope_tables(sample)
        if sample:
            b = c - 4
            m["xin"] = xs[b]
            m["ck"] = f("cache_attn_k")[b, 0].reshape(256, 256)
            m["cv"] = f("cache_attn_v")[b, 0].reshape(256, 256)
            m["sd0"] = f("state_delta")[b, 0]
            m["mC0"] = f("state_mlstm_C")[b, 0]
            m["mn0"] = f("state_mlstm_n")[b, 0]
            m["mm0"] = f("state_mlstm_m")[b, 0]
        else:
            m["xin"] = xp[8 * c:8 * c + 8].reshape(T, D)
            m["ck"] = np.zeros((256, 256), np.float32)
            m["cv"] = np.zeros((256, 256), np.float32)
            m["sd0"] = np.zeros((2, 4, 128, 128), np.float32)
            m["mC0"] = np.zeros((2, 8, 64, 128), np.float32)
            m["mn0"] = np.zeros((2, 8, 64), np.float32)
            m["mm0"] = np.zeros((2, 8), np.float32)
        in_maps.append({k: np.ascontiguousarray(v) for k, v in m.items()})
    res = run_bass_kernel_spmd(nc, in_maps, core_ids=list(range(8)))
    R = res.results
    if stop_after is not None:
        return R
    y_prompt = np.concatenate([R[c]["y"].reshape(8, 256, D) for c in range(4)], axis=0)
    y_sample = np.stack([R[c]["y"] for c in range(4, 8)], axis=0)
    nk = np.concatenate([R[c]["ok"].reshape(8, 1, 256, 2, 128) for c in range(4)], axis=0)
    nv = np.concatenate([R[c]["ov"].reshape(8, 1, 256, 2, 128) for c in range(4)], axis=0)
    nd = np.concatenate([R[c]["od"].reshape(8, 1, 2, 4, 128, 128) for c in range(4)], axis=0)
    nC = np.concatenate([R[c]["oC"].reshape(8, 1, 2, 8, 64, 128) for c in range(4)], axis=0)
    nn = np.concatenate([R[c]["on"].reshape(8, 1, 2, 8, 64) for c in range(4)], axis=0)
    nm = np.concatenate([R[c]["om"].reshape(8, 1, 2, 8) for c in range(4)], axis=0)
    return tuple(np.ascontiguousarray(a, dtype=np.float32) for a in (y_prompt, y_sample, nk, nv, nd, nC, nn, nm))
```

```python
import contextlib
import numpy as np
import concourse.bass as bass
import concourse.mybir as mybir
from concourse.bass_utils import run_bass_kernel_spmd

F32 = mybir.dt.float32
BF16 = mybir.dt.bfloat16
AF = mybir.ActivationFunctionType
ALU = mybir.AluOpType
AX = mybir.AxisListType

D = 1024
T = 2048
NCH = 16
FF = 2816
NFT = 22
AB_IN = 3088
ML_IN = 3104
EPS = 1e-6
NEG = -30000.0

SAME_ENGINE_SYNC = {"act": True, "dve": True, "pool": True, "pe": False, "sp": False}


class Sched:
    ENGS = ("pe", "act", "dve", "pool", "sp")

    def __init__(self, nc):
        self.nc = nc
        self.ops = {e: [] for e in self.ENGS}
        self.n = {e: 0 for e in self.ENGS}
        self.waited = {e: {} for e in self.ENGS}
        self.last_w = {}
        self.readers = {}
        self.dma_cum = {}
        self.needed = {e: set() for e in self.ENGS}
        self.final_waits = []
        self.final_eng = None
        self.fence = {}
        self.phys_map = {}

    def _deps(self, eng, reads, writes):
        deps = []
        for k in reads:
            t = self.last_w.get(k)
            if t is not None:
                deps.append(t)
        for k in writes:
            t = self.last_w.get(k)
            if t is not None:
                deps.append(t)
            deps.extend(self.readers.get(k, ()))
        best = {}
        for (sk, v) in deps:
            if sk == eng and not SAME_ENGINE_SYNC.get(eng, False):
                continue
            if self.waited[eng].get(sk, 0) >= v:
                continue
            best[sk] = max(best.get(sk, 0), v)
        for sk, v in best.items():
            self.waited[eng][sk] = v
            if sk in self.ENGS:
                self.needed[sk].add(v)
        return list(best.items())

    def _commit(self, tok, reads, writes):
        for k in writes:
            self.last_w[k] = tok
            self.readers[k] = []
        for k in reads:
            if k in writes:
                continue
            self.readers.setdefault(k, []).append(tok)

    def op(self, eng, fn, reads=(), writes=(), nofence=False):
        waits = self._deps(eng, reads, writes)
        self.n[eng] += 1
        tok = (eng, self.n[eng])
        self.ops[eng].append((waits, fn, ("self", self.n[eng], nofence)))
        self._commit(tok, reads, writes)
        return tok

    def dma(self, queue, semkey, items, reads=(), writes=()):
        pk_ = (queue, semkey)
        if pk_ not in self.phys_map:
            nq = sum(1 for q, _ in self.phys_map if q == queue)
            self.phys_map[pk_] = f"dma_{queue}{nq}"
        semkey = self.phys_map[pk_]
        waits = self._deps(queue, reads, writes)
        cum = self.dma_cum.get(semkey, 0)
        if cum > 0 and self.waited[queue].get(semkey, 0) < cum:
            self.waited[queue][semkey] = cum
            waits.append((semkey, cum))
        final = cum + 16 * len(items)
        self.dma_cum[semkey] = final
        for i, (o, a, kw) in enumerate(items):
            def fn(e, o=o, a=a, kw=kw):
                return e.dma_start(out=o, in_=a, **kw)
            self.ops[queue].append((waits if i == 0 else [], fn, ("dma", semkey)))
        tok = (semkey, final)
        self._commit(tok, reads, writes)
        return tok

    def barrier(self):
        for e in self.ENGS:
            waits = []
            for e2 in self.ENGS:
                if e2 == e or self.n[e2] == 0 or e2 == "sp":
                    continue
                if self.waited[e].get(e2, 0) < self.n[e2]:
                    waits.append((e2, self.n[e2]))
                    self.waited[e][e2] = self.n[e2]
                    self.needed[e2].add(self.n[e2])
            for sk, cum in self.dma_cum.items():
                if self.waited[e].get(sk, 0) < cum:
                    waits.append((sk, cum))
                    self.waited[e][sk] = cum
            if waits:
                self.ops[e].append((waits, None, None))
        self.last_w = {}
        self.readers = {}
        self.phys_map = {}

    def wait_all_dma(self, eng="sp"):
        self.final_waits = [(sk, v) for sk, v in self.dma_cum.items()]
        self.final_eng = eng

    def emit(self, block, sems):
        rank = {}
        for e in self.ENGS:
            rank[e] = {v: i + 1 for i, v in enumerate(sorted(self.needed[e]))}

        def val(sk, v):
            return rank[sk][v] if sk in self.ENGS else v

        def run(e, h):
            for waits, fn, inc in self.ops[e]:
                for sk, v in waits:
                    h.wait_ge(sems[sk], val(sk, v))
                if fn is None:
                    continue
                ins = fn(h)
                if inc[0] == "self":
                    if inc[1] in rank[e]:
                        if e in self.fence and not inc[2]:
                            ins = self.fence[e](h)
                        ins.then_inc(sems[e], 1)
                else:
                    ins.then_inc(sems[inc[1]], 16)
            if self.final_waits and self.final_eng == e:
                for sk, v in self.final_waits:
                    h.wait_ge(sems[sk], v)

        @block.tensor
        def _(t):
            run("pe", t)

        @block.scalar
        def _(s):
            run("act", s)

        @block.vector
        def _(v):
            run("dve", v)

        @block.gpsimd
        def _(g):
            run("pool", g)

        @block.sync
        def _(s):
            run("sp", s)


C_I, C_ONE, C_U, C_LO, C_NLI, C_NUI, C_NLS, C_NUS, C_ROT = [i * 128 for i in range(9)]
NCONST = 9 * 128


def make_consts():
    i = np.arange(128)[:, None]
    j = np.arange(128)[None, :]
    c = np.zeros((128, NCONST), np.float32)
    c[:, C_I:C_I + 128] = (i == j)
    c[:, C_ONE:C_ONE + 128] = 1.0
    c[:, C_U:C_U + 128] = (i <= j)
    c[:, C_LO:C_LO + 128] = (i >= j)
    c[:, C_NLI:C_NLI + 128] = np.where(i >= j, 0.0, NEG)
    c[:, C_NUI:C_NUI + 128] = np.where(i <= j, 0.0, NEG)
    c[:, C_NLS:C_NLS + 128] = np.where(i > j, 0.0, NEG)
    c[:, C_NUS:C_NUS + 128] = np.where(i < j, 0.0, NEG)
    R = np.zeros((128, 128), np.float32)
    for p in range(128):
        if (p % 64) < 32:
            R[p, p + 32] = -1.0
        else:
            R[p, p - 32] = 1.0
    c[:, C_ROT:C_ROT + 128] = R.T
    return c


class Builder:
    def __init__(self, stop_after=None):
        self.stop_after = stop_after
        nc = bass.Bass("TRN2", target_bir_lowering=False)
        self.nc = nc
        self.S = Sched(nc)
        self.din = {}
        self.dout = {}
        self._uid = 0

    def inp(self, name, shape):
        self.din[name] = self.nc.dram_tensor(name, list(shape), F32, kind="ExternalInput").ap()
        return self.din[name]

    def outp(self, name, shape):
        self.dout[name] = self.nc.dram_tensor(name, list(shape), F32, kind="ExternalOutput").ap()
        return self.dout[name]

    def scratch(self, name, shape, dt):
        return self.nc.dram_tensor(name, list(shape), dt).ap()

    def sb(self, name, shape, dt=F32):
        return self.nc.alloc_sbuf_tensor(name, list(shape), dt).ap()

    def MM(self, out, lhsT, rhs, r, w, start=True, stop=True):
        self.S.op("pe", lambda e: e.matmul(out, lhsT=lhsT, rhs=rhs, start=start, stop=stop), r, w)

    def TR(self, out, in_, ident, r, w):
        self.S.op("pe", lambda e: e.transpose(out, in_, ident), r, w)

    def ACT(self, out, in_, func, r, w, bias=None, scale=1.0, accum=None):
        kw = {}
        if bias is not None:
            kw["bias"] = bias
        if accum is not None:
            kw["accum_out"] = accum
        self.S.op("act", lambda e: e.activation(out=out, in_=in_, func=func, scale=scale, **kw), r, w)

    def TT(self, out, a, b, op, r, w, eng="dve"):
        self.S.op(eng, lambda e: e.tensor_tensor(out=out, in0=a, in1=b, op=op), r, w)

    def TS(self, out, a, s1, op0, r, w, s2=None, op1=None, eng="dve"):
        if op1 is None:
            self.S.op(eng, lambda e: e.tensor_scalar(out=out, in0=a, scalar1=s1, scalar2=None, op0=op0), r, w)
        else:
            self.S.op(eng, lambda e: e.tensor_scalar(out=out, in0=a, scalar1=s1, scalar2=s2, op0=op0, op1=op1), r, w)

    def STT(self, out, in0, scalar, in1, op0, op1, r, w, eng="dve"):
        self.S.op(eng, lambda e: e.scalar_tensor_tensor(out=out, in0=in0, scalar=scalar, in1=in1, op0=op0, op1=op1), r, w)

    def CP(self, out, in_, r, w, eng="dve"):
        if eng == "act":
            self.S.op(eng, lambda e: e.activation(out=out, in_=in_, func=AF.Copy), r, w)
        else:
            self.S.op(eng, lambda e: e.tensor_copy(out=out, in_=in_), r, w)

    def RED(self, out, in_, op, r, w):
        self.S.op("dve", lambda e: e.tensor_reduce(out=out, in_=in_, axis=AX.X, op=op), r, w)

    def RCP(self, out, in_, r, w):
        self.S.op("dve", lambda e: e.reciprocal(out=out, in_=in_), r, w)

    def MSET(self, ap, v, w, eng="dve"):
        self.S.op(eng, lambda e: e.memset(ap, v), (), w)

    def DMAS(self, semkey, pairs, r, w, queue="sp"):
        self.S.dma(queue, semkey, [(o, a, {}) for o, a in pairs], r, w)

    def DMA(self, semkey, out, in_, r, w, queue="sp", **kw):
        self.S.dma(queue, semkey, [(out, in_, kw)], r, w)

    def build(self):
        nc, S = self.nc, self.S
        B = self
        xin = B.inp("xin", [T, D])
        consts = B.inp("consts", [128, NCONST])
        vecA = B.inp("vecA", [2, 72, 128])
        vecB = B.inp("vecB", [63, 128])
        bc0 = B.inp("bc0", [16])
        bc1 = B.inp("bc1", [32])
        mlnw = B.inp("mlnw", [128])
        knf = B.inp("knf", [1])
        maskb = B.inp("maskb", [128, 18 * 8])
        ropec = B.inp("ropec", [128, T])
        ropes = B.inp("ropes", [128, T])
        ck = B.inp("ck", [256, 256])
        cv = B.inp("cv", [256, 256])
        sd0 = B.inp("sd0", [2, 4, 128, 128])
        mC0 = B.inp("mC0", [2, 8, 64, 128])
        mn0 = B.inp("mn0", [2, 8, 64])
        mm0 = B.inp("mm0", [2, 8])
        ada_w = B.inp("ada_w", [2, D, 6 * D])
        ffn_g = B.inp("ffn_w_gate", [2, D, FF])
        ffn_u = B.inp("ffn_w_up", [2, D, FF])
        ffn_d = B.inp("ffn_w_down", [2, FF, D])
        ab_in = B.inp("ab_w_in", [1, D, AB_IN])
        ab_out = B.inp("ab_w_out", [1, D, D])
        ml_in = B.inp("ml_w_in", [1, D, ML_IN])
        ml_out = B.inp("ml_w_out", [1, D, D])

        y = B.outp("y", [T, D])
        ok = B.outp("ok", [T, 256])
        ov = B.outp("ov", [T, 256])
        od = B.outp("od", [8, 2, 4, 128, 128])
        oC = B.outp("oC", [8, 2, 8, 64, 128])
        on = B.outp("on", [8, 2, 8, 64])
        om = B.outp("om", [8, 2, 8])

        xT_d = B.scratch("xT_d", [8, 128, T], F32)

        ps = [nc.alloc_psum_tensor(f"ps{i}", [128, 512], F32).ap() for i in range(8)]
        pk = [f"ps{i}" for i in range(8)]

        cst = B.sb("cst", [128, NCONST])
        cstb = B.sb("cstb", [128, NCONST], BF16)
        epsc = B.sb("epsc", [128, 1])
        knc = B.sb("knc", [128, 1])
        hT = B.sb("hT", [128, 8, T], BF16)
        vA = B.sb("vA", [128, 2, 72])
        vB = B.sb("vB", [128, 63])
        modv = B.sb("modv", [128, 2, 48])
        AA = B.sb("AA", [128, 2, 2, 8])
        NFR = 64
        fsa = B.sb("fence_a", [128, 2 + NFR])
        fsv = B.sb("fence_v", [128, 2 + NFR])
        fcnt = {"act": 0, "dve": 0}

        def fence_act(e):
            fcnt["act"] += 1
            c = 2 + fcnt["act"] % NFR
            return e.activation(out=fsa[:, c:c + 1], in_=fsa[:, 0:1], func=AF.Copy)

        def fence_dve(e):
            fcnt["dve"] += 1
            c = 2 + fcnt["dve"] % NFR
            return e.tensor_copy(out=fsv[:, c:c + 1], in_=fsv[:, 0:1])
        USE_FENCE = False
        if USE_FENCE:
            S.fence["act"] = fence_act
            S.fence["dve"] = fence_dve
        arena = B.sb("arena", [128, 42500])
        self._aoff = 0

        def carve(shape, dt=F32):
            n = int(np.prod(shape[1:]))
            words = n if dt == F32 else (n + 1) // 2
            words = (words + 7) // 8 * 8
            v = arena[0:shape[0], self._aoff:self._aoff + words]
            self._aoff += words
            assert self._aoff <= 42500, self._aoff
            if dt != F32:
                v = v.bitcast(BF16)[:, 0:n]
            else:
                v = v[:, 0:n]
            if len(shape) == 2:
                return v
            names = " ".join(f"a{i}" for i in range(len(shape) - 1))
            kw = {f"a{i}": shape[i + 1] for i in range(len(shape) - 1)}
            return v.rearrange(f"p ({names}) -> p {names}", **kw)

        def new_phase():
            S.barrier()
            self._aoff = 0

        I32 = cst[:, C_I:C_I + 128]
        ONE32 = cst[:, C_ONE:C_ONE + 128]
        Ib = cstb[:, C_I:C_I + 128]
        ONEb = cstb[:, C_ONE:C_ONE + 128]

        def bc4(ap2d):
            return ap2d.unsqueeze(1).to_broadcast([128, 4, 128])

        def bcl(ap2d, n=128):
            return ap2d.unsqueeze(2).to_broadcast([ap2d.shape[0], ap2d.shape[1], n])

        def v3(ap, a=4):
            return ap.rearrange("p (a b) -> p a b", a=a)

        B.DMA("d_c", cst, consts, (), ["cst"])
        B.DMA("d_cb", cstb, consts, (), ["cstb"], queue="pool")
        S.op("dve", lambda e: e.memset(fsv, 0.0), (), ["fsv"], nofence=True)
        S.op("dve", lambda e: e.tensor_copy(out=fsv[:, 1:2], in_=fsv[:, 0:1]), ["fsv"], ["fsv1"], nofence=True)
        S.op("dve", lambda e: e.memset(epsc, EPS), (), ["epsc"], nofence=True)
        S.op("act", lambda e: e.activation(out=fsa, in_=epsc[:, 0:1].to_broadcast([128, 2 + NFR]), func=AF.Copy),
             ["epsc"], ["fsa"], nofence=True)
        S.op("act", lambda e: e.activation(out=fsa[:, 1:2], in_=fsa[:, 0:1], func=AF.Copy), ["fsa"], ["fsa1"], nofence=True)
        B.DMA("d_c", knc, knf.partition_broadcast(128), (), ["knc"])
        vst = carve([128, 128])
        for l in range(2):
            B.DMA("d_v", vst[0:72, :], vecA[l], (), ["vst"])
            B.TR(ps[0][:, 0:72], vst[0:72, :], cst[0:72, C_I:C_I + 72], ["vst", "cst"], [pk[0]])
            B.CP(vA[:, l, :], ps[0][:, 0:72], [pk[0]], ["vA"])
        B.DMA("d_v", vst[0:63, :], vecB, (), ["vst"])
        B.TR(ps[0][:, 0:63], vst[0:63, :], cst[0:63, C_I:C_I + 63], ["vst", "cst"], [pk[0]])
        B.CP(vB, ps[0][:, 0:63], [pk[0]], ["vB"])

        xs = [carve([128, D]) for _ in range(2)]
        xo = [carve([128, 8, 128]) for _ in range(2)]
        for tt in range(NCH):
            s = tt % 2
            B.DMA(f"d_xs{s}", xs[s], xin[tt * 128:(tt + 1) * 128, :], (), [f"xs{s}"])
            for c in range(8):
                b = c // 4
                B.TR(ps[b][:, (c % 4) * 128:(c % 4 + 1) * 128], xs[s][:, c * 128:(c + 1) * 128], I32,
                     [f"xs{s}", "cst"], [pk[b]])
            for b in range(2):
                B.CP(xo[s][:, b * 4:(b + 1) * 4, :], v3(ps[b]), [pk[b]], [f"xo{s}"], eng=("dve" if b == 0 else "act"))
            B.DMA(f"d_xo{s}", xT_d[:, :, tt * 128:(tt + 1) * 128].rearrange("c p t -> p c t"), xo[s],
                  [f"xo{s}"], [("xT", tt // 4)])

        def ada_gen(l, nslots, psb):
            scb = carve([128, 8], BF16)
            sc32 = carve([128, 8])
            B.ACT(sc32, vA[:, l, 64:72], AF.Silu, ["vA"], [f"sc32{l}"])
            B.CP(scb, sc32, [f"sc32{l}"], [f"scb{l}"])
            wa = [carve([128, 8, 512], BF16) for _ in range(nslots)]
            for g in range(12):
                s = g % nslots
                B.DMA(f"d_wa{l}{s}", wa[s], ada_w[l][:, g * 512:(g + 1) * 512].rearrange("(c p) n -> p c n", p=128),
                      (), [f"wa{l}{s}"], queue="pool")
                for j in range(4):
                    col = g * 4 + j
                    for kc in range(8):
                        B.MM(ps[psb][:, col:col + 1], wa[s][:, kc, j * 128:(j + 1) * 128], scb[:, kc:kc + 1],
                             [f"wa{l}{s}", f"scb{l}"], [pk[psb]], start=(kc == 0), stop=(kc == 7))
                yield
            B.TT(modv[:, l, :], ps[psb][:, 0:48], vA[:, l, 0:48], ALU.add, [pk[psb], "vA"], ["modv"])
            for i in range(2):
                sc = modv[:, l, (1 + 3 * i) * 8:(2 + 3 * i) * 8]
                B.STT(AA[:, l, i, :], sc, 1.0, vA[:, l, 48 + 8 * i:56 + 8 * i], ALU.add, ALU.mult, ["modv", "vA"], ["AA"])

        def ada_phase(l):
            for _ in ada_gen(l, 4, 2):
                pass

        def mod(l, j):
            return modv[:, l, j * 8:(j + 1) * 8]

        def norm_phase(l, i):
            new_phase()
            xt = [carve([128, 8, 512]) for _ in range(2)]
            sq = [carve([128, 512]) for _ in range(2)]
            rs = carve([128, 512])
            tmp = [carve([128, 512]) for _ in range(2)]
            for tb in range(4):
                s = tb % 2
                B.DMA(f"d_xt{s}", xt[s], xT_d[:, :, tb * 512:(tb + 1) * 512].rearrange("c p t -> p c t"),
                      [("xT", tb)], [f"xt{s}"])
                for c in range(8):
                    q = c % 2
                    B.ACT(sq[q], xt[s][:, c, :], AF.Square, [f"xt{s}"], [f"sq{q}"])
                    B.MM(ps[3], ONE32, sq[q], ["cst", f"sq{q}"], [pk[3]], start=(c == 0), stop=(c == 7))
                B.ACT(rs, ps[3], AF.Sqrt, [pk[3], "epsc"], ["rs"], bias=epsc[:, 0:1], scale=1.0 / D)
                B.RCP(rs, rs, ["rs"], ["rs"])
                for c in range(8):
                    q = c % 2
                    B.STT(tmp[q], xt[s][:, c, :], AA[:, l, i, c:c + 1], rs, ALU.mult, ALU.mult,
                          [f"xt{s}", "AA", "rs"], [f"tmp{q}"])
                    B.ACT(hT[:, c, tb * 512:(tb + 1) * 512], tmp[q], AF.Identity, [f"tmp{q}", "modv"], ["hT"],
                          bias=mod(l, 3 * i)[:, c:c + 1])

        def res_proj(w_ap, KC, rhs, rkey, gate, wbuf, final, tok0, ntb):
            xr = [carve([128, 512]) for _ in range(2)]
            yo = [carve([128, 4, 128]) for _ in range(2)] if final else None
            cnt = 0
            for dt in range(8):
                for tb in range(ntb):
                    s = cnt % 2
                    cnt += 1
                    gtb = tok0 // 512 + tb
                    B.DMA(f"d_xr{s}", xr[s], xT_d[dt, :, gtb * 512:(gtb + 1) * 512], [("xT", dt, gtb)], [f"xr{s}"])
                    pb = 4 + s
                    for kc in range(KC):
                        B.MM(ps[pb], wbuf[:, kc, dt * 128:(dt + 1) * 128], rhs[:, kc, tb * 512:(tb + 1) * 512],
                             ["wbuf", rkey], [pk[pb]], start=(kc == 0), stop=(kc == KC - 1))
                    B.STT(xr[s], ps[pb], gate[:, dt:dt + 1], xr[s], ALU.mult, ALU.add, [pk[pb], "modv", f"xr{s}"], [f"xr{s}"])
                    if not final:
                        B.DMA(f"d_xw{s}", xT_d[dt, :, gtb * 512:(gtb + 1) * 512], xr[s], [f"xr{s}"], [("xT", dt, gtb)])
                    else:
                        pt = 6 + s
                        for k in range(4):
                            B.TR(ps[pt][:, k * 128:(k + 1) * 128], xr[s][:, k * 128:(k + 1) * 128], I32,
                                 [f"xr{s}", "cst"], [pk[pt]])
                        B.CP(yo[s], v3(ps[pt]), [pk[pt]], [f"yo{s}"], eng="act")
                        B.DMA(f"d_yo{s}", y.rearrange("(n p) f -> p n f", p=128)[:, gtb * 4:(gtb + 1) * 4, dt * 128:(dt + 1) * 128],
                              yo[s], [f"yo{s}"], [])

        def res_proj_tb(KC, rhs, rkey, gate, wbuf, norm):
            l, i = norm
            xr = [carve([128, 8, 512]) for _ in range(2)]
            sq = [carve([128, 512]) for _ in range(2)]
            rs = carve([128, 512])
            tmp = [carve([128, 512]) for _ in range(2)]
            cnt = 0
            for tb in range(4):
                s = tb % 2
                tsl = slice(tb * 512, (tb + 1) * 512)
                B.DMA(f"d_xr{s}", xr[s], xT_d[:, :, tsl].rearrange("c p t -> p c t"), (), [f"xr{s}"])
                for dt in range(8):
                    pb = 4 + cnt % 2
                    cnt += 1
                    for kc in range(KC):
                        B.MM(ps[pb], wbuf[:, kc, dt * 128:(dt + 1) * 128], rhs[:, kc, tsl],
                             ["wbuf", rkey], [pk[pb]], start=(kc == 0), stop=(kc == KC - 1))
                    B.STT(xr[s][:, dt, :], ps[pb], gate[:, dt:dt + 1], xr[s][:, dt, :], ALU.mult, ALU.add,
                          [pk[pb], "modv", f"xr{s}"], [f"xr{s}"])
                B.DMA(f"d_xw{s}", xT_d[:, :, tsl].rearrange("c p t -> p c t"), xr[s], [f"xr{s}"], [])
                for c in range(8):
                    q = c % 2
                    B.ACT(sq[q], xr[s][:, c, :], AF.Square, [f"xr{s}"], [f"sq{q}"])
                    B.MM(ps[3], ONE32, sq[q], ["cst", f"sq{q}"], [pk[3]], start=(c == 0), stop=(c == 7))
                B.ACT(rs, ps[3], AF.Sqrt, [pk[3], "epsc"], ["rs"], bias=epsc[:, 0:1], scale=1.0 / D)
                B.RCP(rs, rs, ["rs"], ["rs"])
                for c in range(8):
                    q = c % 2
                    B.STT(tmp[q], xr[s][:, c, :], AA[:, l, i, c:c + 1], rs, ALU.mult, ALU.mult,
                          [f"xr{s}", "AA", "rs"], [f"tmp{q}"])
                    B.ACT(hT[:, c, tsl], tmp[q], AF.Identity, [f"tmp{q}", "modv"], ["hT"],
                          bias=mod(l, 3 * i)[:, c:c + 1])

        def ffn_phase(l):
            new_phase()
            aT = carve([128, NFT, T], BF16)
            wd = carve([128, NFT, 1024], BF16)
            wg = [carve([128, 8, 256], BF16) for _ in range(2)]
            wu = [carve([128, 8, 256], BF16) for _ in range(2)]
            sg = [carve([128, 512]) for _ in range(2)]
            cnt = 0
            for g in range(11):
                s = g % 2
                B.DMA(f"d_wg{s}", wg[s], ffn_g[l][:, g * 256:(g + 1) * 256].rearrange("(c p) n -> p c n", p=128),
                      (), [f"wg{s}"], queue="pool")
                B.DMA(f"d_wu{s}", wu[s], ffn_u[l][:, g * 256:(g + 1) * 256].rearrange("(c p) n -> p c n", p=128),
                      (), [f"wu{s}"], queue="pool")
                if g == 1:
                    B.DMAS("d_wd", [(wd[:, f, :], ffn_d[l][f * 128:(f + 1) * 128, :]) for f in range(NFT)], (), ["wbuf"], queue="pool")
                for j in range(2):
                    f = g * 2 + j
                    for tb in range(4):
                        q = cnt % 2
                        cnt += 1
                        tsl = slice(tb * 512, (tb + 1) * 512)
                        for kc in range(8):
                            B.MM(ps[q], wg[s][:, kc, j * 128:(j + 1) * 128], hT[:, kc, tsl],
                                 [f"wg{s}", "hT"], [pk[q]], start=(kc == 0), stop=(kc == 7))
                        for kc in range(8):
                            B.MM(ps[2 + q], wu[s][:, kc, j * 128:(j + 1) * 128], hT[:, kc, tsl],
                                 [f"wu{s}", "hT"], [pk[2 + q]], start=(kc == 0), stop=(kc == 7))
                        B.ACT(sg[q], ps[q], AF.Silu, [pk[q]], [f"sg{q}"])
                        B.TT(aT[:, f, tsl], sg[q], ps[2 + q], ALU.mult, [f"sg{q}", pk[2 + q]], ["aT"])
            res_proj(None, NFT, aT, "aT", mod(l, 5), wd, final=(l == 1), tok0=0, ntb=4)

        def proj_fm(w_ap, c0, M, wslots, idx, consume):
            s = idx % 2
            wt = wslots[s]
            B.DMA(f"d_wt{s}", wt[:, :, 0:M], w_ap[:, c0:c0 + M].rearrange("(c p) n -> p c n", p=128), (), [f"wt{s}"], queue="pool")
            for tb in range(4):
                pb = (idx * 4 + tb) % 2
                for kc in range(8):
                    B.MM(ps[pb][0:M, :], wt[:, kc, 0:M], hT[:, kc, tb * 512:(tb + 1) * 512], [f"wt{s}", "hT"], [pk[pb]],
                         start=(kc == 0), stop=(kc == 7))
                consume(ps[pb], pk[pb], tb)

        def proj_tm(w_ap, c0, N, wbuf, wkey, consume, pbase=2):
            B.DMA("d_" + wkey, wbuf[:, :, 0:N], w_ap[:, c0:c0 + N].rearrange("(c p) n -> p c n", p=128), (), [wkey], queue="pool")
            for tt in range(NCH):
                pb = pbase + tt % 2
                for kc in range(8):
                    B.MM(ps[pb][:, 0:N], hT[:, kc, tt * 128:(tt + 1) * 128], wbuf[:, kc, 0:N], ["hT", wkey], [pk[pb]],
                         start=(kc == 0), stop=(kc == 7))
                consume(ps[pb], pk[pb], tt)

        def mixer_ab():
            new_phase()
            W = ab_in[0]
            qT_d = B.scratch("qT_d", [4, 128, T], BF16)
            kT_d = B.scratch("kT_d", [4, 128, T], BF16)
            zs_d = B.scratch("zs_d", [4, 128, T], BF16)
            ktok_d = B.scratch("ktok_d", [NCH, 128, 4, 128], BF16)
            vtok_d = B.scratch("vtok_d", [NCH, 128, 4, 128], BF16)
            oTf_d = B.scratch("oTf_d", [NCH, 128, 4, 128], F32)
            mixT = carve([128, 8, T], BF16)
            wsl = [carve([128, 8, 128], BF16) for _ in range(2)]
            off_persist = self._aoff

            wab = carve([128, 8, 16], BF16)
            ab = carve([128, NCH, 16])
            b0 = carve([128, 16])
            gg = carve([128, 2, NCH, 4])
            nbeta = carve([128, 2, NCH, 4])
            beta = carve([128, 2, NCH, 4])
            t8 = carve([128, NCH, 8])
            nA = carve([128, 8])
            gc = carve([128, 2, NCH, 4])
            gl = carve([128, 2, NCH, 4])
            EG = carve([128, 2, NCH, 4])
            EKD = carve([128, 2, NCH, 4])
            EGL = carve([128, 2, NCH, 4])
            off_rec = self._aoff
            ci2 = [carve([128, 8, 260]) for _ in range(3)]
            acc2 = [carve([128, 8, 256]) for _ in range(3)]
            sa2 = [carve([128, T]) for _ in range(3)]
            sqb = [carve([128, 512]) for _ in range(2)]
            rsb2 = [carve([128, 512]) for _ in range(3)]
            nb2 = [carve([128, T], BF16) for _ in range(3)]
            tok2 = [carve([128, NCH, 128], BF16) for _ in range(3)]
            for par in range(3):
                B.MSET(ci2[par], 0.0, [f"ci{par}"])
            idx = 0
            ag = ada_gen(1, 2, 6)
            for grp in range(3):
                for h in range(4):
                    next(ag, None)
                    ct = grp * 4 + h
                    par = idx % 3
                    ci, acc, sa, nb_, tok, rsb = ci2[par], acc2[par], sa2[par], nb2[par], tok2[par], rsb2[par]
                    kci, kacc, ksa, knb, ktk, krs = f"ci{par}", f"acc{par}", f"sa{par}", f"nb{par}", f"tok{par}", f"rsb{par}"
                    accf = acc.rearrange("p s t -> p (s t)")

                    def cons(p, pkk, tb, ci=ci, kci=kci):
                        B.CP(ci[:, 2 * tb:2 * tb + 2, 2:258], p.rearrange("p (s t) -> p s t", s=2), [pkk], [kci], eng="act")
                    proj_fm(W, ct * 128, 128, wsl, idx, cons)
                    idx += 1
                    B.TS(ci[:, 1:8, 0:2], ci[:, 0:7, 256:258], knc[:, 0:1], ALU.mult, [kci, "knc"], [kci])
                    B.TS(ci[:, 0:7, 258:260], ci[:, 1:8, 2:4], knc[:, 0:1], ALU.mult, [kci, "knc"], [kci])
                    B.TS(acc, ci[:, :, 0:256], vB[:, ct:ct + 1], ALU.mult, [kci, "vB"], [kacc])
                    for j in range(1, 5):
                        B.STT(acc, ci[:, :, j:j + 256], vB[:, j * 12 + ct:j * 12 + ct + 1], acc, ALU.mult, ALU.add,
                              [kci, "vB", kacc], [kacc])
                    B.ACT(sa, accf, AF.Silu, [kacc], [ksa])
                    if grp < 2:
                        for tb in range(4):
                            q = tb % 2
                            B.ACT(sqb[q], sa[:, tb * 512:(tb + 1) * 512], AF.Square, [ksa], [f"sqb{q}"])
                            B.MM(ps[2], ONE32, sqb[q], ["cst", f"sqb{q}"], [pk[2]])
                            B.ACT(rsb, ps[2], AF.Sqrt, [pk[2], "epsc"], [krs], bias=epsc[:, 0:1])
                            B.RCP(rsb, rsb, [krs], [krs])
                            B.TT(nb_[:, tb * 512:(tb + 1) * 512], sa[:, tb * 512:(tb + 1) * 512], rsb, ALU.mult, [ksa, krs], [knb])
                        B.DMA(f"d_qk{par}", (qT_d if grp == 0 else kT_d)[h], nb_, [knb], [("qT" if grp == 0 else "kT", h)])
                    else:
                        B.CP(nb_, sa, [ksa], [knb])
                    if grp >= 1:
                        for g4 in range(4):
                            pbt = ps[3].bitcast(BF16)
                            for k in range(4):
                                n = g4 * 4 + k
                                B.TR(pbt[:, k * 128:(k + 1) * 128], nb_[:, n * 128:(n + 1) * 128], Ib, [knb, "cstb"], [pk[3]])
                            B.CP(tok[:, g4 * 4:(g4 + 1) * 4, :], v3(pbt[:, 0:512]), [pk[3]], [ktk], eng="act")
                        dst = ktok_d if grp == 1 else vtok_d
                        B.DMA(f"d_tok{par}", dst[:, :, h, :].rearrange("n t d -> t n d"), tok, [ktk], [("ktok" if grp == 1 else "vtok", h)])
            for h in range(4):
                par = idx % 3
                nb_, knb = nb2[par], f"nb{par}"

                def consz(p, pkk, tb, nb_=nb_, knb=knb):
                    B.ACT(nb_[:, tb * 512:(tb + 1) * 512], p, AF.Silu, [pkk], [knb])
                proj_fm(W, 1536 + h * 128, 128, wsl, idx, consz)
                idx += 1
                B.DMA(f"d_qk{par}", zs_d[h], nb_, [knb], [("zs", h)])
            for _ in ag:
                pass
            B.DMA("d_b0", b0, bc0.partition_broadcast(128), (), ["b0"])

            def consab(p, pkk, tt):
                B.CP(ab[:, tt, :], p[:, 0:16], [pkk], ["ab"])
            proj_tm(W, 2048, 16, wab, "wab", consab)
            B.ACT(nA, b0[:, 0:8], AF.Exp, ["b0"], ["nA"])
            B.TS(nA, nA, -1.0, ALU.mult, ["nA"], ["nA"])
            B.TT(t8, ab[:, :, 0:8], b0[:, 8:16].unsqueeze(1).to_broadcast([128, NCH, 8]), ALU.add, ["ab", "b0"], ["t8"])
            B.ACT(t8, t8, AF.Exp, ["t8"], ["t8"])
            B.ACT(t8, t8, AF.Ln, ["t8", "cst"], ["t8"], bias=ONE32[:, 0:1])
            B.TT(t8, t8, nA.unsqueeze(1).to_broadcast([128, NCH, 8]), ALU.mult, ["t8", "nA"], ["t8"])
            for dr in range(2):
                B.CP(gg[:, dr], t8[:, :, dr * 4:(dr + 1) * 4], ["t8"], ["gg"])
            B.ACT(t8, ab[:, :, 8:16], AF.Sigmoid, ["ab"], ["t8"])
            for dr in range(2):
                B.CP(beta[:, dr], t8[:, :, dr * 4:(dr + 1) * 4], ["t8"], ["beta"])
                B.TS(nbeta[:, dr], t8[:, :, dr * 4:(dr + 1) * 4], -1.0, ALU.mult, ["t8"], ["nbeta"])
            for dr in range(2):
                tri = cst[:, C_U:C_U + 128] if dr == 0 else cst[:, C_LO:C_LO + 128]
                B.MM(ps[2][:, 0:64], tri, gg[:, dr].rearrange("p n h -> p (n h)"), ["cst", "gg"], [pk[2]])
                B.CP(gc[:, dr].rearrange("p n h -> p (n h)"), ps[2][:, 0:64], [pk[2]], ["gc"])
                B.MM(ps[2][:, 0:64], ONE32, gg[:, dr].rearrange("p n h -> p (n h)"), ["cst", "gg"], [pk[2]])
                B.CP(gl[:, dr].rearrange("p n h -> p (n h)"), ps[2][:, 0:64], [pk[2]], ["gl"])
            f2 = lambda a: a.rearrange("p d n h -> p (d n h)")
            B.ACT(f2(EG), f2(gc), AF.Exp, ["gc"], ["EG"])
            B.ACT(f2(EGL), f2(gl), AF.Exp, ["gl"], ["EGL"])
            B.TT(f2(EKD), f2(gl), f2(gc), ALU.subtract, ["gl", "gc"], ["EKD"])
            B.ACT(f2(EKD), f2(EKD), AF.Exp, ["EKD"], ["EKD"])

            if self.stop_after == "dn_pre":
                return
            S.barrier()
            self._aoff = off_rec
            oTb_d = B.scratch("oTb_d", [NCH, 128, 4, 128], F32)
            f3 = lambda a: a.rearrange("p h t -> p (h t)")
            scale = 128.0 ** -0.5
            Sst = [carve([128, 4, 128]) for _ in range(2)]
            Sb = [carve([128, 4, 128], BF16) for _ in range(2)]
            for dr in range(2):
                B.DMA("d_s0", Sst[dr], sd0[dr].rearrange("h d v -> d h v"), (), [f"S{dr}"])
                B.CP(Sb[dr], Sst[dr], [f"S{dr}"], [f"Sb{dr}"])

            import os as _os2
            R32 = (lambda a: a.bitcast(mybir.dt.float32r)) if _os2.environ.get("DN_F32R", "0") == "1" else (lambda a: a)

            off_comb = self._aoff

            POOLE = "pool" if _os2.environ.get("USE_POOL", "1") == "1" else "dve"

            def dn_stream(dr):
                X = f"x{dr}"
                pb = [ps[4 * dr + i] for i in range(4)]
                pkb = [pk[4 * dr + i] for i in range(4)]
                A_, B_, C_, D_ = 0, 1, 2, 3
                qc = [carve([128, 4, 128], BF16) for _ in range(2)]
                kc_ = [carve([128, 4, 128], BF16) for _ in range(2)]
                ktc = [carve([128, 4, 128], BF16) for _ in range(2)]
                vtc = [carve([128, 4, 128], BF16) for _ in range(2)]
                wA = carve([128, 4, 128])
                wB = carve([128, 4, 128])
                EGr = carve([128, 4, 128])
                u_ = carve([128, 4, 128])
                ot = carve([128, 4, 128])
                QKDT = carve([128, 4, 128], BF16)
                Pf = [carve([128, 4, 128]) for _ in range(2)]
                P = [R32(a) for a in Pf]
                PT = [R32(carve([128, 4, 128])) for _ in range(2)]
                RT = [R32(carve([128, 4, 128])) for _ in range(2)]
                Pb = [carve([128, 4, 128], BF16) for _ in range(2)]
                PTb = [carve([128, 4, 128], BF16) for _ in range(2)]
                RTb = [carve([128, 4, 128], BF16) for _ in range(2)]
                MT = carve([128, 4, 128], BF16)
                kg = carve([128, 4, 128], BF16)
                kdec = carve([128, 4, 128], BF16)
                wT = carve([128, 4, 128], BF16)
                vnew = carve([128, 4, 128], BF16)
                qd = carve([128, 4, 128], BF16)
                order = range(NCH) if dr == 0 else range(NCH - 1, -1, -1)
                NEGs = cst[:, C_NLS:C_NLS + 128] if dr == 0 else cst[:, C_NUS:C_NUS + 128]
                NEGt = cst[:, C_NUI:C_NUI + 128] if dr == 0 else cst[:, C_NLI:C_NLI + 128]
                K = lambda nm: X + nm
                for vi, n in enumerate(order):
                    s = vi % 2
                    tsl = slice(n * 128, (n + 1) * 128)
                    B.DMA(f"d_qc{s}{X}", qc[s], qT_d[:, :, tsl].rearrange("h d t -> d h t"), [("qT", h) for h in range(4)], [K(f"qc{s}")])
                    B.DMA(f"d_kc{s}{X}", kc_[s], kT_d[:, :, tsl].rearrange("h d t -> d h t"), [("kT", h) for h in range(4)], [K(f"kc{s}")])
                    B.DMA(f"d_ktc{s}{X}", ktc[s], ktok_d[n], [("ktok", h) for h in range(4)], [K(f"ktc{s}")])
                    B.DMA(f"d_vtc{s}{X}", vtc[s], vtok_d[n], [("vtok", h) for h in range(4)], [K(f"vtc{s}")])
                    gcn = gc[:, dr, n, :]
                    for h in range(4):
                        B.MM(pb[A_][:, h * 128:(h + 1) * 128], kc_[s][:, h, :], kc_[s][:, h, :], [K(f"kc{s}")], [pkb[A_]])
                    for h in range(4):
                        B.MM(pb[B_][:, h * 128:(h + 1) * 128], kc_[s][:, h, :], qc[s][:, h, :], [K(f"kc{s}"), K(f"qc{s}")], [pkb[B_]])
                    B.TT(wA, bc4(I32), bcl(gcn), ALU.mult, ["cst", "gc"], [K("wA")], eng=POOLE)
                    yield
                    B.MM(pb[C_], ONE32, f3(wA), ["cst", K("wA")], [pkb[C_]])
                    yield
                    B.TT(wA, bc4(NEGs), v3(pb[C_]), ALU.subtract, ["cst", pkb[C_]], [K("wA")])
                    B.TT(wA, wA, bcl(gcn), ALU.add, [K("wA"), "gc"], [K("wA")])
                    B.TT(wB, v3(pb[C_]), bc4(NEGt), ALU.add, ["cst", pkb[C_]], [K("wB")])
                    B.TT(wB, wB, bcl(gcn), ALU.subtract, [K("wB"), "gc"], [K("wB")])
                    yield
                    B.ACT(f3(wA), f3(wA), AF.Exp, [K("wA")], [K("wA")])
                    B.ACT(f3(wB), f3(wB), AF.Exp, [K("wB")], [K("wB")])
                    B.ACT(f3(EGr), pb[C_], AF.Exp, [pkb[C_], K("wA"), K("wB")], [K("EGr")])
                    yield
                    B.TT(wA, v3(pb[A_]), wA, ALU.mult, [pkb[A_], K("wA")], [K("wA")])
                    B.TT(P[0], wA, bcl(nbeta[:, dr, n, :]), ALU.mult, [K("wA"), "nbeta"], [K("P0")], eng=POOLE)
                    B.STT(QKDT, v3(pb[B_]), scale, wB, ALU.mult, ALU.mult, [pkb[B_], K("wB")], [K("QKDT")])
                    yield
                    for h in range(4):
                        B.TR(pb[D_][:, h * 128:(h + 1) * 128], Pf[0][:, h, :], I32, [K("P0"), "cst"], [pkb[D_]])
                    yield
                    B.CP(PT[0], v3(pb[D_]), [pkb[D_]], [K("PT0")], eng="act")
                    yield
                    B.TT(RT[0], PT[0], bc4(I32), ALU.add, [K("PT0"), "cst"], [K("RT0")], eng=POOLE)
                    NFP = 6
                    cur = 0
                    cb = 0
                    for it in range(6):
                        fp = it < NFP
                        nx = 1 - cur
                        nb2_ = 1 - cb
                        if fp:
                            Pin, PTin, RTin = P[cur], PT[cur], RT[cur]
                            kP, kPT, kRT = K(f"P{cur}"), K(f"PT{cur}"), K(f"RT{cur}")
                        else:
                            Pin, PTin, RTin = Pb[cb], PTb[cb], RTb[cb]
                            kP, kPT, kRT = K(f"Pb{cb}"), K(f"PTb{cb}"), K(f"RTb{cb}")
                        for h in range(4):
                            B.MM(pb[A_][:, h * 128:(h + 1) * 128], PTin[:, h, :], Pin[:, h, :], [kPT, kP], [pkb[A_]])
                        if it < 5:
                            for h in range(4):
                                B.MM(pb[B_][:, h * 128:(h + 1) * 128], Pin[:, h, :], PTin[:, h, :], [kPT, kP], [pkb[B_]])
                        yield
                        if it < NFP - 1:
                            Pl, kPl = P[nx], K(f"P{nx}")
                            B.CP(P[nx], v3(pb[A_]), [pkb[A_]], [K(f"P{nx}")], eng="act")
                            B.CP(PT[nx], v3(pb[B_]), [pkb[B_]], [K(f"PT{nx}")], eng="dve")
                        elif it == NFP - 1:
                            Pl, kPl = P[nx], K(f"P{nx}")
                            B.CP(P[nx], v3(pb[A_]), [pkb[A_]], [K(f"P{nx}")], eng="act")
                            if it < 5:
                                B.CP(Pb[0], v3(pb[A_]), [pkb[A_]], [K("Pb0")], eng="act")
                                B.CP(PTb[0], v3(pb[B_]), [pkb[B_]], [K("PTb0")], eng="dve")
                        else:
                            Pl, kPl = Pb[nb2_], K(f"Pb{nb2_}")
                            B.CP(Pb[nb2_], v3(pb[A_]), [pkb[A_]], [K(f"Pb{nb2_}")], eng="act")
                            if it < 5:
                                B.CP(PTb[nb2_], v3(pb[B_]), [pkb[B_]], [K(f"PTb{nb2_}")], eng="dve")
                        yield
                        for h in range(4):
                            B.MM(pb[C_][:, h * 128:(h + 1) * 128], Pl[:, h, :], RTin[:, h, :], [kPl, kRT], [pkb[C_]])
                        yield
                        if it < NFP - 1:
                            B.TT(RT[nx], RTin, v3(pb[C_]), ALU.add, [kRT, pkb[C_]], [K(f"RT{nx}")])
                            cur = nx
                        elif it == NFP - 1:
                            B.TT(RTb[0], RTin, v3(pb[C_]), ALU.add, [kRT, pkb[C_]], [K("RTb0")])
                            cb = 0
                        else:
                            B.TT(RTb[nb2_], RTin, v3(pb[C_]), ALU.add, [kRT, pkb[C_]], [K(f"RTb{nb2_}")])
                            cb = nb2_
                    RTfin, kRTfin = RTb[cb], K(f"RTb{cb}")
                    B.TT(MT, RTfin, bcl(beta[:, dr, n, :]), ALU.mult, [kRTfin, "beta"], [K("MT")], eng=POOLE)
                    B.TT(kg, ktc[s], bcl(EG[:, dr, n, :]), ALU.mult, [K(f"ktc{s}"), "EG"], [K("kg")], eng=POOLE)
                    B.TT(kdec, ktc[s], bcl(EKD[:, dr, n, :]), ALU.mult, [K(f"ktc{s}"), "EKD"], [K("kdec")], eng=POOLE)
                    B.STT(qd, qc[s], scale, EGr, ALU.mult, ALU.mult, [K(f"qc{s}"), K("EGr")], [K("qd")])
                    yield
                    for h in range(4):
                        B.MM(pb[A_][:, h * 128:(h + 1) * 128], MT[:, h, :], vtc[s][:, h, :], [K("MT"), K(f"vtc{s}")], [pkb[A_]])
                    for h in range(4):
                        B.MM(pb[B_][:, h * 128:(h + 1) * 128], kg[:, h, :], MT[:, h, :], [K("MT"), K("kg")], [pkb[B_]])
                    yield
                    B.CP(u_, v3(pb[A_]), [pkb[A_]], [K("u")], eng="act")
                    B.CP(wT, v3(pb[B_]), [pkb[B_]], [K("wT")], eng="act")
                    yield
                    for h in range(4):
                        B.MM(pb[C_][:, h * 128:(h + 1) * 128], wT[:, h, :], Sb[dr][:, h, :], [K("wT"), f"Sb{dr}"], [pkb[C_]])
                    yield
                    B.TT(vnew, u_, v3(pb[C_]), ALU.subtract, [K("u"), pkb[C_]], [K("vnew")])
                    yield
                    for h in range(4):
                        B.MM(pb[A_][:, h * 128:(h + 1) * 128], Sb[dr][:, h, :], qd[:, h, :], [K("qd"), f"Sb{dr}"], [pkb[A_]], start=True, stop=False)
                        B.MM(pb[A_][:, h * 128:(h + 1) * 128], vnew[:, h, :], QKDT[:, h, :], [K("vnew"), K("QKDT")], [pkb[A_]], start=False, stop=True)
                    for h in range(4):
                        B.MM(pb[B_][:, h * 128:(h + 1) * 128], kdec[:, h, :], vnew[:, h, :], [K("kdec"), K("vnew")], [pkb[B_]])
                    B.TT(Sst[dr], Sst[dr], bcl(EGL[:, dr, n, :]), ALU.mult, [f"S{dr}", "EGL"], [f"S{dr}"], eng=POOLE)
                    yield
                    B.TT(Sst[dr], Sst[dr], v3(pb[B_]), ALU.add, [f"S{dr}", pkb[B_]], [f"S{dr}"])
                    seg_end = (n % 2 == 1) if dr == 0 else (n % 2 == 0)
                    if seg_end:
                        B.DMA(f"d_od{dr}", od[n // 2, dr].rearrange("h d v -> d h v"), Sst[dr], [f"S{dr}"], [])
                        B.TS(Sst[dr], Sst[dr], knc[:, 0:1], ALU.mult, [f"S{dr}", "knc"], [f"S{dr}"])
                    B.CP(Sb[dr], Sst[dr], [f"S{dr}"], [f"Sb{dr}"], eng="act")
                    B.CP(ot, v3(pb[A_]), [pkb[A_]], [K("ot")], eng="act")
                    B.DMA(f"d_ot{X}", (oTf_d if dr == 0 else oTb_d)[n], ot, [K("ot")], [("oT", dr, n)])
                    yield

            gens = [dn_stream(0), dn_stream(1)]
            alive = [True, True]
            import os
            if os.environ.get("DN_SEQ"):
                ny = int(os.environ.get("DN_YIELDS", "100000"))
                for g in gens:
                    for i, _ in enumerate(g):
                        if i + 1 >= ny:
                            break
                alive = [False, False]
            while any(alive):
                for gi, g in enumerate(gens):
                    if alive[gi]:
                        try:
                            next(g)
                        except StopIteration:
                            alive[gi] = False
            S.barrier()
            self._aoff = off_comb
            ofc = [carve([128, 4, 128]) for _ in range(4)]
            obc = [carve([128, 4, 128]) for _ in range(4)]
            zc = [carve([128, 4, 128], BF16) for _ in range(4)]
            osq = [carve([128, 4, 128]) for _ in range(4)]
            for n in range(NCH):
                s = n % 4
                tsl = slice(n * 128, (n + 1) * 128)
                B.DMA(f"d_ofc{s}", ofc[s], oTf_d[n], [("oT", 0, n)], [f"ofc{s}"])
                B.DMA(f"d_obc{s}", obc[s], oTb_d[n], [("oT", 1, n)], [f"obc{s}"])
                B.DMA(f"d_zc{s}", zc[s], zs_d[:, :, tsl].rearrange("h d t -> d h t"), [("zs", h) for h in range(4)], [f"zc{s}"])
                B.TT(ofc[s], ofc[s], obc[s], ALU.add, [f"ofc{s}", f"obc{s}"], [f"ofc{s}"])
                B.ACT(f3(osq[s]), f3(ofc[s]), AF.Square, [f"ofc{s}"], [f"osq{s}"])
                B.MM(ps[s], ONE32, f3(osq[s]), ["cst", f"osq{s}"], [pk[s]])
                B.ACT(f3(osq[s]), ps[s], AF.Sqrt, [pk[s], "epsc"], [f"osq{s}"], bias=epsc[:, 0:1], scale=1.0 / 128)
                B.RCP(osq[s], osq[s], [f"osq{s}"], [f"osq{s}"])
                B.TT(ofc[s], ofc[s], osq[s], ALU.mult, [f"ofc{s}", f"osq{s}"], [f"ofc{s}"])
                B.STT(mixT[:, 0:4, tsl], ofc[s], vB[:, 60:61], zc[s], ALU.mult, ALU.mult, [f"ofc{s}", "vB", f"zc{s}"], ["mixT"])

            if self.stop_after == "dn":
                return
            S.barrier()
            self._aoff = off_persist
            qr = carve([128, 4, T], BF16)
            kr = carve([128, 2, T + 256], BF16)
            vat = carve([128, 18, 2, 128], BF16)
            mb = carve([128, 18 * 8])
            B.DMA("d_mb", mb, maskb, (), ["mb"])
            cosT = carve([128, T])
            sinT = carve([128, T])
            B.DMA("d_cs", cosT, ropec, (), ["cosT"])
            B.DMA("d_cs", sinT, ropes, (), ["sinT"])
            a32_ = [carve([128, 512]) for _ in range(2)]
            asq_ = [carve([128, 512]) for _ in range(2)]
            ars_ = [carve([128, 512]) for _ in range(2)]
            arot_ = [carve([128, 512]) for _ in range(2)]
            kst_ = [carve([128, 4, 128]) for _ in range(2)]
            idx = 0
            for j in range(6):
                c0 = 2064 + j * 128
                nw = vB[:, 61:62] if j < 4 else vB[:, 62:63]

                def consq(p, pkk, tb, j=j, nw=nw):
                    q = tb % 2
                    a32, asq, ars, arot, kst = a32_[q], asq_[q], ars_[q], arot_[q], kst_[q]
                    ka, ks, kr_, ko, kk = f"a32{q}", f"asq{q}", f"ars{q}", f"arot{q}", f"kst{q}"
                    tsl = slice(tb * 512, (tb + 1) * 512)
                    B.ACT(asq, p, AF.Square, [pkk], [ks])
                    B.MM(ps[2 + q], ONE32, asq, ["cst", ks], [pk[2 + q]])
                    B.ACT(ars, ps[2 + q], AF.Sqrt, [pk[2 + q], "epsc"], [kr_], bias=epsc[:, 0:1], scale=1.0 / 128)
                    B.RCP(ars, ars, [kr_], [kr_])
                    B.STT(a32, p, nw, ars, ALU.mult, ALU.mult, [pkk, "vB", kr_], [ka])
                    if j >= 4:
                        for k in range(4):
                            B.TR(ps[6 + q][:, k * 128:(k + 1) * 128], a32[:, k * 128:(k + 1) * 128], I32, [ka, "cst"], [pk[6 + q]])
                        B.CP(kst, v3(ps[6 + q]), [pk[6 + q]], [kk], eng="act")
                        B.DMA(f"d_kst{q}", ok.rearrange("(n p) f -> p n f", p=128)[:, tb * 4:(tb + 1) * 4, (j - 4) * 128:(j - 3) * 128],
                              kst, [kk], [])
                    B.MM(ps[4 + q], cst[:, C_ROT:C_ROT + 128], a32, ["cst", ka], [pk[4 + q]])
                    B.TT(arot, ps[4 + q], sinT[:, tsl], ALU.mult, [pk[4 + q], "sinT"], [ko])
                    B.TT(a32, a32, cosT[:, tsl], ALU.mult, [ka, "cosT"], [ka])
                    dst = qr[:, j, tsl] if j < 4 else kr[:, j - 4, tsl]
                    B.TT(dst, a32, arot, ALU.add, [ka, ko], ["qr" if j < 4 else "kr"])
                proj_fm(W, c0, 128, wsl, idx, consq)
                idx += 1
            wv = carve([128, 8, 256], BF16)
            vst32 = [carve([128, 256]) for _ in range(2)]

            def consv(p, pkk, tt):
                q = tt % 2
                B.CP(vst32[q], p[:, 0:256], [pkk], [f"vst{q}"], eng="act")
                B.CP(vat[:, tt].rearrange("p a b -> p (a b)"), vst32[q], [f"vst{q}"], ["vat"])
                B.DMA(f"d_vst{q}", ov[tt * 128:(tt + 1) * 128, :], vst32[q], [f"vst{q}"], [])
            proj_tm(W, 2064 + 768, 256, wv, "wv", consv, pbase=4)
            ckt = carve([128, 2, 256])
            B.DMA("d_ck", ckt, ck.rearrange("(n p) f -> p n f", p=128), (), ["ckt"])
            for kv in range(2):
                for n2 in range(2):
                    B.TR(ps[3][:, (kv * 2 + n2) * 128:(kv * 2 + n2 + 1) * 128], ckt[:, n2, kv * 128:(kv + 1) * 128], I32, ["ckt", "cst"], [pk[3]])
            B.CP(kr[:, :, T:T + 256], ps[3].rearrange("p (a b) -> p a b", a=2), [pk[3]], ["kr"])
            B.DMA("d_cv", vat[:, 16:18].rearrange("p n a b -> p n (a b)"), cv.rearrange("(n p) f -> p n f", p=128), (), ["vat"], queue="pool")
            pT = [carve([128, 512], BF16) for _ in range(4)]
            rden = [carve([128, 512]) for _ in range(2)]
            asc = 128.0 ** -0.5
            iters = [(kv, qb, kt) for kv in range(2) for qb in range(8) for kt in range(18)]

            def emit_scores(it):
                kv, qb, kt = iters[it]
                pb = it % 3
                qsl = slice(qb * 256, (qb + 1) * 256)
                for g in range(2):
                    B.MM(ps[pb][:, g * 256:(g + 1) * 256], kr[:, kv, kt * 128:(kt + 1) * 128], qr[:, kv * 2 + g, qsl],
                         ["kr", "qr"], [pk[pb]])

            import os as _os
            AHEAD = int(_os.environ.get('ATT_AHEAD', '1'))
            for it in range(min(AHEAD, len(iters))):
                emit_scores(it)
            for it, (kv, qb, kt) in enumerate(iters):
                if it + AHEAD < len(iters):
                    emit_scores(it + AHEAD)
                pb = it % 3
                r4 = it % 4
                acc = (kv * 8 + qb) % 2
                pn, pd = 4 + 2 * acc, 5 + 2 * acc
                qsl = slice(qb * 256, (qb + 1) * 256)
                B.ACT(pT[r4], ps[pb], AF.Exp, [pk[pb], "mb"], [f"pT{r4}"], bias=mb[:, kt * 8 + qb:kt * 8 + qb + 1], scale=asc)
                B.MM(ps[pn], vat[:, kt, kv, :], pT[r4], ["vat", f"pT{r4}"], [pk[pn]], start=(kt == 0), stop=(kt == 17))
                B.MM(ps[pd], ONEb, pT[r4], ["cstb", f"pT{r4}"], [pk[pd]], start=(kt == 0), stop=(kt == 17))
                if kt == 17:
                    B.RCP(rden[acc], ps[pd], [pk[pd]], [f"rden{acc}"])
                    for g in range(2):
                        B.TT(mixT[:, 4 + kv * 2 + g, qsl], ps[pn][:, g * 256:(g + 1) * 256], rden[acc][:, g * 256:(g + 1) * 256], ALU.mult,
                             [pk[pn], f"rden{acc}"], ["mixT"])
            if self.stop_after == "attn":
                return
            S.barrier()
            self._aoff = off_persist
            wo = carve([128, 8, 1024], BF16)
            B.DMAS("d_wo", [(wo[:, c, :], ab_out[0][c * 128:(c + 1) * 128, :]) for c in range(8)], (), ["wbuf"], queue="pool")
            res_proj_tb(8, mixT, "mixT", mod(0, 2), wo, (0, 1))

        def mixer_c():
            new_phase()
            W = ml_in[0]
            mqT_d = B.scratch("mqT_d", [8, 64, T], BF16)
            mkT_d = B.scratch("mkT_d", [8, 64, T], BF16)
            mk_d = B.scratch("mk_d", [NCH, 128, 8, 64], BF16)
            mv_d = B.scratch("mv_d", [NCH, 128, 8, 128], BF16)
            mo_d = B.scratch("mo_d", [NCH, 128, 8, 128], BF16)
            hf_d = B.scratch("hf_d", [NCH, 128, 8, 128], F32)
            mixT = carve([128, 8, T], BF16)
            wsl = [carve([128, 8, 128], BF16) for _ in range(2)]
            off_persist = self._aoff
            nb_ = carve([64, T], BF16)
            idx = 0
            for grp in range(2):
                for h in range(8):
                    def cons(p, pkk, tb, grp=grp):
                        if grp == 0:
                            B.ACT(nb_[:, tb * 512:(tb + 1) * 512], p[0:64, :], AF.Copy, [pkk], ["nb"], scale=0.125)
                        else:
                            B.CP(nb_[:, tb * 512:(tb + 1) * 512], p[0:64, :], [pkk], ["nb"], eng="act")
                    proj_fm(W, grp * 512 + h * 64, 64, wsl, idx, cons)
                    idx += 1
                    B.DMA("d_mqk", (mqT_d if grp == 0 else mkT_d)[h], nb_, ["nb"], [("mq" if grp == 0 else "mk", h)])
            wbig = carve([128, 8, 512], BF16)
            st = [carve([128, 512], BF16) for _ in range(2)]
            jobs = [(512, mk_d, "p (h k) -> p h k", 64, 0, 8, "mktok", None),
                    (1024, mv_d, "p (h k) -> p h k", 128, 0, 4, "mvtok", None),
                    (1536, mv_d, "p (h k) -> p h k", 128, 4, 4, "mvtok", None),
                    (2048, mo_d, "p (h k) -> p h k", 128, 0, 4, "motok", AF.Sigmoid),
                    (2560, mo_d, "p (h k) -> p h k", 128, 4, 4, "motok", AF.Sigmoid)]
            for ji, (c0, dst, pat, kk, h0, nh, key, fn) in enumerate(jobs):
                def const(p, pkk, tt, dst=dst, kk=kk, h0=h0, nh=nh, key=key, fn=fn, ji=ji):
                    q = tt % 2
                    B.ACT(st[q], p, fn if fn is not None else AF.Copy, [pkk], [f"st{q}"])
                    B.DMA(f"d_st{q}", dst[tt][:, h0:h0 + nh, :], st[q].rearrange("p (h k) -> p h k", k=kk), [f"st{q}"], [(key, tt, ji)])
                proj_tm(W, c0, 512, wbig, "wbig", const)
            wif = carve([128, 8, 32], BF16)
            gif = carve([128, NCH, 32])
            b1 = carve([128, 32])
            B.DMA("d_b1", b1, bc1.partition_broadcast(128), (), ["b1"])

            def consif(p, pkk, tt):
                B.CP(gif[:, tt, :], p[:, 0:32], [pkk], ["gif"])
            proj_tm(W, 3072, 32, wif, "wif", consif)
            li = carve([128, 2, NCH, 8])
            lf = carve([128, 2, NCH, 8])
            t16 = carve([128, NCH, 16])
            B.TT(t16, gif[:, :, 0:16], b1[:, 0:16].unsqueeze(1).to_broadcast([128, NCH, 16]), ALU.add, ["gif", "b1"], ["t16"])
            for dr in range(2):
                B.CP(li[:, dr], t16[:, :, dr * 8:(dr + 1) * 8], ["t16"], ["li"])
            B.TT(t16, gif[:, :, 16:32], b1[:, 16:32].unsqueeze(1).to_broadcast([128, NCH, 16]), ALU.add, ["gif", "b1"], ["t16"])
            B.ACT(t16, t16, AF.Exp, ["t16"], ["t16"], scale=-1.0)
            B.ACT(t16, t16, AF.Ln, ["t16", "cst"], ["t16"], bias=ONE32[:, 0:1])
            for dr in range(2):
                B.TS(lf[:, dr], t16[:, :, dr * 8:(dr + 1) * 8], -1.0, ALU.mult, ["t16"], ["lf"])
            bb = carve([128, 2, NCH, 8])
            bl = carve([128, 2, NCH, 8])
            aa = carve([128, 2, NCH, 8])
            lw = carve([128, 2, NCH, 8])
            mend = carve([128, 2, NCH, 8])
            g2 = lambda a, dr: a[:, dr].rearrange("p n h -> p (n h)")
            for dr in range(2):
                tri = cst[:, C_U:C_U + 128] if dr == 0 else cst[:, C_LO:C_LO + 128]
                B.MM(ps[2][:, 0:128], tri, g2(lf, dr), ["cst", "lf"], [pk[2]])
                B.CP(g2(bb, dr), ps[2][:, 0:128], [pk[2]], ["bb"])
                B.MM(ps[2][:, 0:128], ONE32, g2(lf, dr), ["cst", "lf"], [pk[2]])
                B.CP(g2(bl, dr), ps[2][:, 0:128], [pk[2]], ["bl"])
            f2 = lambda a: a.rearrange("p d n h -> p (d n h)")
            B.TT(f2(aa), f2(li), f2(bb), ALU.subtract, ["li", "bb"], ["aa"])
            B.TT(f2(lw), f2(aa), f2(bl), ALU.add, ["aa", "bl"], ["lw"])
            mx = carve([128, 1])
            mxb = carve([128, 128])
            for dr in range(2):
                B.TR(ps[2][:, 0:128], g2(lw, dr), I32, ["lw", "cst"], [pk[2]])
                B.RED(mx, ps[2][:, 0:128], ALU.max, [pk[2]], ["mx"])
                B.CP(mxb, mx[:, 0:1].to_broadcast([128, 128]), ["mx"], ["mxb"])
                B.TR(ps[2][:, 0:128], mxb, I32, ["mxb", "cst"], [pk[2]])
                B.CP(g2(mend, dr), ps[2][:, 0:128], [pk[2]], ["mend"])

            Cst = [carve([64, 8, 128]) for _ in range(2)]
            Cb = [carve([64, 8, 128], BF16) for _ in range(2)]
            nst = [carve([64, 8]) for _ in range(2)]
            nbf = [carve([64, 8], BF16) for _ in range(2)]
            mst = [carve([128, 8]) for _ in range(2)]
            for dr in range(2):
                B.DMA("d_c0", Cst[dr], mC0[dr].rearrange("h k v -> k h v"), (), [f"C{dr}"])
                B.DMA("d_c0", nst[dr], mn0[dr].rearrange("h k -> k h"), (), [f"n{dr}"], allow_slow_non_contiguous=True)
                B.DMA("d_c0", mst[dr], mm0[dr].partition_broadcast(128), (), [f"m{dr}"])
                B.CP(Cb[dr], Cst[dr], [f"C{dr}"], [f"Cb{dr}"])
                B.CP(nbf[dr], nst[dr], [f"n{dr}"], [f"nb{dr}"])
            nwb = carve([128, 128])
            B.DMA("d_b1", nwb, mlnw.partition_broadcast(128), (), ["nwb"])
            hb_d = B.scratch("hb_d", [NCH, 128, 8, 128], F32)
            off_streams = self._aoff
            f3 = lambda a: a.rearrange("p h t -> p (h t)")

            import os as _os3
            POOLM = "pool" if _os3.environ.get("USE_POOL", "1") == "1" else "dve"

            def ml_stream(dr):
                X = f"y{dr}"
                K = lambda nm: X + nm
                pb = [ps[4 * dr + i] for i in range(4)]
                pkb = [pk[4 * dr + i] for i in range(4)]
                qc = [carve([64, 8, 128], BF16) for _ in range(2)]
                kc_ = [carve([64, 8, 128], BF16) for _ in range(2)]
                ktc = [carve([128, 8, 64], BF16) for _ in range(2)]
                vtc = [carve([128, 8, 128], BF16) for _ in range(2)]
                dg2 = [carve([128, 4, 128]) for _ in range(2)]
                LD = carve([128, 8, 128])
                s32 = carve([128, 8, 128])
                sbf = carve([128, 8, 128], BF16)
                sT = carve([128, 8, 128], BF16)
                num = carve([128, 8, 128])
                hh = carve([128, 8, 128])
                wk = carve([128, 8, 64], BF16)
                sm = {k: carve([128, 8]) for k in ["mintra", "min", "mt", "winter", "emt", "rsum", "den", "ew", "mnew", "astate", "neg", "qn"]}
                order = range(NCH) if dr == 0 else range(NCH - 1, -1, -1)
                NEGi = cst[:, C_NLI:C_NLI + 128] if dr == 0 else cst[:, C_NUI:C_NUI + 128]
                for vi, n in enumerate(order):
                    s = vi % 2
                    B.DMA(f"d_qc{s}{X}", qc[s], mqT_d[:, :, n * 128:(n + 1) * 128].rearrange("h d t -> d h t"), [("mq", h) for h in range(8)], [K(f"qc{s}")])
                    B.DMA(f"d_kc{s}{X}", kc_[s], mkT_d[:, :, n * 128:(n + 1) * 128].rearrange("h d t -> d h t"), [("mk", h) for h in range(8)], [K(f"kc{s}")])
                    B.DMA(f"d_ktc{s}{X}", ktc[s], mk_d[n], [("mktok", n, 0)], [K(f"ktc{s}")])
                    B.DMA(f"d_vtc{s}{X}", vtc[s], mv_d[n], [("mvtok", n, 1), ("mvtok", n, 2)], [K(f"vtc{s}")])
                    bn = bb[:, dr, n, :]
                    for h in range(8):
                        B.MM(pb[h // 4][:, (h % 4) * 128:(h % 4 + 1) * 128], qc[s][:, h, :], kc_[s][:, h, :], [K(f"qc{s}"), K(f"kc{s}")], [pkb[h // 4]])
                    for hf in range(2):
                        B.TT(dg2[hf], bc4(I32), bcl(aa[:, dr, n, hf * 4:(hf + 1) * 4]), ALU.mult, ["cst", "aa"], [K(f"dg{hf}")], eng=POOLM)
                    yield
                    for hf in range(2):
                        B.MM(pb[2 + hf], ONE32, f3(dg2[hf]), ["cst", K(f"dg{hf}")], [pkb[2 + hf]])
                    yield
                    for hf in range(2):
                        B.TT(LD[:, hf * 4:(hf + 1) * 4, :], v3(pb[2 + hf]), bc4(NEGi), ALU.add, [pkb[2 + hf], "cst"], [K("LD")])
                    B.TT(LD, LD, bcl(bn), ALU.add, [K("LD"), "bb"], [K("LD")], eng=POOLM)
                    B.RED(sm["mintra"], LD, ALU.max, [K("LD")], [K("mintra")])
                    B.TT(sm["min"], bn, mst[dr], ALU.add, ["bb", f"m{dr}"], [K("min")])
                    B.TT(sm["mt"], sm["min"], sm["mintra"], ALU.max, [K("min"), K("mintra")], [K("mt")])
                    B.TT(sm["winter"], sm["min"], sm["mt"], ALU.subtract, [K("min"), K("mt")], [K("winter")])
                    B.TT(LD, LD, bcl(sm["mt"]), ALU.subtract, [K("LD"), K("mt")], [K("LD")], eng=POOLM)
                    yield
                    B.ACT(sm["winter"], sm["winter"], AF.Exp, [K("winter")], [K("winter")])
                    B.ACT(sm["emt"], sm["mt"], AF.Exp, [K("mt")], [K("emt")], scale=-1.0)
                    B.ACT(f3(LD), f3(LD), AF.Exp, [K("LD")], [K("LD")])
                    yield
                    for hf in range(2):
                        B.TT(s32[:, hf * 4:(hf + 1) * 4, :], v3(pb[hf]), LD[:, hf * 4:(hf + 1) * 4, :], ALU.mult, [pkb[hf], K("LD")], [K("s32")])
                    B.RED(sm["rsum"], s32, ALU.add, [K("s32")], [K("rsum")])
                    yield
                    B.CP(sbf, s32, [K("s32")], [K("sbf")], eng="act")
                    for h in range(8):
                        B.MM(pb[0][:, h:h + 1], qc[s][:, h, :], nbf[dr][:, h:h + 1], [K(f"qc{s}"), f"nb{dr}"], [pkb[0]])
                    yield
                    B.TT(sm["qn"], pb[0][:, 0:8], sm["winter"], ALU.mult, [pkb[0], K("winter")], [K("qn")])
                    pbt = pb[2].bitcast(BF16)
                    for h in range(8):
                        B.TR(pbt[:, h * 128:(h + 1) * 128], sbf[:, h, :], Ib, [K("sbf"), "cstb"], [pkb[2]])
                    yield
                    B.CP(sT, pbt.rearrange("p (a b) -> p a b", a=8), [pkb[2]], [K("sT")], eng="act")
                    yield
                    for h in range(8):
                        B.MM(pb[h // 4][:, (h % 4) * 128:(h % 4 + 1) * 128], sT[:, h, :], vtc[s][:, h, :], [K("sT"), K(f"vtc{s}")], [pkb[h // 4]])
                    for h in range(8):
                        B.MM(pb[2 + h // 4][:, (h % 4) * 128:(h % 4 + 1) * 128], qc[s][:, h, :], Cb[dr][:, h, :], [K(f"qc{s}"), f"Cb{dr}"], [pkb[2 + h // 4]])
                    B.TT(sm["mnew"], bl[:, dr, n, :], mst[dr], ALU.add, ["bl", f"m{dr}"], [K("mnew")])
                    B.TT(sm["astate"], sm["mnew"], sm["mnew"], ALU.max, [K("mnew")], [K("astate")])
                    B.TT(sm["mnew"], sm["mnew"], mend[:, dr, n, :], ALU.max, [K("mnew"), "mend"], [K("mnew")])
                    B.TT(sm["astate"], sm["astate"], sm["mnew"], ALU.subtract, [K("astate"), K("mnew")], [K("astate")])
                    B.TT(sm["ew"], lw[:, dr, n, :], sm["mnew"], ALU.subtract, ["lw", K("mnew")], [K("ew")])
                    yield
                    B.ACT(sm["astate"], sm["astate"], AF.Exp, [K("astate")], [K("astate")])
                    B.ACT(sm["ew"], sm["ew"], AF.Exp, [K("ew")], [K("ew")])
                    yield
                    for hf in range(2):
                        hs_ = slice(hf * 4, (hf + 1) * 4)
                        B.TT(num[:, hs_, :], v3(pb[2 + hf]), bcl(sm["winter"][:, hs_]), ALU.mult, [pkb[2 + hf], K("winter")], [K("num")])
                        B.TT(num[:, hs_, :], num[:, hs_, :], v3(pb[hf]), ALU.add, [K("num"), pkb[hf]], [K("num")])
                    B.TT(sm["den"], sm["qn"], sm["rsum"], ALU.add, [K("qn"), K("rsum")], [K("den")])
                    B.TS(sm["neg"], sm["den"], -1.0, ALU.mult, [K("den")], [K("neg")])
                    B.TT(sm["den"], sm["den"], sm["neg"], ALU.max, [K("den"), K("neg")], [K("den")])
                    B.TT(sm["den"], sm["den"], sm["emt"], ALU.max, [K("den"), K("emt")], [K("den")])
                    B.RCP(sm["den"], sm["den"], [K("den")], [K("den")])
                    B.TT(hh, num, bcl(sm["den"]), ALU.mult, [K("num"), K("den")], [K("hh")], eng=POOLM)
                    B.DMA(f"d_hh{X}", (hf_d if dr == 0 else hb_d)[n], hh, [K("hh")], [("hh", dr, n)])
                    B.TT(wk, ktc[s], bcl(sm["ew"], 64), ALU.mult, [K(f"ktc{s}"), K("ew")], [K("wk")], eng=POOLM)
                    yield
                    for h in range(8):
                        B.MM(pb[h // 4][0:64, (h % 4) * 128:(h % 4 + 1) * 128], wk[:, h, :], vtc[s][:, h, :], [K("wk"), K(f"vtc{s}")], [pkb[h // 4]])
                    for h in range(8):
                        B.MM(pb[2][0:64, h:h + 1], wk[:, h, :], ONEb[:, 0:1], [K("wk"), "cstb"], [pkb[2]])
                    B.TT(Cst[dr], Cst[dr], bcl(sm["astate"][0:64, :]), ALU.mult, [f"C{dr}", K("astate")], [f"C{dr}"], eng=POOLM)
                    B.TT(nst[dr], nst[dr], sm["astate"][0:64, :], ALU.mult, [f"n{dr}", K("astate")], [f"n{dr}"])
                    yield
                    for hf in range(2):
                        hs_ = slice(hf * 4, (hf + 1) * 4)
                        B.TT(Cst[dr][:, hs_, :], Cst[dr][:, hs_, :], v3(pb[hf][0:64, :]), ALU.add, [f"C{dr}", pkb[hf]], [f"C{dr}"])
                    B.TT(nst[dr], nst[dr], pb[2][0:64, 0:8], ALU.add, [f"n{dr}", pkb[2]], [f"n{dr}"])
                    B.CP(mst[dr], sm["mnew"], [K("mnew")], [f"m{dr}"])
                    seg_end = (n % 2 == 1) if dr == 0 else (n % 2 == 0)
                    if seg_end:
                        sg_ = n // 2
                        B.DMA(f"d_oC{dr}", oC[sg_, dr].rearrange("h k v -> k h v"), Cst[dr], [f"C{dr}"], [])
                        B.DMA(f"d_on{dr}", on[sg_, dr].rearrange("h k -> k h"), nst[dr], [f"n{dr}"], [], allow_slow_non_contiguous=True)
                        B.DMA(f"d_om{dr}", om[sg_, dr:dr + 1, :], mst[dr][0:1, :], [f"m{dr}"], [])
                        B.TS(Cst[dr], Cst[dr], knc[0:64, 0:1], ALU.mult, [f"C{dr}", "knc"], [f"C{dr}"])
                        B.TS(nst[dr], nst[dr], knc[0:64, 0:1], ALU.mult, [f"n{dr}", "knc"], [f"n{dr}"])
                        B.TS(mst[dr], mst[dr], knc[:, 0:1], ALU.mult, [f"m{dr}", "knc"], [f"m{dr}"])
                    B.CP(Cb[dr], Cst[dr], [f"C{dr}"], [f"Cb{dr}"], eng="act")
                    B.CP(nbf[dr], nst[dr], [f"n{dr}"], [f"nb{dr}"], eng="act")
                    yield

            gens = [ml_stream(0), ml_stream(1)]
            alive = [True, True]
            while any(alive):
                for gi, g in enumerate(gens):
                    if alive[gi]:
                        try:
                            next(g)
                        except StopIteration:
                            alive[gi] = False
            S.barrier()
            self._aoff = off_streams
            NR = 4
            hfc = [carve([128, 8, 128]) for _ in range(NR)]
            hbc = [carve([128, 8, 128]) for _ in range(NR)]
            oc = [carve([128, 8, 128], BF16) for _ in range(NR)]
            sq2 = [carve([128, 8, 128]) for _ in range(2)]
            gb = [carve([128, 8, 128], BF16) for _ in range(NR)]
            ss = [carve([128, 8]) for _ in range(NR)]
            for n in range(NCH):
                s = n % NR
                q2 = n % 2
                tsl = slice(n * 128, (n + 1) * 128)
                B.DMA(f"d_hfc{s}", hfc[s], hf_d[n], (), [f"hfc{s}"])
                B.DMA(f"d_hbc{s}", hbc[s], hb_d[n], (), [f"hbc{s}"])
                B.DMA(f"d_oc{s}", oc[s], mo_d[n], (), [f"oc{s}"])
                B.TT(hfc[s], hfc[s], hbc[s], ALU.add, [f"hfc{s}", f"hbc{s}"], [f"hfc{s}"])
                B.ACT(f3(sq2[q2]), f3(hfc[s]), AF.Square, [f"hfc{s}"], [f"sq2{q2}"])
                B.RED(ss[s], sq2[q2], ALU.add, [f"sq2{q2}"], [f"ss{s}"])
                B.ACT(ss[s], ss[s], AF.Sqrt, [f"ss{s}", "epsc"], [f"ss{s}"], bias=epsc[:, 0:1], scale=1.0 / 128)
                B.RCP(ss[s], ss[s], [f"ss{s}"], [f"ss{s}"])
                B.TT(hfc[s], hfc[s], bcl(ss[s]), ALU.mult, [f"hfc{s}", f"ss{s}"], [f"hfc{s}"])
                B.TT(hfc[s], hfc[s], nwb.unsqueeze(1).to_broadcast([128, 8, 128]), ALU.mult, [f"hfc{s}", "nwb"], [f"hfc{s}"])
                B.TT(gb[s], hfc[s], oc[s], ALU.mult, [f"hfc{s}", f"oc{s}"], [f"gb{s}"])
                for hf in range(2):
                    pbi = 2 * s + hf
                    pbt = ps[pbi].bitcast(BF16)
                    for k in range(4):
                        h = hf * 4 + k
                        B.TR(pbt[:, k * 128:(k + 1) * 128], gb[s][:, h, :], Ib, [f"gb{s}", "cstb"], [pk[pbi]])
                    B.CP(mixT[:, hf * 4:(hf + 1) * 4, tsl], v3(pbt[:, 0:512]), [pk[pbi]], ["mixT"], eng="act")
            S.barrier()
            self._aoff = off_persist
            wo = carve([128, 8, 1024], BF16)
            B.DMAS("d_wo", [(wo[:, c, :], ml_out[0][c * 128:(c + 1) * 128, :]) for c in range(8)], (), ["wbuf"], queue="pool")
            res_proj_tb(8, mixT, "mixT", mod(1, 2), wo, (1, 1))

        def run():
            ada_phase(0)
            norm_phase(0, 0)
            mixer_ab()
            if self.stop_after in ("dn_pre", "dn", "attn", "mix0"):
                return
            ffn_phase(0)
            if self.stop_after == "l0":
                return
            norm_phase(1, 0)
            mixer_c()
            if self.stop_after == "mix1":
                return
            ffn_phase(1)
        run()
        if self.stop_after is not None:
            S.barrier()
            self._aoff = 0
            dbg = carve([128, 8, 512])
            for tb in range(4):
                B.DMA("d_dbg", dbg, xT_d[:, :, tb * 512:(tb + 1) * 512].rearrange("c p t -> p c t"), (), ["dbg"])
                B.DMA("d_dbg2", self.dbg_out(tb), dbg, ["dbg"], [])
        S.wait_all_dma("sp")
        with contextlib.ExitStack() as es:
            sems = {}
            for k in list(S.ENGS) + list(S.dma_cum.keys()):
                sems[k] = es.enter_context(nc.semaphore(str(k)))
            block = es.enter_context(nc.Block())
            S.emit(block, sems)
        return nc

    def dbg_out(self, tb):
        yv = self.dout["y"].rearrange("(c p q) d -> c p (q d)", c=8, p=128)
        return yv[:, :, tb * 512:(tb + 1) * 512].rearrange("c p t -> p c t")


def rope_tables(sample):
    if not sample:
        return np.ones((128, T), np.float32), np.zeros((128, T), np.float32)
    t = np.arange(T)
    rows = (t // 64).astype(np.float32)
    cols = (t % 64).astype(np.float32)
    nf = 32
    inv = (10000.0 ** (-np.arange(nf, dtype=np.float32) / nf)).astype(np.float32)
    ang = np.zeros((128, T), np.float32)
    for p in range(128):
        pos = rows if p < 64 else cols
        ang[p] = pos * inv[p % 32]
    return np.cos(ang).astype(np.float32), np.sin(ang).astype(np.float32)


_CACHE = {}


def kernel(**inp):
    f = lambda k: np.ascontiguousarray(np.asarray(inp[k], dtype=np.float32))
    stop_after = inp.get("_stop_after", None)
    key = ("nc", stop_after)
    if key not in _CACHE:
        _CACHE[key] = Builder(stop_after).build()
    nc = _CACHE[key]
    consts = make_consts()
    xp, xs = f("x_prompt"), f("x_sample")
    shared = {k: f(k) for k in ["ada_w", "ffn_w_gate", "ffn_w_up", "ffn_w_down", "ab_w_in", "ab_w_out", "ml_w_in", "ml_w_out"]}
    ada_b, n1, n2 = f("ada_b"), f("norm1_w"), f("norm2_w")
    vecB = np.concatenate([f("dn_conv_w")[0].reshape(5 * 12, 128), f("dn_norm_w")[0][None], f("at_q_norm")[0][None],
                           f("at_k_norm")[0][None]], axis=0)
    bc0 = np.concatenate([f("dn_A_log")[0].reshape(8), f("dn_dt_bias")[0].reshape(8)])
    bc1 = np.concatenate([f("ml_i_bias")[0].reshape(16), f("ml_f_bias")[0].reshape(16)])
    in_maps = []
    for c in range(8):
        sample = c >= 4
        m = dict(shared)
        m["consts"] = consts
        cond = f("c")[c - 4] if sample else f("c_ctx")
        m["vecA"] = np.stack([np.concatenate([ada_b[l].reshape(48, 128), n1[l].reshape(8, 128), n2[l].reshape(8, 128),
                                              cond.reshape(8, 128)], axis=0) for l in range(2)])
        m["vecB"] = vecB
        m["bc0"], m["bc1"], m["mlnw"] = bc0, bc1, f("ml_norm_w")[0]
        m["knf"] = np.array([1.0 if sample else 0.0], np.float32)
        mb = np.zeros((128, 18, 8), np.float32)
        if not sample:
            mb[:] = NEG
            for qb in range(8):
                mb[:, 2 * qb:2 * qb + 2, qb] = 0.0
        m["maskb"] = mb.reshape(128, 144)
        m["ropec"], m["ropes"] = rope_tables(sample)
        if sample:
            b = c - 4
            m["xin"] = xs[b]
            m["ck"] = f("cache_attn_k")[b, 0].reshape(256, 256)
            m["cv"] = f("cache_attn_v")[b, 0].reshape(256, 256)
            m["sd0"] = f("state_delta")[b, 0]
            m["mC0"] = f("state_mlstm_C")[b, 0]
            m["mn0"] = f("state_mlstm_n")[b, 0]
            m["mm0"] = f("state_mlstm_m")[b, 0]
        else:
            m["xin"] = xp[8 * c:8 * c + 8].reshape(T, D)
            m["ck"] = np.zeros((256, 256), np.float32)
            m["cv"] = np.zeros((256, 256), np.float32)
            m["sd0"] = np.zeros((2, 4, 128, 128), np.float32)
            m["mC0"] = np.zeros((2, 8, 64, 128), np.float32)
            m["mn0"] = np.zeros((2, 8, 64), np.float32)
            m["mm0"] = np.zeros((2, 8), np.float32)
        in_maps.append({k: np.ascontiguousarray(v) for k, v in m.items()})
    res = run_bass_kernel_spmd(nc, in_maps, core_ids=list(range(8)))
    R = res.results
    if stop_after is not None:
        return R
    y_prompt = np.concatenate([R[c]["y"].reshape(8, 256, D) for c in range(4)], axis=0)
    y_sample = np.stack([R[c]["y"] for c in range(4, 8)], axis=0)
    nk = np.concatenate([R[c]["ok"].reshape(8, 1, 256, 2, 128) for c in range(4)], axis=0)
    nv = np.concatenate([R[c]["ov"].reshape(8, 1, 256, 2, 128) for c in range(4)], axis=0)
    nd = np.concatenate([R[c]["od"].reshape(8, 1, 2, 4, 128, 128) for c in range(4)], axis=0)
    nC = np.concatenate([R[c]["oC"].reshape(8, 1, 2, 8, 64, 128) for c in range(4)], axis=0)
    nn = np.concatenate([R[c]["on"].reshape(8, 1, 2, 8, 64) for c in range(4)], axis=0)
    nm = np.concatenate([R[c]["om"].reshape(8, 1, 2, 8) for c in range(4)], axis=0)
    return tuple(np.ascontiguousarray(a, dtype=np.float32) for a in (y_prompt, y_sample, nk, nv, nd, nC, nn, nm))
```
